# Optimizing a Trainium2 kernel written in Bass

```python
import math
import jax, jax.numpy as jnp
from jax import lax
import numpy as np

D_MODEL = 1024
BATCH = 16
SEQ = 2048
DEPTH = 1
DEC_BATCH = 128
DEC_SEQ = 8
PAST_LEN = 16384
PAGE_SIZE = 128

N_HEADS = 8
QK_NOPE = 64
QK_ROPE = 32
QK_HEAD = QK_NOPE + QK_ROPE
V_HEAD = 64
Q_LORA = 384
KV_LORA = 256
ROPE_THETA = 10000.0
ATTN_SCALE = QK_HEAD ** -0.5
Q_BLOCK = 128
CONV_DIM = 512
CONV_WIDTH = 3
D_FF = int(math.ceil(8 * D_MODEL / 3 / 256)) * 256
ALPHA = (2.0 * DEPTH) ** 0.25
BETA = (8.0 * DEPTH) ** -0.25
LN_EPS = 1e-5
RMS_EPS = 1e-6
SPLITS = [Q_LORA, Q_LORA + KV_LORA, Q_LORA + KV_LORA + QK_ROPE, Q_LORA + KV_LORA + QK_ROPE + 3 * CONV_DIM]
N_IN = SPLITS[-1] + 2 * D_MODEL

kernel_name = 'hybrid_mla_shortconv_adaln_deepnorm_step'


def layer_norm(x, g, b):
    xf = x.astype(jnp.float32)
    mu = jnp.mean(xf, -1, keepdims=True)
    var = jnp.mean(jnp.square(xf - mu), -1, keepdims=True)
    return ((xf - mu) * lax.rsqrt(var + LN_EPS) * g + b).astype(x.dtype)


def rms_norm(x, g):
    xf = x.astype(jnp.float32)
    return (xf * lax.rsqrt(jnp.mean(xf * xf, -1, keepdims=True) + RMS_EPS) * g).astype(x.dtype)


def rope_tables(pos):
    inv = ROPE_THETA ** (-jnp.arange(0, QK_ROPE, 2, dtype=jnp.float32) / QK_ROPE)
    ang = pos.astype(jnp.float32)[:, None] * inv[None, :]
    return jnp.cos(ang), jnp.sin(ang)


def apply_rope(x, cos, sin):
    x1, x2 = jnp.split(x.astype(jnp.float32), 2, axis=-1)
    return jnp.concatenate([x1 * cos - x2 * sin, x1 * sin + x2 * cos], -1).astype(x.dtype)


def ada_terms(c, w_ada, b_ada):
    a = jax.nn.silu(c) @ w_ada + b_ada
    return jnp.split(a[:, None, :], 6, axis=-1)


def mixer_in(h, cos, sin, w_in, q_norm_g, kv_norm_g, w_uq):
    B, T = h.shape[0], h.shape[1]
    z = h @ w_in
    q_lat, kv_lat, kr, conv_in, gates = jnp.split(z, SPLITS, axis=-1)
    q = (rms_norm(q_lat, q_norm_g) @ w_uq).reshape(B, T, N_HEADS, QK_HEAD)
    q_nope = q[..., :QK_NOPE]
    q_rope = apply_rope(q[..., QK_NOPE:], cos[:, None, :], sin[:, None, :])
    c_kv = rms_norm(kv_lat, kv_norm_g)
    k_rope = apply_rope(kr, cos, sin)
    b_g, c_g, v_c = jnp.split(conv_in, 3, axis=-1)
    u = c_g * v_c
    g_a, g_c = jnp.split(gates, 2, axis=-1)
    return q_nope, q_rope, c_kv, k_rope, u, b_g, g_a, g_c


def short_conv(full, conv_w, T):
    out = full[:, 0:T] * conv_w[0]
    for k in range(1, CONV_WIDTH):
        out = out + full[:, k:k + T] * conv_w[k]
    return out


def mixer_out(o, conv_y, b_g, g_a, g_c, w_oa, w_oc, w_o):
    B, T = o.shape[0], o.shape[1]
    y_a = o.reshape(B, T, N_HEADS * V_HEAD) @ w_oa
    y_c = (b_g * conv_y) @ w_oc
    m = jax.nn.sigmoid(g_a) * y_a + jax.nn.sigmoid(g_c) * y_c
    return m @ w_o


def finish(x, mix, ada, ln1_g, ln1_b, w_ff1, w_ff3, w_ff2, ln2_g, ln2_b):
    _, _, g1, sh2, sc2, g2 = ada
    x = layer_norm(ALPHA * x + g1 * mix, ln1_g, ln1_b)
    h2 = x * (1 + sc2) + sh2
    f = (jax.nn.silu(h2 @ w_ff1) * (h2 @ w_ff3)) @ w_ff2
    return layer_norm(ALPHA * x + g2 * f, ln2_g, ln2_b)


def prompt_attention(q_nope, q_rope, c_kv, k_rope, w_ukv):
    B, S = c_kv.shape[0], c_kv.shape[1]
    kv = jnp.einsum('bsl,lhd->bshd', c_kv, w_ukv)
    k_nope, v = kv[..., :QK_NOPE], kv[..., QK_NOPE:]
    k = jnp.concatenate([k_nope, jnp.broadcast_to(k_rope[:, :, None, :], (B, S, N_HEADS, QK_ROPE))], -1)
    q = jnp.concatenate([q_nope, q_rope], -1)
    nb = S // Q_BLOCK
    qb = q.reshape(B, nb, Q_BLOCK, N_HEADS, QK_HEAD).transpose(1, 0, 2, 3, 4)
    kpos = jnp.arange(S)

    def block(args):
        qi, i = args
        s = jnp.einsum('bqhd,bkhd->bhqk', qi, k, preferred_element_type=jnp.float32) * ATTN_SCALE
        qpos = i * Q_BLOCK + jnp.arange(Q_BLOCK)
        s = jnp.where(kpos[None, :] <= qpos[:, None], s, -jnp.inf)
        p = jax.nn.softmax(s, axis=-1).astype(v.dtype)
        return jnp.einsum('bhqk,bkhv->bqhv', p, v)

    o = lax.map(block, (qb, jnp.arange(nb)))
    return o.transpose(1, 0, 2, 3, 4).reshape(B, S, N_HEADS, V_HEAD)


def sample_attention(q_nope, q_rope, c_new, kr_new, cache_kv_latent, cache_k_rope, layer, page_table, w_ukv):
    f32 = jnp.float32
    w_uk, w_uv = w_ukv[..., :QK_NOPE], w_ukv[..., QK_NOPE:]
    q_abs = jnp.einsum('bthn,lhn->bhtl', q_nope, w_uk, preferred_element_type=f32)
    q_r = q_rope.transpose(0, 2, 1, 3).astype(f32)

    def scores(c, kr):
        return (jnp.einsum('bhtl,bkl->bhtk', q_abs, c.astype(f32))
                + jnp.einsum('bhtr,bkr->bhtk', q_r, kr.astype(f32))) * ATTN_SCALE

    T = c_new.shape[1]
    s = jnp.where(jnp.tril(jnp.ones((T, T), bool)), scores(c_new, kr_new), -jnp.inf)
    m = jnp.max(s, -1)
    p = jnp.exp(s - m[..., None])
    l = jnp.sum(p, -1)
    acc = jnp.einsum('bhtk,bkl->bhtl', p, c_new.astype(f32))

    def step(carry, pid):
        m, l, acc = carry
        c = cache_kv_latent[layer, pid]
        kr = cache_k_rope[layer, pid]
        s = scores(c, kr)
        m_new = jnp.maximum(m, jnp.max(s, -1))
        corr = jnp.exp(m - m_new)
        p = jnp.exp(s - m_new[..., None])
        acc = acc * corr[..., None] + jnp.einsum('bhtk,bkl->bhtl', p, c.astype(f32))
        return (m_new, l * corr + jnp.sum(p, -1), acc), None

    (m, l, acc), _ = lax.scan(step, (m, l, acc), page_table.T)
    o_lat = acc / l[..., None]
    return jnp.einsum('bhtl,lhv->bthv', o_lat, w_uv.astype(f32)).astype(c_new.dtype)


def setup_inputs(seed: int = 0) -> dict:
    key = jax.random.key(seed)
    ks = jax.random.split(key, 32)
    n_pages = PAST_LEN // PAGE_SIZE
    n_pool = -(-5 * DEC_BATCH * n_pages // 4)
    L = DEPTH

    def nrm(k, shape, scale):
        return jax.random.normal(k, shape, jnp.float32) * scale

    page_table = jax.random.permutation(ks[5], n_pool)[:DEC_BATCH * n_pages].reshape(DEC_BATCH, n_pages).astype(jnp.int32)
    return {
        'x_prompt': nrm(ks[0], (BATCH, SEQ, D_MODEL), 1.0),
        'x_sample': nrm(ks[1], (DEC_BATCH, DEC_SEQ, D_MODEL), 1.0),
        'cache_kv_latent': nrm(ks[2], (L, n_pool, PAGE_SIZE, KV_LORA), 1.0),
        'cache_k_rope': nrm(ks[3], (L, n_pool, PAGE_SIZE, QK_ROPE), 1.0),
        'state_conv': nrm(ks[4], (L, DEC_BATCH, CONV_WIDTH - 1, CONV_DIM), 1.0),
        'page_table': page_table,
        'c_prompt': nrm(ks[6], (BATCH, D_MODEL), 1.0),
        'c_sample': nrm(ks[7], (DEC_BATCH, D_MODEL), 1.0),
        'w_ada': nrm(ks[8], (L, D_MODEL, 6 * D_MODEL), 0.5 * D_MODEL ** -0.5),
        'b_ada': nrm(ks[9], (L, 6 * D_MODEL), 0.01),
        'w_in': nrm(ks[10], (L, D_MODEL, N_IN), D_MODEL ** -0.5),
        'q_norm_g': 1.0 + nrm(ks[11], (L, Q_LORA), 0.01),
        'kv_norm_g': 1.0 + nrm(ks[12], (L, KV_LORA), 0.01),
        'w_uq': nrm(ks[13], (L, Q_LORA, N_HEADS * QK_HEAD), Q_LORA ** -0.5),
        'w_ukv': nrm(ks[14], (L, KV_LORA, N_HEADS, QK_NOPE + V_HEAD), KV_LORA ** -0.5),
        'w_oa': nrm(ks[15], (L, N_HEADS * V_HEAD, D_MODEL), BETA * (N_HEADS * V_HEAD) ** -0.5),
        'conv_w': nrm(ks[16], (L, CONV_WIDTH, CONV_DIM), CONV_WIDTH ** -0.5),
        'w_oc': nrm(ks[17], (L, CONV_DIM, D_MODEL), BETA * CONV_DIM ** -0.5),
        'w_o': nrm(ks[18], (L, D_MODEL, D_MODEL), BETA * D_MODEL ** -0.5),
        'ln1_g': 1.0 + nrm(ks[19], (L, D_MODEL), 0.01),
        'ln1_b': nrm(ks[20], (L, D_MODEL), 0.01),
        'w_ff1': nrm(ks[21], (L, D_MODEL, D_FF), D_MODEL ** -0.5),
        'w_ff3': nrm(ks[22], (L, D_MODEL, D_FF), D_MODEL ** -0.5),
        'w_ff2': nrm(ks[23], (L, D_FF, D_MODEL), BETA * D_FF ** -0.5),
        'ln2_g': 1.0 + nrm(ks[24], (L, D_MODEL), 0.01),
        'ln2_b': nrm(ks[25], (L, D_MODEL), 0.01),
    }


def reference(x_prompt, x_sample, cache_kv_latent, cache_k_rope, state_conv, page_table, c_prompt, c_sample,
              w_ada, b_ada, w_in, q_norm_g, kv_norm_g, w_uq, w_ukv, w_oa, conv_w, w_oc, w_o,
              ln1_g, ln1_b, w_ff1, w_ff3, w_ff2, ln2_g, ln2_b):
    S = x_prompt.shape[1]
    T = x_sample.shape[1]
    cos_p, sin_p = rope_tables(jnp.arange(S))
    cos_s, sin_s = rope_tables(PAST_LEN + jnp.arange(T))
    xp, xs = x_prompt, x_sample
    lat_p, kr_p, conv_p, lat_s, kr_s, conv_s = [], [], [], [], [], []
    for l in range(DEPTH):
        ada = ada_terms(c_prompt, w_ada[l], b_ada[l])
        h = xp * (1 + ada[1]) + ada[0]
        qn, qr, ckv, kr, u, bg, ga, gc = mixer_in(h, cos_p, sin_p, w_in[l], q_norm_g[l], kv_norm_g[l], w_uq[l])
        o = prompt_attention(qn, qr, ckv, kr, w_ukv[l])
        full = jnp.pad(u, ((0, 0), (CONV_WIDTH - 1, 0), (0, 0)))
        cy = short_conv(full, conv_w[l], S)
        mix = mixer_out(o, cy, bg, ga, gc, w_oa[l], w_oc[l], w_o[l])
        xp = finish(xp, mix, ada, ln1_g[l], ln1_b[l], w_ff1[l], w_ff3[l], w_ff2[l], ln2_g[l], ln2_b[l])
        lat_p.append(ckv)
        kr_p.append(kr)
        conv_p.append(full[:, full.shape[1] - (CONV_WIDTH - 1):])
        ada = ada_terms(c_sample, w_ada[l], b_ada[l])
        h = xs * (1 + ada[1]) + ada[0]
        qn, qr, ckv, kr, u, bg, ga, gc = mixer_in(h, cos_s, sin_s, w_in[l], q_norm_g[l], kv_norm_g[l], w_uq[l])
        o = sample_attention(qn, qr, ckv, kr, cache_kv_latent, cache_k_rope, l, page_table, w_ukv[l])
        full = jnp.concatenate([state_conv[l].astype(u.dtype), u], axis=1)
        cy = short_conv(full, conv_w[l], T)
        mix = mixer_out(o, cy, bg, ga, gc, w_oa[l], w_oc[l], w_o[l])
        xs = finish(xs, mix, ada, ln1_g[l], ln1_b[l], w_ff1[l], w_ff3[l], w_ff2[l], ln2_g[l], ln2_b[l])
        lat_s.append(ckv)
        kr_s.append(kr)
        conv_s.append(full[:, full.shape[1] - (CONV_WIDTH - 1):])
    return (xp, xs, jnp.stack(lat_p), jnp.stack(kr_p), jnp.stack(conv_p), jnp.stack(lat_s), jnp.stack(kr_s), jnp.stack(conv_s))
```

```python
import math
from contextlib import ExitStack
import numpy as np
import ml_dtypes
import concourse.bass as bass
import concourse.mybir as mybir
from concourse.bass_utils import run_bass_kernel_spmd

F32 = mybir.dt.float32
BF16 = mybir.dt.bfloat16
I32 = mybir.dt.int32
AF = mybir.ActivationFunctionType
ALU = mybir.AluOpType
AX = mybir.AxisListType

D = 1024
NH = 8
QKN, QKR, QKH, VH = 64, 32, 96, 64
QL, KVL = 384, 256
CD = 512
DFF = 2816
NIN = 4256
DEPTH = 1
ALPHA = (2.0 * DEPTH) ** 0.25
LN_EPS = 1e-5
RMS_EPS = 1e-6
SCALE = QKH ** -0.5
PB, SB, T = 2, 16, 8
PAGE = 128
LAT = KVL + QKR
NEG = -30000.0
TCH = 4
NCH = PAGE // TCH
SAFE_SAME = True
NRING = 16


class Tracker:
    ENG = ("pe", "act", "dve", "pool", "sp")

    def __init__(self):
        self.prog = {e: [] for e in self.ENG}
        self.last_w = {}
        self.readers = {}

    def op(self, eng, fn, r=(), w=(), dma=False):
        idx = len(self.prog[eng])
        deps = set()
        for k in r:
            deps.update(self.last_w.get(k, ()))
        for k in w:
            deps.update(self.last_w.get(k, ()))
            rd = self.readers.get(k)
            if rd:
                for e2, v in rd.items():
                    if isinstance(v, list):
                        deps.update(v)
                    else:
                        deps.add((e2, v))
        deps.discard((eng, idx))
        self.prog[eng].append(dict(fn=fn, deps=deps, dma=dma))
        for k in r:
            rd = self.readers.setdefault(k, {})
            if dma:
                rd.setdefault("dma", []).append((eng, idx))
            else:
                rd[eng] = idx
        for k in w:
            prev = self.last_w.get(k, [])
            if dma and prev and all(self.prog[e2][i2]["dma"] for (e2, i2) in prev) and not self.readers.get(k):
                self.last_w[k] = prev + [(eng, idx)]
            else:
                self.last_w[k] = [(eng, idx)]
            self.readers[k] = {}
        return (eng, idx)

    DMAQ = ("sp", "pool")

    def emit(self, nc, sems, ring, final_sem):
        prog = self.prog
        dma_no = {}
        ndma = {}
        for q in self.DMAQ:
            n = 0
            for i, ins in enumerate(prog[q]):
                if ins["dma"]:
                    dma_no[(q, i)] = n
                    n += 1
            ndma[q] = n

        def is_dma(e2, i2):
            return prog[e2][i2]["dma"]

        need = set()
        for e in self.ENG:
            for i, ins in enumerate(prog[e]):
                for (e2, i2) in ins["deps"]:
                    if is_dma(e2, i2):
                        continue
                    if e2 == e and not (SAFE_SAME and e in ("act", "dve", "pool")):
                        continue
                    need.add((e2, i2))
        val = {}
        for e in self.ENG:
            c = 0
            for i in range(len(prog[e])):
                if (e, i) in need:
                    c += 1
                    val[(e, i)] = c

        def run(e, h):
            waited = {}
            for i, ins in enumerate(prog[e]):
                waits = {}
                for (e2, i2) in ins["deps"]:
                    if is_dma(e2, i2):
                        d = dma_no[(e2, i2)]
                        key = ("ring", e2, d % NRING)
                        v = 16 * (d // NRING + 1)
                    else:
                        if (e2, i2) not in val:
                            continue
                        key = e2
                        v = val[(e2, i2)]
                    if waited.get(key, 0) >= v:
                        continue
                    waits[key] = max(waits.get(key, 0), v)
                if ins["dma"]:
                    d = dma_no[(e, i)]
                    if d >= NRING:
                        key = ("ring", e, d % NRING)
                        v = 16 * (d // NRING)
                        if waited.get(key, 0) < v:
                            waits[key] = max(waits.get(key, 0), v)
                for key, v in waits.items():
                    sem = ring[key[1]][key[2]] if isinstance(key, tuple) else sems[key]
                    h.wait_ge(sem, v)
                    waited[key] = v
                bi = ins["fn"](h)
                if ins["dma"]:
                    bi.then_inc(ring[e][dma_no[(e, i)] % NRING], 16)
                elif (e, i) in val:
                    bi.then_inc(sems[e], 1)
            if e in self.DMAQ:
                for sl in range(min(NRING, ndma[e])):
                    cntd = (ndma[e] - 1 - sl) // NRING + 1
                    h.wait_ge(ring[e][sl], 16 * cntd)

        with nc.Block() as block:
            @block.sync
            def _(h):
                run("sp", h)

            @block.tensor
            def _(h):
                run("pe", h)

            @block.scalar
            def _(h):
                run("act", h)

            @block.vector
            def _(h):
                run("dve", h)

            @block.gpsimd
            def _(h):
                run("pool", h)


WSPEC = {
    "w_ada": (D, 6 * D), "w_in": (D, NIN), "w_uq": (QL, NH * QKH), "w_ukv": (KVL, NH * 128),
    "w_oa": (NH * VH, D), "w_oc": (CD, D), "w_o": (D, D), "w_ff1": (D, DFF), "w_ff3": (D, DFF), "w_ff2": (DFF, D),
}


class _Stop(Exception):
    pass


def build(SEQ, NPG, NPOOL):
    import os
    dbg_stop = int(os.environ.get("KDBG", "0"))
    ckc = [0]

    def ck(name):
        ckc[0] += 1
        if (dbg_stop and ckc[0] == dbg_stop) or (os.environ.get("KDBG_NAME") == name):
            print("KDBG stop at checkpoint", ckc[0], name, flush=True)
            raise _Stop()
    NTS = SEQ // 128
    nc = bass.Bass("TRN2", target_bir_lowering=False)
    tr = Tracker()
    es = ExitStack()

    def din(name, shape, dt=F32):
        return nc.dram_tensor(name, list(shape), dt, kind="ExternalInput").ap()

    def dout(name, shape, dt=F32):
        return nc.dram_tensor(name, list(shape), dt, kind="ExternalOutput").ap()

    xp = din("xp", [PB * SEQ, D]); xs = din("xs", [128, D])
    pool_d = din("pool", [NPOOL * NCH, TCH * LAT]); pt_d = din("pt", [SB, NPG], I32)
    sconv = din("sconv", [SB * 2, CD]); cp_d = din("cp", [PB, D]); cs_d = din("cs", [SB, D])
    wd = {k: din(k, [v[0], v[1]]) for k, v in WSPEC.items()}
    b_ada = din("b_ada", [1, 6 * D]); qg = din("q_norm_g", [1, QL]); kvg = din("kv_norm_g", [1, KVL])
    convw_d = din("conv_w", [3, CD])
    lnd = [din(n, [1, D]) for n in ("ln1_g", "ln1_b", "ln2_g", "ln2_b")]
    cosp_d = din("cosp", [SEQ, 16]); sinp_d = din("sinp", [SEQ, 16])
    coss_d = din("coss", [128, 16]); sins_d = din("sins", [128, 16])
    ident_d = din("ident", [128, 128], BF16); maskc_d = din("maskc", [128, 128], BF16)
    mbig_d = din("mbig", [64, 248], BF16)

    yp = dout("yp", [PB * SEQ, D]); ys = dout("ys", [128, D])
    latp = dout("latp", [PB * SEQ, KVL]); krp = dout("krp", [PB * SEQ, QKR]); convp = dout("convp", [PB * 2, CD])
    lats = dout("lats", [128, KVL]); krs = dout("krs", [128, QKR]); convs = dout("convs", [SB * 2, CD])

    wb = {k: nc.dram_tensor(k + "_bf", [128, v[0] // 128, v[1]], BF16, kind="Internal").ap() for k, v in WSPEC.items()}

    def sb(name, shape, dt=F32):
        return es.enter_context(nc.sbuf_tensor("sb_" + name, list(shape), dt))

    def ps(name, shape, dt=F32):
        return es.enter_context(nc.psum_tensor("ps_" + name, list(shape), dt))

    ident = sb("ident", [128, 128], BF16); maskc = sb("maskc", [128, 128], BF16); mbig = sb("mbig", [64, 248], BF16)
    lnbc = sb("lnbc", [128, 2, D]); qgbc = sb("qgbc", [128, QL]); kvgbc = sb("kvgbc", [128, KVL])
    convw = sb("convw", [128, 4, 3])
    cosp = sb("cosp", [128, NTS, 16]); sinp = sb("sinp", [128, NTS, 16])
    coss = sb("coss", [128, 16]); sins = sb("sins", [128, 16])
    ada = sb("ada", [128, 6 * D])
    cT = sb("cT", [128, 8, SB]); cTs = sb("cTs", [128, 8, SB]); cTexp = sb("cTexp", [128, 8, 128], BF16)
    xt = [sb(f"xt{i}", [128, D]) for i in range(1)]
    hbf = sb("hbf", [128, D], BF16); hT = sb("hT", [128, 8, 128], BF16)
    tmp = sb("tmp", [128, D]); tmp2 = sb("tmp2", [128, D])
    z = sb("z", [128, 672])
    st8 = sb("st8", [128, 16]); sth = [sb(f"sth{i}", [128, 8]) for i in range(2)]
    qn = sb("qn", [128, QL], BF16); qnT = sb("qnT", [128, 3, 128], BF16)
    ckv = sb("ckv", [128, KVL]); ckvb = sb("ckvb", [128, KVL], BF16); ckvT = sb("ckvT", [128, 2, 128], BF16)
    krt = sb("krt", [128, QKR]); krb = sb("krb", [128, QKR], BF16)
    q = sb("q", [128, NH, QKR]); qb = sb("qb", [128, NH, QKH], BF16); qT = sb("qT", [128, NH, 128], BF16)
    kk = sb("kk", [128, NH, QKH], BF16)
    Kt = sb("Kt", [128, NH, SEQ], BF16); Vt = sb("Vt", [128, NTS, NH * VH], BF16)
    P = [sb(f"P{i}", [128, SEQ], BF16) for i in range(2)]; PT = sb("PT", [128, 8, 128], BF16)
    oT = sb("oT", [128, 4, 128], BF16)
    uT = sb("uT", [128, 4, SB, 2 + T]); cvT = sb("cvT", [128, 4, 128]); cbT = sb("cbT", [128, 4, 128], BF16)
    cgs = sb("cgs", [128, 4, 128])
    sga = sb("sga", [128, 512]); sgc = sb("sgc", [128, 512])
    mT = sb("mT", [128, 8, 128], BF16)
    ab = sb("ab", [128, 512], BF16); aT = sb("aT", [128, 22, 128], BF16)
    slabs = [sb(f"slab{i}", [128, 8 * 512], BF16) for i in range(3)]
    ptT = sb("ptT", [128, SB], I32)
    idxr = [sb(f"idxr{i}", [128, 1], I32) for i in range(4)]
    wukT = sb("wukT", [128, 4, KVL], BF16)
    wuvb = sb("wuvb", [128, 2, NH * VH], BF16)
    qnpT = sb("qnpT", [128, 4, 128], BF16)
    QaT = sb("QaT", [128, 2, SB, 64], BF16)
    QrT = sb("QrT", [32, SB, 64], BF16)
    Xb = [sb(f"Xb{i}", [128, TCH, LAT], BF16) for i in range(4)]
    XT = [sb(f"XT{i}", [128, TCH, 3, 128], BF16) for i in range(2)]
    Ps = [sb(f"Ps{i}", [64, 512], BF16) for i in range(2)]; PTs = [sb(f"PTs{i}", [128, 4, 64], BF16) for i in range(2)]
    accs = sb("accs", [64, KVL]); sm = sb("sm", [64, 16]); olat = sb("olat", [64, KVL], BF16); olT = sb("olT", [128, 2, 64], BF16)

    pmm = [ps(f"pmm{i}", [128, 512]) for i in range(2)]
    ptr = [ps(f"ptr{i}", [128, 1024], BF16) for i in range(2)]
    psc = [ps(f"psc{i}", [128, 512]) for i in range(4)]

    cnt = {"slab": 0, "pmm": 0, "ptr": 0, "stg": 0, "x": 0, "pg": 0}
    stg = [tmp[:, :], tmp2[:, :]]
    stg_key = ["tmp", "tmp2"]
    stgb = [hbf[:, :], mT[:].rearrange("p a b -> p (a b)")]
    stgb_key = ["hbf", "mT"]
    ob = hbf[:, 0:NH * VH]

    def nxt(kind, n):
        i = cnt[kind] % n
        cnt[kind] += 1
        return i

    def dma(out, in_, r=(), w=(), nonc=False):
        if nonc:
            tr.op("sp", lambda h: h.dma_start(out=out, in_=in_, allow_slow_non_contiguous=True), r=r, w=w, dma=True)
        else:
            tr.op("sp", lambda h: h.dma_start(out=out, in_=in_), r=r, w=w, dma=True)

    try:
        dma(ident[:], ident_d, w=["ident"]); dma(maskc[:], maskc_d, w=["maskc"]); dma(mbig[:], mbig_d, w=["mbig"])
        dma(qgbc[:], qg.partition_broadcast(128), w=["qgbc"]); dma(kvgbc[:], kvg.partition_broadcast(128), w=["kvgbc"])
        for k in range(3):
            dma(convw[:, :, k], convw_d[k, :].rearrange("(c p) -> p c", p=128), w=["convw"], nonc=True)
        dma(cosp[:], cosp_d.rearrange("(n p) f -> p n f", p=128), w=["rope"]); dma(sinp[:], sinp_d.rearrange("(n p) f -> p n f", p=128), w=["rope2"])
        dma(coss[:], coss_d, w=["ropes"]); dma(sins[:], sins_d, w=["ropes2"])

        ck("constants")
        def convert(name):
            K, N = WSPEC[name]
            for kc in range(K // 128):
                for c0 in range(0, N, 1024):
                    c1 = min(N, c0 + 1024)
                    i = nxt("stg", 2)
                    dma(stg[i][:, 0:c1 - c0], wd[name][kc * 128:(kc + 1) * 128, c0:c1], w=[stg_key[i]])
                    if i == 1:
                        tr.op("dve", lambda h, i=i, n=c1 - c0: h.tensor_copy(out=stgb[i][:, 0:n], in_=stg[i][:, 0:n]), r=[stg_key[i]], w=[stgb_key[i]])
                    else:
                        tr.op("act", lambda h, i=i, n=c1 - c0: h.copy(out=stgb[i][:, 0:n], in_=stg[i][:, 0:n]), r=[stg_key[i]], w=[stgb_key[i]])
                    dma(wb[name][:, kc, c0:c1], stgb[i][:, 0:c1 - c0], r=[stgb_key[i]], w=[("wb", name)])

        for name in WSPEC:
            convert(name)

        ck("conversions")
        def slab(name, c0, c1, k0=0, k1=None):
            K, N = WSPEC[name]
            if k1 is None:
                k1 = K // 128
            KC = k1 - k0
            n = c1 - c0
            assert KC * n <= 8 * 512
            i = nxt("slab", 3); buf = slabs[i]; key = ("slab", i)
            view = buf[:, 0:KC * n].rearrange("p (k n) -> p k n", n=n)
            dma(view, wb[name][:, k0:k1, c0:c1], r=[("wb", name)], w=[key])
            return view, key

        def mm(out, lhsT, rhs, start, stop, r, w):
            tr.op("pe", lambda h: h.matmul(out, lhsT, rhs, start=start, stop=stop), r=r, w=w)

        def transposes(src_ap_fn, nchunk, dst, dst_key, src_key, rows=128, cols=128):
            i = nxt("ptr", 2)
            for j in range(nchunk):
                tr.op("pe", lambda h, j=j, i=i: h.transpose(ptr[i][0:cols, j * 128:j * 128 + rows], src_ap_fn(j), ident[0:rows, 0:rows]),
                      r=[src_key, "ident"], w=[("ptr", i)])
            tr.op("dve", lambda h, i=i: h.tensor_copy(out=dst[0:cols, 0:nchunk, 0:rows],
                                                     in_=ptr[i][0:cols, 0:nchunk * 128].rearrange("p (j t) -> p j t", t=128)[:, :, 0:rows]),
                  r=[("ptr", i)], w=[dst_key])

        wukc = hbf[:, :].rearrange("p (k n) -> p k n", k=2)
        for kc in range(2):
            src = wb["w_ukv"][:, kc, :].rearrange("p (h e) -> p h e", e=128)
            dma(wuvb[:, kc, :].rearrange("p (h e) -> p h e", e=VH), src[:, :, QKN:128], r=[("wb", "w_ukv")], w=["wuvb"])
            dma(wukc[:, kc, :].rearrange("p (h e) -> p h e", e=QKN), src[:, :, 0:QKN], r=[("wb", "w_ukv")], w=["hbf"])
        for kc in range(2):
            i = nxt("ptr", 2)
            for pr in range(4):
                tr.op("pe", lambda h, kc=kc, pr=pr, i=i: h.transpose(ptr[i][:, pr * 128:(pr + 1) * 128], wukc[:, kc, pr * 128:(pr + 1) * 128], ident[:]),
                      r=["hbf", "ident"], w=[("ptr", i)])
            tr.op("dve", lambda h, kc=kc, i=i: h.tensor_copy(out=wukT[:, :, kc * 128:(kc + 1) * 128],
                                                            in_=ptr[i][:, 0:512].rearrange("p (j t) -> p j t", t=128)),
                  r=[("ptr", i)], w=["wukT"])

        ck("wuk setup")
        def compute_ada(c_src, nseq, rep):
            for kc in range(8):
                dma(cT[:, kc, 0:nseq], c_src[:, kc * 128:(kc + 1) * 128].rearrange("s p -> p s"), w=[("cT", kc)], nonc=True)
            tr.op("act", lambda h: h.activation(out=cTs[:, :, 0:nseq], in_=cT[:, :, 0:nseq], func=AF.Silu), r=[("cT", k) for k in range(8)], w=["cTs"])
            for kc in range(8):
                tr.op("dve", lambda h, kc=kc: h.tensor_copy(out=cTexp[:, kc, :].rearrange("p (s r) -> p s r", r=rep),
                                                             in_=cTs[:, kc, 0:nseq].unsqueeze(2).to_broadcast([128, nseq, rep])),
                      r=["cTs"], w=["cTexp"])
            dma(ada[:], b_ada.partition_broadcast(128), w=["ada"])
            for c in range(12):
                sl, key = slab("w_ada", c * 512, (c + 1) * 512)
                i = nxt("pmm", 2)
                for kc in range(8):
                    mm(pmm[i][:, :], cTexp[:, kc, :], sl[:, kc, :], kc == 0, kc == 7, r=["cTexp", key], w=[("pmm", i)])
                tr.op("dve", lambda h, c=c, i=i: h.tensor_tensor(out=ada[:, c * 512:(c + 1) * 512], in0=pmm[i][:, :], in1=ada[:, c * 512:(c + 1) * 512], op=ALU.add),
                      r=[("pmm", i), "ada"], w=["ada"])
            for off in (1 * D, 4 * D):
                tr.op("dve", lambda h, off=off: h.tensor_scalar_add(out=ada[:, off:off + D], in0=ada[:, off:off + D], scalar1=1.0), r=["ada"], w=["ada"])

        def layer_norm(src, gi, dst, skey, dkey):
            dma(lnbc[:, 0, :], lnd[gi].partition_broadcast(128), w=[("lnbc", 0)])
            dma(lnbc[:, 1, :], lnd[gi + 1].partition_broadcast(128), w=[("lnbc", 1)])
            gi = 0
            tr.op("dve", lambda h: h.reduce_sum(out=st8[:, 0:1], in_=src, axis=AX.X), r=[skey], w=["st8"])
            tr.op("act", lambda h: h.activation(out=tmp2[:], in_=src, func=AF.Square, accum_out=st8[:, 1:2]), r=[skey, "st8"], w=["tmp2", "st8"])
            tr.op("dve", lambda h: h.tensor_scalar(out=st8[:, 2:4], in0=st8[:, 0:2], scalar1=1.0 / D, scalar2=None, op0=ALU.mult), r=["st8"], w=["st8"])
            tr.op("dve", lambda h: h.tensor_tensor(out=st8[:, 4:5], in0=st8[:, 2:3], in1=st8[:, 2:3], op=ALU.mult), r=["st8"], w=["st8"])
            tr.op("dve", lambda h: h.tensor_tensor(out=st8[:, 5:6], in0=st8[:, 3:4], in1=st8[:, 4:5], op=ALU.subtract), r=["st8"], w=["st8"])
            tr.op("dve", lambda h: h.tensor_scalar_add(out=st8[:, 5:6], in0=st8[:, 5:6], scalar1=LN_EPS), r=["st8"], w=["st8"])
            tr.op("act", lambda h: h.activation(out=st8[:, 7:8], in_=st8[:, 5:6], func=AF.Sqrt), r=["st8"], w=["st8"])
            tr.op("dve", lambda h: h.reciprocal(out=st8[:, 6:7], in_=st8[:, 7:8]), r=["st8"], w=["st8"])
            tr.op("dve", lambda h: h.tensor_scalar(out=dst, in0=src, scalar1=st8[:, 2:3], scalar2=st8[:, 6:7], op0=ALU.subtract, op1=ALU.mult), r=[skey, "st8"], w=[dkey])
            tr.op("dve", lambda h: h.tensor_tensor(out=dst, in0=dst, in1=lnbc[:, gi, :], op=ALU.mult), r=[dkey, ("lnbc", gi)], w=[dkey])
            tr.op("dve", lambda h: h.tensor_tensor(out=dst, in0=dst, in1=lnbc[:, gi + 1, :], op=ALU.add), r=[dkey, ("lnbc", gi + 1)], w=[dkey])

        def rms_norm(src, n, gbc, gkey, dst, skey, dkey, dst_bf=None, bkey=None):
            tr.op("act", lambda h: h.activation(out=tmp2[:, 0:n], in_=src, func=AF.Square, accum_out=st8[:, 8:9]), r=[skey, "st8"], w=["tmp2", "st8"])
            tr.op("dve", lambda h: h.tensor_scalar(out=st8[:, 9:10], in0=st8[:, 8:9], scalar1=1.0 / n, scalar2=RMS_EPS, op0=ALU.mult, op1=ALU.add), r=["st8"], w=["st8"])
            tr.op("act", lambda h: h.activation(out=st8[:, 9:10], in_=st8[:, 9:10], func=AF.Sqrt), r=["st8"], w=["st8"])
            tr.op("dve", lambda h: h.reciprocal(out=st8[:, 10:11], in_=st8[:, 9:10]), r=["st8"], w=["st8"])
            tr.op("dve", lambda h: h.tensor_scalar(out=dst, in0=src, scalar1=st8[:, 10:11], scalar2=None, op0=ALU.mult), r=[skey, "st8"], w=[dkey])
            tr.op("dve", lambda h: h.tensor_tensor(out=dst, in0=dst, in1=gbc, op=ALU.mult), r=[dkey, gkey], w=[dkey])
            if dst_bf is not None:
                tr.op("act", lambda h: h.copy(out=dst_bf, in_=dst), r=[dkey], w=[bkey])

        def rope(src, dst, cos, sin, nh, skey, dkey, ckeys):
            cb = cos.unsqueeze(1).to_broadcast([128, nh, 16]) if nh > 1 else cos
            sbc = sin.unsqueeze(1).to_broadcast([128, nh, 16]) if nh > 1 else sin
            if nh > 1:
                x1_, x2_ = src[:, :, 0:16], src[:, :, 16:32]
                d1, d2 = dst[:, :, 0:16], dst[:, :, 16:32]
                t1 = tmp2[:, 0:nh * 16].rearrange("p (h f) -> p h f", f=16)
                t2 = tmp2[:, 256:256 + nh * 16].rearrange("p (h f) -> p h f", f=16)
            else:
                x1_, x2_ = src[:, 0:16], src[:, 16:32]
                d1, d2 = dst[:, 0:16], dst[:, 16:32]
                t1 = tmp2[:, 0:16]
                t2 = tmp2[:, 256:272]
            rk = [skey] + list(ckeys)
            tr.op("dve", lambda h: h.tensor_tensor(out=t1, in0=x1_, in1=cb, op=ALU.mult), r=rk, w=["tmp2"])
            tr.op("dve", lambda h: h.tensor_tensor(out=t2, in0=x2_, in1=sbc, op=ALU.mult), r=rk, w=["tmp2"])
            tr.op("dve", lambda h: h.tensor_tensor(out=d1, in0=t1, in1=t2, op=ALU.subtract), r=["tmp2"], w=[dkey])
            tr.op("dve", lambda h: h.tensor_tensor(out=t1, in0=x1_, in1=sbc, op=ALU.mult), r=rk, w=["tmp2"])
            tr.op("dve", lambda h: h.tensor_tensor(out=t2, in0=x2_, in1=cb, op=ALU.mult), r=rk, w=["tmp2"])
            tr.op("dve", lambda h: h.tensor_tensor(out=d2, in0=t1, in1=t2, op=ALU.add), r=["tmp2"], w=[dkey])

        def tile(x_src, y_dst, lat_dst, kr_dst, cos, sin, ckeys, prompt, ti=0, conv_dst=None, first=False, last=False):
            xi = 0
            X = xt[xi]
            xk = ("x", xi)
            dma(X[:], x_src, w=[xk])
            tr.op("dve", lambda h: h.tensor_tensor(out=tmp[:], in0=X[:], in1=ada[:, D:2 * D], op=ALU.mult), r=[xk, "ada"], w=["tmp"])
            tr.op("dve", lambda h: h.tensor_tensor(out=hbf[:], in0=tmp[:], in1=ada[:, 0:D], op=ALU.add), r=["tmp", "ada"], w=["hbf"])
            transposes(lambda j: hbf[:, j * 128:(j + 1) * 128], 8, hT, "hT", "hbf")
            ck("tile: h transposes")
            for (c0, c1) in ((0, 512), (512, 672)):
                sl, key = slab("w_in", c0, c1)
                i = nxt("pmm", 2)
                for kc in range(8):
                    mm(pmm[i][:, 0:c1 - c0], hT[:, kc, :], sl[:, kc, :], kc == 0, kc == 7, r=["hT", key], w=[("pmm", i)])
                tr.op("act", lambda h, i=i, c0=c0, c1=c1: h.copy(out=z[:, c0:c1], in_=pmm[i][:, 0:c1 - c0]), r=[("pmm", i)], w=["z"])
            ck("tile: z")
            rms_norm(z[:, 0:QL], QL, qgbc[:], "qgbc", tmp[:, 0:QL], "z", "tmp", qn[:], "qn")
            ck("q: rms")
            transposes(lambda j: qn[:, j * 128:(j + 1) * 128], 3, qnT, "qnT", "qn")
            ck("q: qnT")
            for (c0, c1) in ((0, 384), (384, 768)):
                sl, key = slab("w_uq", c0, c1)
                i = nxt("pmm", 2)
                for kc in range(3):
                    mm(pmm[i][:, 0:384], qnT[:, kc, :], sl[:, kc, :], kc == 0, kc == 2, r=["qnT", key], w=[("pmm", i)])
                ck("q: mm%d" % c0)
                pq = pmm[i][:, 0:384].rearrange("p (h e) -> p h e", e=QKH)
                tr.op("act", lambda h, pq=pq, c0=c0: h.copy(out=q[:, c0 // QKH:c0 // QKH + 4, :], in_=pq[:, :, QKN:QKH]), r=[("pmm", i)], w=["q"])
                ck("q: evA%d" % c0)
                tr.op("act", lambda h, pq=pq, c0=c0: h.copy(out=qb[:, c0 // QKH:c0 // QKH + 4, 0:QKN], in_=pq[:, :, 0:QKN]), r=[("pmm", i)], w=["qb"])
                ck("q: evB%d" % c0)
            ck("q: wuq")
            rope(q[:, :, :], qb[:, :, QKN:QKH], cos, sin, NH, "q", "qb", ckeys)
            ck("tile: q path")
            rms_norm(z[:, QL:QL + KVL], KVL, kvgbc[:], "kvgbc", ckv[:], "z", "ckv", ckvb[:], "ckvb")
            dma(lat_dst, ckv[:], r=["ckv"])
            rope(z[:, QL + KVL:672], krt[:], cos, sin, 1, "z", "krt", ckeys)
            dma(kr_dst, krt[:], r=["krt"])
            transposes(lambda j: ckvb[:, j * 128:(j + 1) * 128], 2, ckvT, "ckvT", "ckvb")
            ck("tile: kv path")
            slc_, keyc = slab("w_in", 672 + CD, 672 + 2 * CD)
            slv_, keyv = slab("w_in", 672 + 2 * CD, 672 + 3 * CD)
            for j in range(4):
                ic = nxt("pmm", 2)
                for kc in range(8):
                    mm(pmm[ic][:, 0:128], slc_[:, kc, j * 128:(j + 1) * 128], hT[:, kc, :], kc == 0, kc == 7, r=["hT", keyc], w=[("pmm", ic)])
                tr.op("act", lambda h, j=j, ic=ic: h.copy(out=cgs[:, j, :], in_=pmm[ic][:, 0:128]), r=[("pmm", ic)], w=["cgs"])
                iv = nxt("pmm", 2)
                for kc in range(8):
                    mm(pmm[iv][:, 0:128], slv_[:, kc, j * 128:(j + 1) * 128], hT[:, kc, :], kc == 0, kc == 7, r=["hT", keyv], w=[("pmm", iv)])
                if prompt:
                    uflat = uT[:].rearrange("p c s t -> p c (s t)")
                    tr.op("dve", lambda h, j=j, iv=iv: h.tensor_tensor(out=uflat[:, j, 2:130], in0=cgs[:, j, :], in1=pmm[iv][:, 0:128], op=ALU.mult), r=["cgs", ("pmm", iv)], w=["uT"])
                else:
                    tr.op("dve", lambda h, j=j, iv=iv: h.tensor_tensor(out=uT[:, j, :, 2:2 + T], in0=cgs[:, j, :].rearrange("p (s t) -> p s t", t=T),
                                                                      in1=pmm[iv][:, 0:128].rearrange("p (s t) -> p s t", t=T), op=ALU.mult), r=["cgs", ("pmm", iv)], w=["uT"])
            sl, key = slab("w_in", 672, 672 + CD)
            for j in range(4):
                ib = nxt("pmm", 2)
                for kc in range(8):
                    mm(pmm[ib][:, 0:128], sl[:, kc, j * 128:(j + 1) * 128], hT[:, kc, :], kc == 0, kc == 7, r=["hT", key], w=[("pmm", ib)])
                if prompt:
                    uflat = uT[:].rearrange("p c s t -> p c (s t)")
                    u0, u1, u2 = uflat[:, j, 0:128], uflat[:, j, 1:129], uflat[:, j, 2:130]
                    cv = cvT[:, j, :]
                    bsrc = pmm[ib][:, 0:128]
                    cbo = cbT[:, j, :]
                else:
                    u0, u1, u2 = uT[:, j, :, 0:T], uT[:, j, :, 1:1 + T], uT[:, j, :, 2:2 + T]
                    cv = cvT[:, j, :].rearrange("p (s t) -> p s t", t=T)
                    bsrc = pmm[ib][:, 0:128].rearrange("p (s t) -> p s t", t=T)
                    cbo = cbT[:, j, :].rearrange("p (s t) -> p s t", t=T)
                tr.op("dve", lambda h, j=j, u0=u0, cv=cv: h.tensor_scalar(out=cv, in0=u0, scalar1=convw[:, j, 0:1], scalar2=None, op0=ALU.mult), r=["uT", "convw"], w=["cvT"])
                tr.op("dve", lambda h, j=j, u1=u1, cv=cv: h.scalar_tensor_tensor(out=cv, in0=u1, scalar=convw[:, j, 1:2], in1=cv, op0=ALU.mult, op1=ALU.add), r=["uT", "convw", "cvT"], w=["cvT"])
                tr.op("dve", lambda h, j=j, u2=u2, cv=cv: h.scalar_tensor_tensor(out=cv, in0=u2, scalar=convw[:, j, 2:3], in1=cv, op0=ALU.mult, op1=ALU.add), r=["uT", "convw", "cvT"], w=["cvT"])
                tr.op("dve", lambda h, cv=cv, bsrc=bsrc, cbo=cbo: h.tensor_tensor(out=cbo, in0=cv, in1=bsrc, op=ALU.mult), r=["cvT", ("pmm", ib)], w=["cbT"])
            if prompt:
                uflat = uT[:].rearrange("p c s t -> p c (s t)")
                if last:
                    for c in range(4):
                        dma(conv_dst[:, c * 128:(c + 1) * 128].rearrange("t p -> p t"), uflat[:, c, 128:130], r=["uT"], nonc=True)
                tr.op("pool", lambda h: h.tensor_copy(out=uflat[:, :, 0:2], in_=uflat[:, :, 128:130]), r=["uT"], w=["uT"])
            else:
                for c in range(4):
                    for t2 in range(2):
                        dma(conv_dst[:, c * 128:(c + 1) * 128].rearrange("(s t) p -> p s t", t=2)[:, :, t2], uT[:, c, :, T + t2], r=["uT"], nonc=True)
            ck("tile: conv")
            if prompt:
                prompt_attention(ti)
            else:
                sample_attention()
            ck("tile: attention")
            for c in range(2):
                for gi, gdst, gkey in ((0, sga, "sga"), (1, sgc, "sgc")):
                    c0 = 672 + 3 * CD + gi * D + c * 512
                    sl2, key2 = slab("w_in", c0, c0 + 512)
                    i = nxt("pmm", 2)
                    for kc in range(8):
                        mm(pmm[i][:, :], hT[:, kc, :], sl2[:, kc, :], kc == 0, kc == 7, r=["hT", key2], w=[("pmm", i)])
                    tr.op("act", lambda h, i=i, gdst=gdst: h.activation(out=gdst[:, :], in_=pmm[i][:, :], func=AF.Sigmoid), r=[("pmm", i)], w=[gkey])
                sla, ka = slab("w_oa", c * 512, (c + 1) * 512)
                ia = nxt("pmm", 2)
                for kc in range(4):
                    mm(pmm[ia][:, :], oT[:, kc, :], sla[:, kc, :], kc == 0, kc == 3, r=["oT", ka], w=[("pmm", ia)])
                tr.op("dve", lambda h, c=c, ia=ia: h.tensor_tensor(out=tmp[:, c * 512:(c + 1) * 512], in0=pmm[ia][:, :], in1=sga[:, :], op=ALU.mult), r=[("pmm", ia), "sga"], w=["tmp"])
                slc, kc_ = slab("w_oc", c * 512, (c + 1) * 512)
                ic = nxt("pmm", 2)
                for kc in range(4):
                    mm(pmm[ic][:, :], cbT[:, kc, :], slc[:, kc, :], kc == 0, kc == 3, r=["cbT", kc_], w=[("pmm", ic)])
                tr.op("dve", lambda h, c=c, ic=ic: h.tensor_tensor(out=tmp2[:, c * 512:(c + 1) * 512], in0=pmm[ic][:, :], in1=sgc[:, :], op=ALU.mult), r=[("pmm", ic), "sgc"], w=["tmp2"])
            tr.op("dve", lambda h: h.tensor_tensor(out=hbf[:], in0=tmp[:], in1=tmp2[:], op=ALU.add), r=["tmp", "tmp2"], w=["hbf"])
            transposes(lambda j: hbf[:, j * 128:(j + 1) * 128], 8, mT, "mT", "hbf")
            ck("tile: m")
            for c in range(2):
                sl3, k3 = slab("w_o", c * 512, (c + 1) * 512)
                i = nxt("pmm", 2)
                for kc in range(8):
                    mm(pmm[i][:, :], mT[:, kc, :], sl3[:, kc, :], kc == 0, kc == 7, r=["mT", k3], w=[("pmm", i)])
                tr.op("dve", lambda h, c=c, i=i: h.tensor_tensor(out=tmp[:, c * 512:(c + 1) * 512], in0=pmm[i][:, :], in1=ada[:, 2 * D + c * 512:2 * D + (c + 1) * 512], op=ALU.mult), r=[("pmm", i), "ada"], w=["tmp"])
            tr.op("dve", lambda h: h.scalar_tensor_tensor(out=tmp[:], in0=X[:], scalar=ALPHA, in1=tmp[:], op0=ALU.mult, op1=ALU.add), r=[xk, "tmp"], w=["tmp"])
            layer_norm(tmp[:], 0, X[:], "tmp", xk)
            ck("tile: LN1")
            tr.op("dve", lambda h: h.tensor_tensor(out=tmp[:], in0=X[:], in1=ada[:, 4 * D:5 * D], op=ALU.mult), r=[xk, "ada"], w=["tmp"])
            tr.op("dve", lambda h: h.tensor_tensor(out=hbf[:], in0=tmp[:], in1=ada[:, 3 * D:4 * D], op=ALU.add), r=["tmp", "ada"], w=["hbf"])
            transposes(lambda j: hbf[:, j * 128:(j + 1) * 128], 8, hT, "hT", "hbf")
            ck("tile: h2")
            chunks = [(c0, min(DFF, c0 + 512)) for c0 in range(0, DFF, 512)]
            pend = {}

            def ffn_mm(c0, c1):
                n = c1 - c0
                s1, k1 = slab("w_ff1", c0, c1)
                i1 = nxt("pmm", 2)
                for kc in range(8):
                    mm(pmm[i1][:, 0:n], hT[:, kc, :], s1[:, kc, :], kc == 0, kc == 7, r=["hT", k1], w=[("pmm", i1)])
                tr.op("act", lambda h: h.activation(out=tmp[:, 0:n], in_=pmm[i1][:, 0:n], func=AF.Silu), r=[("pmm", i1)], w=["tmp"])
                s3, k3 = slab("w_ff3", c0, c1)
                i3 = nxt("pmm", 2)
                for kc in range(8):
                    mm(pmm[i3][:, 0:n], hT[:, kc, :], s3[:, kc, :], kc == 0, kc == 7, r=["hT", k3], w=[("pmm", i3)])
                pend[c0] = i3

            def ffn_mult(c0, c1):
                n = c1 - c0
                i3 = pend.pop(c0)
                tr.op("dve", lambda h: h.tensor_tensor(out=ab[:, 0:n], in0=tmp[:, 0:n], in1=pmm[i3][:, 0:n], op=ALU.mult), r=["tmp", ("pmm", i3)], w=["ab"])

            def ffn_tr(c0, c1):
                n = c1 - c0
                g = c0 // 128
                transposes(lambda j: ab[:, j * 128:(j + 1) * 128], n // 128, aT[:, g:g + n // 128, :], "aT", "ab")

            ffn_mm(*chunks[0])
            ffn_mult(*chunks[0])
            for ci in range(len(chunks)):
                if ci + 1 < len(chunks):
                    ffn_mm(*chunks[ci + 1])
                ffn_tr(*chunks[ci])
                if ci + 1 < len(chunks):
                    ffn_mult(*chunks[ci + 1])
            for c in range(2):
                i = nxt("pmm", 2)
                for (ka, kb) in ((0, 8), (8, 16), (16, 22)):
                    s2, k2 = slab("w_ff2", c * 512, (c + 1) * 512, ka, kb)
                    for kc in range(ka, kb):
                        mm(pmm[i][:, :], aT[:, kc, :], s2[:, kc - ka, :], kc == 0, kc == 21, r=["aT", k2], w=[("pmm", i)])
                tr.op("dve", lambda h, c=c, i=i: h.tensor_tensor(out=tmp[:, c * 512:(c + 1) * 512], in0=pmm[i][:, :], in1=ada[:, 5 * D + c * 512:5 * D + (c + 1) * 512], op=ALU.mult), r=[("pmm", i), "ada"], w=["tmp"])
            tr.op("dve", lambda h: h.scalar_tensor_tensor(out=tmp[:], in0=X[:], scalar=ALPHA, in1=tmp[:], op0=ALU.mult, op1=ALU.add), r=[xk, "tmp"], w=["tmp"])
            ck("tile: FFN")
            layer_norm(tmp[:], 2, X[:], "tmp", xk)
            dma(y_dst, X[:], r=[xk])

        def prompt_attention(ti):
            slk, kk_ = slab("w_ukv", 0, NH * 128)
            for half in range(2):
                i = nxt("pmm", 2)
                for kc in range(2):
                    mm(pmm[i][:, :], ckvT[:, kc, :], slk[:, kc, half * 512:(half + 1) * 512], kc == 0, kc == 1, r=["ckvT", kk_], w=[("pmm", i)])
                pv = pmm[i][:, :].rearrange("p (h e) -> p h e", e=128)
                tr.op("act", lambda h, half=half, pv=pv: h.copy(out=kk[:, half * 4:(half + 1) * 4, 0:QKN], in_=pv[:, :, 0:QKN]), r=[("pmm", i)], w=["kk"])
                tr.op("act", lambda h, half=half, pv=pv: h.copy(out=Vt[:, ti, half * 256:(half + 1) * 256].rearrange("p (h e) -> p h e", e=VH), in_=pv[:, :, QKN:128]), r=[("pmm", i)], w=[("Vt", ti)])
            tr.op("pool", lambda h: h.tensor_copy(out=krb[:], in_=krt[:]), r=["krt"], w=["krb"])
            tr.op("pool", lambda h: h.tensor_copy(out=kk[:, :, QKN:QKH], in_=krb[:].unsqueeze(1).to_broadcast([128, NH, QKR])), r=["krb"], w=["kk"])
            i = nxt("ptr", 2)
            for hh in range(NH):
                tr.op("pe", lambda h, hh=hh, i=i: h.transpose(ptr[i][0:QKH, hh * 128:(hh + 1) * 128], kk[:, hh, :], ident[:]), r=["kk", "ident"], w=[("ptr", i)])
            tr.op("dve", lambda h, i=i: h.tensor_copy(out=Kt[0:QKH, :, ti * 128:(ti + 1) * 128], in_=ptr[i][0:QKH, :].rearrange("p (j t) -> p j t", t=128)), r=[("ptr", i)], w=[("Kt", ti)])
            i = nxt("ptr", 2)
            for hh in range(NH):
                tr.op("pe", lambda h, hh=hh, i=i: h.transpose(ptr[i][0:QKH, hh * 128:(hh + 1) * 128], qb[:, hh, :], ident[:]), r=["qb", "ident"], w=[("ptr", i)])
            tr.op("dve", lambda h, i=i: h.tensor_copy(out=qT[0:QKH, :, :], in_=ptr[i][0:QKH, :].rearrange("p (j t) -> p j t", t=128)), r=[("ptr", i)], w=["qT"])
            nk = ti + 1
            kkeys = [("Kt", t) for t in range(nk)]
            vkeys = [("Vt", t) for t in range(nk)]
            nch = (nk + 3) // 4

            def head_front(hh):
                par = hh % 2
                sh_, shk = sth[par], ("sth", par)
                Pp = P[par]
                for ch in range(nch):
                    k0 = ch * 512
                    k1 = min(nk * 128, k0 + 512)
                    isl = (ch == nch - 1)
                    mm(psc[ch][:, 0:k1 - k0], qT[0:QKH, hh, :], Kt[0:QKH, hh, k0:k1], True, not isl, r=["qT"] + kkeys, w=[("psc", ch)])
                    if isl:
                        d0 = (nk - 1) * 128 - k0
                        mm(psc[ch][:, d0:d0 + 128], ident[:], maskc[:], False, True, r=["ident", "maskc"], w=[("psc", ch)])
                for ch in range(nch):
                    k0 = ch * 512
                    k1 = min(nk * 128, k0 + 512)
                    tr.op("dve", lambda h, ch=ch, n=k1 - k0: h.reduce_max(out=sh_[:, 1 + ch:2 + ch], in_=psc[ch][:, 0:n], axis=AX.X), r=[("psc", ch)], w=[shk])
                tr.op("dve", lambda h: h.reduce_max(out=sh_[:, 0:1], in_=sh_[:, 1:1 + nch], axis=AX.X), r=[shk], w=[shk])
                tr.op("dve", lambda h: h.tensor_scalar(out=sh_[:, 0:1], in0=sh_[:, 0:1], scalar1=-SCALE, scalar2=None, op0=ALU.mult), r=[shk], w=[shk])
                for ch in range(nch):
                    k0 = ch * 512
                    k1 = min(nk * 128, k0 + 512)
                    tr.op("act", lambda h, ch=ch, k0=k0, k1=k1: h.activation(out=Pp[:, k0:k1], in_=psc[ch][:, 0:k1 - k0], func=AF.Exp, bias=sh_[:, 0:1], scale=SCALE, accum_out=sh_[:, 1 + ch:2 + ch]),
                          r=[("psc", ch), shk], w=[("P", par, ch), shk])
                tr.op("dve", lambda h: h.reduce_sum(out=sh_[:, 5:6], in_=sh_[:, 1:1 + nch], axis=AX.X), r=[shk], w=[shk])
                tr.op("dve", lambda h: h.reciprocal(out=sh_[:, 6:7], in_=sh_[:, 5:6]), r=[shk], w=[shk])

            def head_back(hh):
                par = hh % 2
                sh_, shk = sth[par], ("sth", par)
                Pp = P[par]
                io = nxt("pmm", 2)
                for g in range(0, nk, 8):
                    ng = min(8, nk - g)
                    pkeys = [("P", par, c) for c in range(g // 4, (g + ng + 3) // 4)]
                    i = nxt("ptr", 2)
                    for j in range(ng):
                        tr.op("pe", lambda h, j=j, i=i, g=g: h.transpose(ptr[i][:, j * 128:(j + 1) * 128], Pp[:, (g + j) * 128:(g + j + 1) * 128], ident[:]),
                              r=pkeys + ["ident"], w=[("ptr", i)])
                    tr.op("dve", lambda h, i=i, ng=ng: h.tensor_copy(out=PT[:, 0:ng, :], in_=ptr[i][:, 0:ng * 128].rearrange("p (j t) -> p j t", t=128)), r=[("ptr", i)], w=["PT"])
                    for t in range(g, g + ng):
                        mm(pmm[io][:, 0:VH], PT[:, t - g, :], Vt[:, t, hh * VH:(hh + 1) * VH], t == 0, t == nk - 1, r=["PT"] + vkeys, w=[("pmm", io)])
                tr.op("act", lambda h, io=io: h.activation(out=ob[:, hh * VH:(hh + 1) * VH], in_=pmm[io][:, 0:VH], func=AF.Copy, scale=sh_[:, 6:7]), r=[("pmm", io), shk], w=["hbf"])

            head_front(0)
            for hh in range(NH):
                if hh + 1 < NH:
                    head_front(hh + 1)
                head_back(hh)
            transposes(lambda j: ob[:, j * 128:(j + 1) * 128], 4, oT, "oT", "hbf")

        def sample_attention():
            i = nxt("ptr", 2)
            qbf = qb[:]
            tr.op("dve", lambda h: h.tensor_copy(out=hbf[:, 0:512].rearrange("p (h e) -> p h e", e=QKN), in_=qb[:, :, 0:QKN]), r=["qb"], w=["hbf"])
            for pr in range(4):
                tr.op("pe", lambda h, pr=pr, i=i: h.transpose(ptr[i][:, pr * 128:(pr + 1) * 128], hbf[:, pr * 128:(pr + 1) * 128], ident[:]), r=["hbf", "ident"], w=[("ptr", i)])
            tr.op("dve", lambda h, i=i: h.tensor_copy(out=qnpT[:], in_=ptr[i][:, 0:512].rearrange("p (j t) -> p j t", t=128)), r=[("ptr", i)], w=["qnpT"])
            for kc in range(2):
                for hh in range(NH):
                    pr, off = hh // 2, (hh % 2) * 64
                    ia = nxt("pmm", 2)
                    mm(pmm[ia][:, 0:128], wukT[off:off + 64, pr, kc * 128:(kc + 1) * 128], qnpT[off:off + 64, pr, :], True, True, r=["wukT", "qnpT"], w=[("pmm", ia)])
                    tr.op("act", lambda h, kc=kc, hh=hh, ia=ia: h.copy(out=QaT[:, kc, :, hh * T:(hh + 1) * T], in_=pmm[ia][:, 0:128].rearrange("p (s t) -> p s t", t=T)), r=[("pmm", ia)], w=["QaT"])
            tr.op("dve", lambda h: h.tensor_copy(out=hbf[:, 512:768].rearrange("p (h e) -> p h e", e=QKR), in_=qb[:, :, QKN:QKH]), r=["qb"], w=["hbf"])
            i = nxt("ptr", 2)
            for hh in range(NH):
                tr.op("pe", lambda h, hh=hh, i=i: h.transpose(ptr[i][0:QKR, hh * 128:(hh + 1) * 128], hbf[:, 512 + hh * QKR:512 + (hh + 1) * QKR], ident[:]), r=["hbf", "ident"], w=[("ptr", i)])
            for hh in range(NH):
                tr.op("dve", lambda h, hh=hh, i=i: h.tensor_copy(out=QrT[:, :, hh * T:(hh + 1) * T], in_=ptr[i][0:QKR, hh * 128:(hh + 1) * 128].rearrange("p (s t) -> p s t", t=T)), r=[("ptr", i)], w=["QrT"])
            XTn = sb("XTn", [128, 3, 128], BF16)
            tr.op("pool", lambda h: h.tensor_copy(out=krb[:], in_=krt[:]), r=["krt"], w=["krb"])
            tr.op("pool", lambda h: h.tensor_copy(out=XTn[:, 0:2, :], in_=ckvT[:]), r=["ckvT"], w=["XTn"])
            i = nxt("ptr", 2)
            tr.op("pe", lambda h, i=i: h.transpose(ptr[i][0:QKR, 0:128], krb[:], ident[:]), r=["krb", "ident"], w=[("ptr", i)])
            tr.op("dve", lambda h, i=i: h.tensor_copy(out=XTn[0:QKR, 2, :], in_=ptr[i][0:QKR, 0:128]), r=[("ptr", i)], w=["XTn"])

            npg_cnt = [0]
            KPG = NPG

            def flash_front(par, s, xts, KP, mask_off=None):
                for gi, (xv, xk_) in enumerate(xts):
                    o = psc[par][0:64, gi * KP:(gi + 1) * KP]
                    lastmm = mask_off is None
                    mm(o, QaT[:, 0, s, :], xv[:, 0, 0:KP], True, False, r=["QaT", xk_], w=[("psc", par)])
                    mm(o, QaT[:, 1, s, :], xv[:, 1, 0:KP], False, False, r=["QaT", xk_], w=[("psc", par)])
                    mm(o, QrT[:, s, :], xv[0:QKR, 2, 0:KP], False, lastmm, r=["QrT", xk_], w=[("psc", par)])
                    if mask_off is not None:
                        mm(o, ident[0:64, 0:64], mbig[:, mask_off:mask_off + KP], False, True, r=["ident", "mbig"], w=[("psc", par)])

            def flash_back(par, g, xtoks, KP, first):
                W = g * KP
                sc = psc[par]
                pa = psc[2 + par]
                pk = ("psc", par)
                pak = ("psc", 2 + par)
                Psb, PTb = Ps[par], PTs[par]
                Pk, PTk = ("Ps", par), ("PTs", par)
                tr.op("dve", lambda h: h.reduce_max(out=sm[:, 1:2], in_=sc[0:64, 0:W], axis=AX.X), r=[pk], w=["sm"])
                if not first:
                    tr.op("dve", lambda h: h.tensor_tensor(out=sm[:, 1:2], in0=sm[:, 1:2], in1=sm[:, 0:1], op=ALU.max), r=["sm"], w=["sm"])
                    tr.op("dve", lambda h: h.tensor_tensor(out=sm[:, 6:7], in0=sm[:, 0:1], in1=sm[:, 1:2], op=ALU.subtract), r=["sm"], w=["sm"])
                    tr.op("act", lambda h: h.activation(out=sm[:, 3:4], in_=sm[:, 6:7], func=AF.Exp, scale=SCALE), r=["sm"], w=["sm"])
                tr.op("dve", lambda h: h.tensor_scalar(out=sm[:, 2:3], in0=sm[:, 1:2], scalar1=-SCALE, scalar2=None, op0=ALU.mult), r=["sm"], w=["sm"])
                tr.op("act", lambda h: h.activation(out=Psb[:, 0:W], in_=sc[0:64, 0:W], func=AF.Exp, bias=sm[:, 2:3], scale=SCALE, accum_out=sm[:, 5:6]), r=[pk, "sm"], w=[Pk, "sm"])
                tr.op("dve", lambda h: h.tensor_copy(out=sm[:, 0:1], in_=sm[:, 1:2]), r=["sm"], w=["sm"])
                if first:
                    tr.op("dve", lambda h: h.tensor_copy(out=sm[:, 4:5], in_=sm[:, 5:6]), r=["sm"], w=["sm"])
                else:
                    tr.op("dve", lambda h: h.scalar_tensor_tensor(out=sm[:, 4:5], in0=sm[:, 4:5], scalar=sm[:, 3:4], in1=sm[:, 5:6], op0=ALU.mult, op1=ALU.add), r=["sm"], w=["sm"])
                ip = nxt("ptr", 2)
                for gi in range(g):
                    tr.op("pe", lambda h, gi=gi, ip=ip: h.transpose(ptr[ip][0:KP, gi * 128:gi * 128 + 64], Psb[:, gi * KP:(gi + 1) * KP], ident[0:64, 0:64]), r=[Pk, "ident"], w=[("ptr", ip)])
                tr.op("dve", lambda h, ip=ip: h.tensor_copy(out=PTb[0:KP, 0:g, :], in_=ptr[ip][0:KP, 0:g * 128].rearrange("p (j t) -> p j t", t=128)[:, :, 0:64]), r=[("ptr", ip)], w=[PTk])
                for gi, (xa, xak) in enumerate(xtoks):
                    mm(pa[0:64, 0:KVL], PTb[0:KP, gi, :], xa, gi == 0, gi == g - 1, r=[PTk, xak], w=[pak])
                if first:
                    tr.op("dve", lambda h: h.tensor_copy(out=accs[:], in_=pa[0:64, 0:KVL]), r=[pak], w=["accs"])
                else:
                    tr.op("dve", lambda h: h.scalar_tensor_tensor(out=accs[:], in0=accs[:], scalar=sm[:, 3:4], in1=pa[0:64, 0:KVL], op0=ALU.mult, op1=ALU.add), r=["accs", "sm", pak], w=["accs"])

            def finalize(s):
                tr.op("dve", lambda h: h.reciprocal(out=sm[:, 7:8], in_=sm[:, 4:5]), r=["sm"], w=["sm"])
                tr.op("act", lambda h: h.activation(out=olat[:], in_=accs[:], func=AF.Copy, scale=sm[:, 7:8]), r=["accs", "sm"], w=["olat"])
                io = nxt("ptr", 2)
                for kc in range(2):
                    tr.op("pe", lambda h, kc=kc, io=io: h.transpose(ptr[io][:, kc * 128:kc * 128 + 64], olat[:, kc * 128:(kc + 1) * 128], ident[0:64, 0:64]), r=["olat", "ident"], w=[("ptr", io)])
                tr.op("dve", lambda h, io=io: h.tensor_copy(out=olT[:], in_=ptr[io][:, 0:256].rearrange("p (j t) -> p j t", t=128)[:, :, 0:64]), r=[("ptr", io)], w=["olT"])
                for pr in range(4):
                    ia = nxt("pmm", 2)
                    for sub in range(2):
                        hh = pr * 2 + sub
                        for kc in range(2):
                            mm(pmm[ia][sub * 64:(sub + 1) * 64, 0:T], wuvb[:, kc, hh * VH:(hh + 1) * VH], olT[:, kc, hh * T:(hh + 1) * T], kc == 0, kc == 1, r=["wuvb", "olT"], w=[("pmm", ia)])
                    tr.op("act", lambda h, pr=pr, ia=ia, s=s: h.copy(out=oT[:, pr, s * T:(s + 1) * T], in_=pmm[ia][:, 0:T]), r=[("pmm", ia)], w=["oT"])

            dma(ptT[0:KPG, :], pt_d.rearrange("s p -> p s"), w=["ptT"], nonc=True)
            groups = []
            for s in range(SB):
                groups.append(("new", s, None))
                for c in range(NCH):
                    groups.append(("page", s, c))
            state = {}

            issued = [0]
            NPGRP = SB * NCH

            def ensure_issued(upto):
                while issued[0] < min(upto, NPGRP):
                    m = issued[0]
                    issued[0] += 1
                    ms, mc, mk = m // NCH, m % NCH, m % 4
                    tr.op("dve", lambda h, ms=ms, mc=mc, mk=mk: h.tensor_scalar(out=idxr[mk][0:KPG, :], in0=ptT[0:KPG, ms:ms + 1], scalar1=NCH, scalar2=mc, op0=ALU.mult, op1=ALU.add),
                          r=["ptT"], w=[("idxr", mk)])
                    tr.op("pool", lambda h, mk=mk: h.indirect_dma_start(out=Xb[mk][0:KPG].rearrange("p t l -> p (t l)"), out_offset=None, in_=pool_d[:, :],
                                                                       in_offset=bass.IndirectOffsetOnAxis(ap=idxr[mk][0:KPG, 0:1], axis=0)),
                          r=[("idxr", mk)], w=[("Xb", mk)], dma=True)

            def front(gidx):
                kind, s, c = groups[gidx]
                par = gidx % 2
                if kind == "new":
                    flash_front(par, s, [(XTn, "XTn")], 128, mask_off=120 - s * T)
                    state[gidx] = (1, [(ckvb[:, :], "ckvb")], 128, True)
                    return
                n = npg_cnt[0]
                npg_cnt[0] += 1
                b = n % 4
                ensure_issued(n + 3)
                xts, xtoks = [], []
                XTp = XT[par]
                for t in range(TCH):
                    it = nxt("ptr", 2)
                    tr.op("pe", lambda h, t=t, it=it: h.transpose(ptr[it][:, 0:KPG], Xb[b][0:KPG, t, 0:128], ident[0:KPG, 0:KPG]), r=[("Xb", b), "ident"], w=[("ptr", it)])
                    tr.op("pe", lambda h, t=t, it=it: h.transpose(ptr[it][:, 128:128 + KPG], Xb[b][0:KPG, t, 128:256], ident[0:KPG, 0:KPG]), r=[("Xb", b), "ident"], w=[("ptr", it)])
                    tr.op("pe", lambda h, t=t, it=it: h.transpose(ptr[it][0:QKR, 256:256 + KPG], Xb[b][0:KPG, t, 256:LAT], ident[0:KPG, 0:KPG]), r=[("Xb", b), "ident"], w=[("ptr", it)])
                    tr.op("dve", lambda h, t=t, it=it: h.tensor_copy(out=XTp[:, t, 0:2, 0:KPG], in_=ptr[it][:, 0:256].rearrange("p (j t) -> p j t", t=128)[:, :, 0:KPG]), r=[("ptr", it)], w=[("XT", par, t)])
                    tr.op("act", lambda h, t=t, it=it: h.copy(out=XTp[0:QKR, t, 2, 0:KPG], in_=ptr[it][0:QKR, 256:256 + KPG]), r=[("ptr", it)], w=[("XT", par, t)])
                    xts.append((XTp[:, t, :, :], ("XT", par, t)))
                    xtoks.append((Xb[b][0:KPG, t, 0:KVL], ("Xb", b)))
                flash_front(par, s, xts, KPG)
                state[gidx] = (TCH, xtoks, KPG, False)

            def back(gidx):
                kind, s, c = groups[gidx]
                g, xtoks, KP, first = state.pop(gidx)
                flash_back(gidx % 2, g, xtoks, KP, first)
                if kind == "page" and c == NCH - 1:
                    finalize(s)

            front(0)
            for gidx in range(len(groups)):
                if gidx + 1 < len(groups):
                    front(gidx + 1)
                back(gidx)

        for b in range(PB):
            compute_ada(cp_d[b:b + 1, :], 1, 128)
            ck("ada")
            tr.op("pool", lambda h: h.memset(uT[:], 0.0), r=["uT"], w=["uT"])
            for ti in range(NTS):
                r0 = b * SEQ + ti * 128
                tile(xp[r0:r0 + 128, :], yp[r0:r0 + 128, :], latp[r0:r0 + 128, :], krp[r0:r0 + 128, :],
                     cosp[:, ti, :], sinp[:, ti, :], ["rope", "rope2"], True, ti=ti,
                     conv_dst=convp[b * 2:(b + 1) * 2, :], first=(ti == 0), last=(ti == NTS - 1))
        compute_ada(cs_d, SB, T)
        tr.op("pool", lambda h: h.memset(uT[:], 0.0), r=["uT"], w=["uT", "uTm"])
        for c in range(4):
            for t2 in range(2):
                dma(uT[:, c, :, t2], sconv[:, c * 128:(c + 1) * 128].rearrange("(s t) p -> p s t", t=2)[:, :, t2], r=["uTm"], w=["uT"], nonc=True)
        tile(xs, ys, lats, krs, coss[:], sins[:], ["ropes", "ropes2"], False, conv_dst=convs)

    except _Stop:
        pass
    sem_es = ExitStack()
    sems = {e: sem_es.enter_context(nc.semaphore("s_" + e)) for e in ("pe", "act", "dve", "pool")}
    ring = {q: [sem_es.enter_context(nc.semaphore(f"ring_{q}{i}")) for i in range(NRING)] for q in Tracker.DMAQ}
    tr.emit(nc, sems, ring, None)
    sem_es.close()
    es.close()
    return nc


def rope_tables(pos):
    inv = (10000.0 ** (-(np.arange(0, QKR, 2, dtype=np.float32)) / np.float32(QKR))).astype(np.float32)
    ang = (pos.astype(np.float32)[:, None] * inv[None, :]).astype(np.float32)
    return np.cos(ang).astype(np.float32), np.sin(ang).astype(np.float32)


_CACHE = {}


def run(inputs, SEQ, NPG, NPOOL, PAST):
    key = (SEQ, NPG, NPOOL)
    if key not in _CACHE:
        _CACHE[key] = build(SEQ, NPG, NPOOL)
    nc = _CACHE[key]
    f = lambda a: np.ascontiguousarray(np.asarray(a))
    pool = np.concatenate([np.asarray(inputs["cache_kv_latent"])[0], np.asarray(inputs["cache_k_rope"])[0]], axis=-1)
    pool = np.ascontiguousarray(pool, dtype=np.float32).reshape(NPOOL * NCH, TCH * LAT)
    cosp, sinp = rope_tables(np.arange(SEQ))
    cs_, ss_ = rope_tables(PAST + np.arange(T))
    coss = np.tile(cs_, (SB, 1)); sins = np.tile(ss_, (SB, 1))
    ident = np.eye(128, dtype=np.float32).astype(ml_dtypes.bfloat16)
    maskc = np.where(np.arange(128)[None, :] <= np.arange(128)[:, None], 0.0, NEG).astype(np.float32).astype(ml_dtypes.bfloat16)
    mbig = np.full((64, 248), NEG, np.float32)
    for r in range(64):
        t = r % T
        mbig[r, 120:120 + t + 1] = 0.0
    mbig = mbig.astype(ml_dtypes.bfloat16)
    common = {k: f(inputs[k])[0] for k in WSPEC if k != "w_ukv"}
    common["w_ukv"] = f(inputs["w_ukv"])[0].reshape(KVL, NH * 128)
    for k in ("b_ada", "q_norm_g", "kv_norm_g", "ln1_g", "ln1_b", "ln2_g", "ln2_b"):
        common[k] = f(inputs[k]).reshape(1, -1)
    common["conv_w"] = f(inputs["conv_w"])[0]
    common.update(cosp=cosp, sinp=sinp, coss=coss, sins=sins, ident=ident, maskc=maskc, mbig=mbig, pool=pool)
    xp = f(inputs["x_prompt"]); xs = f(inputs["x_sample"]); ptab = f(inputs["page_table"]).astype(np.int32)
    sc = f(inputs["state_conv"])[0]; cp = f(inputs["c_prompt"]); cs = f(inputs["c_sample"])
    in_maps = []
    for c in range(8):
        m = dict(common)
        m["xp"] = xp[c * PB:(c + 1) * PB].reshape(PB * SEQ, D)
        m["xs"] = xs[c * SB:(c + 1) * SB].reshape(SB * T, D)
        m["pt"] = np.ascontiguousarray(ptab[c * SB:(c + 1) * SB])
        m["sconv"] = np.ascontiguousarray(sc[c * SB:(c + 1) * SB].reshape(SB * 2, CD))
        m["cp"] = np.ascontiguousarray(cp[c * PB:(c + 1) * PB]); m["cs"] = np.ascontiguousarray(cs[c * SB:(c + 1) * SB])
        in_maps.append(m)
    import os as _os
    if _os.environ.get("KTRACE"):
        res = run_bass_kernel_spmd(nc, in_maps, core_ids=list(range(8)), trace=True)
        print("KTRACE exec_time_ns", res.exec_time_ns, flush=True)
    else:
        res = run_bass_kernel_spmd(nc, in_maps, core_ids=list(range(8)))
    R = res.results
    cat = lambda k: np.concatenate([np.asarray(R[c][k]) for c in range(8)], axis=0)
    B = 8 * PB
    y_p = cat("yp").reshape(B, SEQ, D); y_s = cat("ys").reshape(8 * SB, T, D)
    lat_p = cat("latp").reshape(1, B, SEQ, KVL); kr_p = cat("krp").reshape(1, B, SEQ, QKR); cv_p = cat("convp").reshape(1, B, 2, CD)
    lat_s = cat("lats").reshape(1, 8 * SB, T, KVL); kr_s = cat("krs").reshape(1, 8 * SB, T, QKR); cv_s = cat("convs").reshape(1, 8 * SB, 2, CD)
    return (y_p, y_s, lat_p, kr_p, cv_p, lat_s, kr_s, cv_s)


def kernel(**inputs):
    SEQ = inputs["x_prompt"].shape[1]
    NPG = inputs["page_table"].shape[1]
    NPOOL = inputs["cache_kv_latent"].shape[1]
    return run(inputs, SEQ, NPG, NPOOL, NPG * PAGE)
```

```python
import math
from contextlib import ExitStack
import numpy as np
import ml_dtypes
import concourse.bass as bass
import concourse.mybir as mybir
from concourse.bass_utils import run_bass_kernel_spmd

F32 = mybir.dt.float32
BF16 = mybir.dt.bfloat16
I32 = mybir.dt.int32
AF = mybir.ActivationFunctionType
ALU = mybir.AluOpType
AX = mybir.AxisListType

D = 1024
NH = 8
QKN, QKR, QKH, VH = 64, 32, 96, 64
QL, KVL = 384, 256
CD = 512
DFF = 2816
NIN = 4256
DEPTH = 1
ALPHA = (2.0 * DEPTH) ** 0.25
LN_EPS = 1e-5
RMS_EPS = 1e-6
SCALE = QKH ** -0.5
PB, SB, T = 2, 16, 8
PAGE = 128
LAT = KVL + QKR
NEG = -30000.0
TCH = 4
NCH = PAGE // TCH
SAFE_SAME = True
NRING = 16


class Tracker:
    ENG = ("pe", "act", "dve", "pool", "sp")

    def __init__(self):
        self.prog = {e: [] for e in self.ENG}
        self.last_w = {}
        self.readers = {}

    def op(self, eng, fn, r=(), w=(), dma=False):
        idx = len(self.prog[eng])
        deps = set()
        for k in r:
            deps.update(self.last_w.get(k, ()))
        for k in w:
            deps.update(self.last_w.get(k, ()))
            rd = self.readers.get(k)
            if rd:
                for e2, v in rd.items():
                    if isinstance(v, list):
                        deps.update(v)
                    else:
                        deps.add((e2, v))
        deps.discard((eng, idx))
        self.prog[eng].append(dict(fn=fn, deps=deps, dma=dma))
        for k in r:
            rd = self.readers.setdefault(k, {})
            if dma:
                rd.setdefault("dma", []).append((eng, idx))
            else:
                rd[eng] = idx
        for k in w:
            prev = self.last_w.get(k, [])
            if dma and prev and all(self.prog[e2][i2]["dma"] for (e2, i2) in prev) and not self.readers.get(k):
                self.last_w[k] = prev + [(eng, idx)]
            else:
                self.last_w[k] = [(eng, idx)]
            self.readers[k] = {}
        return (eng, idx)

    DMAQ = ("sp", "pool")

    def emit(self, nc, sems, ring, final_sem):
        prog = self.prog
        dma_no = {}
        ndma = {}
        for q in self.DMAQ:
            n = 0
            for i, ins in enumerate(prog[q]):
                if ins["dma"]:
                    dma_no[(q, i)] = n
                    n += 1
            ndma[q] = n

        def is_dma(e2, i2):
            return prog[e2][i2]["dma"]

        need = set()
        for e in self.ENG:
            for i, ins in enumerate(prog[e]):
                for (e2, i2) in ins["deps"]:
                    if is_dma(e2, i2):
                        continue
                    if e2 == e and not (SAFE_SAME and e in ("act", "dve", "pool")):
                        continue
                    need.add((e2, i2))
        val = {}
        for e in self.ENG:
            c = 0
            for i in range(len(prog[e])):
                if (e, i) in need:
                    c += 1
                    val[(e, i)] = c

        def run(e, h):
            waited = {}
            for i, ins in enumerate(prog[e]):
                waits = {}
                for (e2, i2) in ins["deps"]:
                    if is_dma(e2, i2):
                        d = dma_no[(e2, i2)]
                        key = ("ring", e2, d % NRING)
                        v = 16 * (d // NRING + 1)
                    else:
                        if (e2, i2) not in val:
                            continue
                        key = e2
                        v = val[(e2, i2)]
                    if waited.get(key, 0) >= v:
                        continue
                    waits[key] = max(waits.get(key, 0), v)
                if ins["dma"]:
                    d = dma_no[(e, i)]
                    if d >= NRING:
                        key = ("ring", e, d % NRING)
                        v = 16 * (d // NRING)
                        if waited.get(key, 0) < v:
                            waits[key] = max(waits.get(key, 0), v)
                for key, v in waits.items():
                    sem = ring[key[1]][key[2]] if isinstance(key, tuple) else sems[key]
                    h.wait_ge(sem, v)
                    waited[key] = v
                bi = ins["fn"](h)
                if ins["dma"]:
                    bi.then_inc(ring[e][dma_no[(e, i)] % NRING], 16)
                elif (e, i) in val:
                    bi.then_inc(sems[e], 1)
            if e in self.DMAQ:
                for sl in range(min(NRING, ndma[e])):
                    cntd = (ndma[e] - 1 - sl) // NRING + 1
                    h.wait_ge(ring[e][sl], 16 * cntd)

        with nc.Block() as block:
            @block.sync
            def _(h):
                run("sp", h)

            @block.tensor
            def _(h):
                run("pe", h)

            @block.scalar
            def _(h):
                run("act", h)

            @block.vector
            def _(h):
                run("dve", h)

            @block.gpsimd
            def _(h):
                run("pool", h)


WSPEC = {
    "w_ada": (D, 6 * D), "w_in": (D, NIN), "w_uq": (QL, NH * QKH), "w_ukv": (KVL, NH * 128),
    "w_oa": (NH * VH, D), "w_oc": (CD, D), "w_o": (D, D), "w_ff1": (D, DFF), "w_ff3": (D, DFF), "w_ff2": (DFF, D),
}


class _Stop(Exception):
    pass


def build(SEQ, NPG, NPOOL):
    import os
    dbg_stop = int(os.environ.get("KDBG", "0"))
    ckc = [0]

    def ck(name):
        ckc[0] += 1
        if (dbg_stop and ckc[0] == dbg_stop) or (os.environ.get("KDBG_NAME") == name):
            print("KDBG stop at checkpoint", ckc[0], name, flush=True)
            raise _Stop()
    NTS = SEQ // 128
    nc = bass.Bass("TRN2", target_bir_lowering=False)
    tr = Tracker()
    es = ExitStack()

    def din(name, shape, dt=F32):
        return nc.dram_tensor(name, list(shape), dt, kind="ExternalInput").ap()

    def dout(name, shape, dt=F32):
        return nc.dram_tensor(name, list(shape), dt, kind="ExternalOutput").ap()

    xp = din("xp", [PB * SEQ, D]); xs = din("xs", [128, D])
    pool_d = din("pool", [NPOOL * NCH, TCH * LAT]); pt_d = din("pt", [SB, NPG], I32)
    sconv = din("sconv", [SB * 2, CD]); cp_d = din("cp", [PB, D]); cs_d = din("cs", [SB, D])
    wd = {k: din(k, [v[0], v[1]]) for k, v in WSPEC.items()}
    b_ada = din("b_ada", [1, 6 * D]); qg = din("q_norm_g", [1, QL]); kvg = din("kv_norm_g", [1, KVL])
    convw_d = din("conv_w", [3, CD])
    lnd = [din(n, [1, D]) for n in ("ln1_g", "ln1_b", "ln2_g", "ln2_b")]
    cosp_d = din("cosp", [SEQ, 16]); sinp_d = din("sinp", [SEQ, 16])
    coss_d = din("coss", [128, 16]); sins_d = din("sins", [128, 16])
    ident_d = din("ident", [128, 128], BF16); maskc_d = din("maskc", [128, 128], BF16)
    mbig_d = din("mbig", [64, 248], BF16)

    yp = dout("yp", [PB * SEQ, D]); ys = dout("ys", [128, D])
    latp = dout("latp", [PB * SEQ, KVL]); krp = dout("krp", [PB * SEQ, QKR]); convp = dout("convp", [PB * 2, CD])
    lats = dout("lats", [128, KVL]); krs = dout("krs", [128, QKR]); convs = dout("convs", [SB * 2, CD])

    wb = {k: nc.dram_tensor(k + "_bf", [128, v[0] // 128, v[1]], BF16, kind="Internal").ap() for k, v in WSPEC.items()}

    def sb(name, shape, dt=F32):
        return es.enter_context(nc.sbuf_tensor("sb_" + name, list(shape), dt))

    def ps(name, shape, dt=F32):
        return es.enter_context(nc.psum_tensor("ps_" + name, list(shape), dt))

    ident = sb("ident", [128, 128], BF16); maskc = sb("maskc", [128, 128], BF16); mbig = sb("mbig", [64, 248], BF16)
    lnbc = sb("lnbc", [128, 2, D]); qgbc = sb("qgbc", [128, QL]); kvgbc = sb("kvgbc", [128, KVL])
    convw = sb("convw", [128, 4, 3])
    cosp = sb("cosp", [128, NTS, 16]); sinp = sb("sinp", [128, NTS, 16])
    coss = sb("coss", [128, 16]); sins = sb("sins", [128, 16])
    ada = sb("ada", [128, 6 * D])
    cT = sb("cT", [128, 8, SB]); cTs = sb("cTs", [128, 8, SB]); cTexp = sb("cTexp", [128, 8, 128], BF16)
    xt = [sb(f"xt{i}", [128, D]) for i in range(1)]
    hbf = sb("hbf", [128, D], BF16); hT = sb("hT", [128, 8, 128], BF16)
    tmp = sb("tmp", [128, D]); tmp2 = sb("tmp2", [128, D])
    z = sb("z", [128, 672])
    st8 = sb("st8", [128, 16]); sth = [sb(f"sth{i}", [128, 8]) for i in range(2)]
    qn = sb("qn", [128, QL], BF16); qnT = sb("qnT", [128, 3, 128], BF16)
    ckv = sb("ckv", [128, KVL]); ckvb = sb("ckvb", [128, KVL], BF16); ckvT = sb("ckvT", [128, 2, 128], BF16)
    krt = sb("krt", [128, QKR]); krb = sb("krb", [128, QKR], BF16)
    q = sb("q", [128, NH, QKR]); qb = sb("qb", [128, NH, QKH], BF16); qT = sb("qT", [128, NH, 128], BF16)
    kk = sb("kk", [128, NH, QKH], BF16)
    Kt = sb("Kt", [128, NH, SEQ], BF16); Vt = sb("Vt", [128, NTS, NH * VH], BF16)
    P = [sb(f"P{i}", [128, SEQ], BF16) for i in range(2)]; PT = sb("PT", [128, 8, 128], BF16)
    oT = sb("oT", [128, 4, 128], BF16)
    uT = sb("uT", [128, 4, SB, 2 + T]); cvT = sb("cvT", [128, 4, 128]); cbT = sb("cbT", [128, 4, 128], BF16)
    cgs = sb("cgs", [128, 4, 128])
    sga = sb("sga", [128, 512]); sgc = sb("sgc", [128, 512])
    mT = sb("mT", [128, 8, 128], BF16)
    ab = sb("ab", [128, 512], BF16); aT = sb("aT", [128, 22, 128], BF16)
    slabs = [sb(f"slab{i}", [128, 8 * 512], BF16) for i in range(3)]
    ptT = sb("ptT", [128, SB], I32)
    idxr = [sb(f"idxr{i}", [128, 1], I32) for i in range(4)]
    wukT = sb("wukT", [128, 4, KVL], BF16)
    wuvb = sb("wuvb", [128, 2, NH * VH], BF16)
    qnpT = sb("qnpT", [128, 4, 128], BF16)
    QaT = sb("QaT", [128, 2, SB, 64], BF16)
    QrT = sb("QrT", [32, SB, 64], BF16)
    Xb = [sb(f"Xb{i}", [128, TCH, LAT], BF16) for i in range(4)]
    XT = [sb(f"XT{i}", [128, 3, TCH, 128], BF16) for i in range(2)]
    Ps = [sb(f"Ps{i}", [64, 512], BF16) for i in range(2)]; PTs = [sb(f"PTs{i}", [128, 4, 64], BF16) for i in range(2)]
    accs = sb("accs", [64, KVL]); sm = sb("sm", [64, 16]); olat = sb("olat", [64, KVL], BF16); olT = sb("olT", [128, 2, 64], BF16)

    pmm = [ps(f"pmm{i}", [128, 512]) for i in range(2)]
    ptr = [ps(f"ptr{i}", [128, 1024], BF16) for i in range(2)]
    psc = [ps(f"psc{i}", [128, 512]) for i in range(4)]

    cnt = {"slab": 0, "pmm": 0, "ptr": 0, "stg": 0, "x": 0, "pg": 0}
    stg = [tmp[:, :], tmp2[:, :]]
    stg_key = ["tmp", "tmp2"]
    stgb = [hbf[:, :], mT[:].rearrange("p a b -> p (a b)")]
    stgb_key = ["hbf", "mT"]
    ob = hbf[:, 0:NH * VH]

    def nxt(kind, n):
        i = cnt[kind] % n
        cnt[kind] += 1
        return i

    def dma(out, in_, r=(), w=(), nonc=False):
        if nonc:
            tr.op("sp", lambda h: h.dma_start(out=out, in_=in_, allow_slow_non_contiguous=True), r=r, w=w, dma=True)
        else:
            tr.op("sp", lambda h: h.dma_start(out=out, in_=in_), r=r, w=w, dma=True)

    try:
        dma(ident[:], ident_d, w=["ident"]); dma(maskc[:], maskc_d, w=["maskc"]); dma(mbig[:], mbig_d, w=["mbig"])
        dma(qgbc[:], qg.partition_broadcast(128), w=["qgbc"]); dma(kvgbc[:], kvg.partition_broadcast(128), w=["kvgbc"])
        for k in range(3):
            dma(convw[:, :, k], convw_d[k, :].rearrange("(c p) -> p c", p=128), w=["convw"], nonc=True)
        dma(cosp[:], cosp_d.rearrange("(n p) f -> p n f", p=128), w=["rope"]); dma(sinp[:], sinp_d.rearrange("(n p) f -> p n f", p=128), w=["rope2"])
        dma(coss[:], coss_d, w=["ropes"]); dma(sins[:], sins_d, w=["ropes2"])

        ck("constants")
        def convert(name):
            K, N = WSPEC[name]
            for kc in range(K // 128):
                for c0 in range(0, N, 1024):
                    c1 = min(N, c0 + 1024)
                    i = nxt("stg", 2)
                    dma(stg[i][:, 0:c1 - c0], wd[name][kc * 128:(kc + 1) * 128, c0:c1], w=[stg_key[i]])
                    if i == 1:
                        tr.op("dve", lambda h, i=i, n=c1 - c0: h.tensor_copy(out=stgb[i][:, 0:n], in_=stg[i][:, 0:n]), r=[stg_key[i]], w=[stgb_key[i]])
                    else:
                        tr.op("act", lambda h, i=i, n=c1 - c0: h.copy(out=stgb[i][:, 0:n], in_=stg[i][:, 0:n]), r=[stg_key[i]], w=[stgb_key[i]])
                    dma(wb[name][:, kc, c0:c1], stgb[i][:, 0:c1 - c0], r=[stgb_key[i]], w=[("wb", name)])

        for name in WSPEC:
            convert(name)

        ck("conversions")
        def slab(name, c0, c1, k0=0, k1=None):
            K, N = WSPEC[name]
            if k1 is None:
                k1 = K // 128
            KC = k1 - k0
            n = c1 - c0
            assert KC * n <= 8 * 512
            i = nxt("slab", 3); buf = slabs[i]; key = ("slab", i)
            view = buf[:, 0:KC * n].rearrange("p (k n) -> p k n", n=n)
            dma(view, wb[name][:, k0:k1, c0:c1], r=[("wb", name)], w=[key])
            return view, key

        def mm(out, lhsT, rhs, start, stop, r, w):
            tr.op("pe", lambda h: h.matmul(out, lhsT, rhs, start=start, stop=stop), r=r, w=w)

        def transposes(src_ap_fn, nchunk, dst, dst_key, src_key, rows=128, cols=128):
            i = nxt("ptr", 2)
            for j in range(nchunk):
                tr.op("pe", lambda h, j=j, i=i: h.transpose(ptr[i][0:cols, j * 128:j * 128 + rows], src_ap_fn(j), ident[0:rows, 0:rows]),
                      r=[src_key, "ident"], w=[("ptr", i)])
            tr.op("dve", lambda h, i=i: h.tensor_copy(out=dst[0:cols, 0:nchunk, 0:rows],
                                                     in_=ptr[i][0:cols, 0:nchunk * 128].rearrange("p (j t) -> p j t", t=128)[:, :, 0:rows]),
                  r=[("ptr", i)], w=[dst_key])

        wukc = hbf[:, :].rearrange("p (k n) -> p k n", k=2)
        for kc in range(2):
            src = wb["w_ukv"][:, kc, :].rearrange("p (h e) -> p h e", e=128)
            dma(wuvb[:, kc, :].rearrange("p (h e) -> p h e", e=VH), src[:, :, QKN:128], r=[("wb", "w_ukv")], w=["wuvb"])
            dma(wukc[:, kc, :].rearrange("p (h e) -> p h e", e=QKN), src[:, :, 0:QKN], r=[("wb", "w_ukv")], w=["hbf"])
        for kc in range(2):
            i = nxt("ptr", 2)
            for pr in range(4):
                tr.op("pe", lambda h, kc=kc, pr=pr, i=i: h.transpose(ptr[i][:, pr * 128:(pr + 1) * 128], wukc[:, kc, pr * 128:(pr + 1) * 128], ident[:]),
                      r=["hbf", "ident"], w=[("ptr", i)])
            tr.op("dve", lambda h, kc=kc, i=i: h.tensor_copy(out=wukT[:, :, kc * 128:(kc + 1) * 128],
                                                            in_=ptr[i][:, 0:512].rearrange("p (j t) -> p j t", t=128)),
                  r=[("ptr", i)], w=["wukT"])

        ck("wuk setup")
        def compute_ada(c_src, nseq, rep):
            for kc in range(8):
                dma(cT[:, kc, 0:nseq], c_src[:, kc * 128:(kc + 1) * 128].rearrange("s p -> p s"), w=[("cT", kc)], nonc=True)
            tr.op("act", lambda h: h.activation(out=cTs[:, :, 0:nseq], in_=cT[:, :, 0:nseq], func=AF.Silu), r=[("cT", k) for k in range(8)], w=["cTs"])
            for kc in range(8):
                tr.op("dve", lambda h, kc=kc: h.tensor_copy(out=cTexp[:, kc, :].rearrange("p (s r) -> p s r", r=rep),
                                                             in_=cTs[:, kc, 0:nseq].unsqueeze(2).to_broadcast([128, nseq, rep])),
                      r=["cTs"], w=["cTexp"])
            dma(ada[:], b_ada.partition_broadcast(128), w=["ada"])
            for c in range(12):
                sl, key = slab("w_ada", c * 512, (c + 1) * 512)
                i = nxt("pmm", 2)
                for kc in range(8):
                    mm(pmm[i][:, :], cTexp[:, kc, :], sl[:, kc, :], kc == 0, kc == 7, r=["cTexp", key], w=[("pmm", i)])
                tr.op("dve", lambda h, c=c, i=i: h.tensor_tensor(out=ada[:, c * 512:(c + 1) * 512], in0=pmm[i][:, :], in1=ada[:, c * 512:(c + 1) * 512], op=ALU.add),
                      r=[("pmm", i), "ada"], w=["ada"])
            for off in (1 * D, 4 * D):
                tr.op("dve", lambda h, off=off: h.tensor_scalar_add(out=ada[:, off:off + D], in0=ada[:, off:off + D], scalar1=1.0), r=["ada"], w=["ada"])

        def layer_norm(src, gi, dst, skey, dkey):
            dma(lnbc[:, 0, :], lnd[gi].partition_broadcast(128), w=[("lnbc", 0)])
            dma(lnbc[:, 1, :], lnd[gi + 1].partition_broadcast(128), w=[("lnbc", 1)])
            gi = 0
            tr.op("dve", lambda h: h.reduce_sum(out=st8[:, 0:1], in_=src, axis=AX.X), r=[skey], w=["st8"])
            tr.op("act", lambda h: h.activation(out=tmp2[:], in_=src, func=AF.Square, accum_out=st8[:, 1:2]), r=[skey, "st8"], w=["tmp2", "st8"])
            tr.op("dve", lambda h: h.tensor_scalar(out=st8[:, 2:4], in0=st8[:, 0:2], scalar1=1.0 / D, scalar2=None, op0=ALU.mult), r=["st8"], w=["st8"])
            tr.op("dve", lambda h: h.tensor_tensor(out=st8[:, 4:5], in0=st8[:, 2:3], in1=st8[:, 2:3], op=ALU.mult), r=["st8"], w=["st8"])
            tr.op("dve", lambda h: h.tensor_tensor(out=st8[:, 5:6], in0=st8[:, 3:4], in1=st8[:, 4:5], op=ALU.subtract), r=["st8"], w=["st8"])
            tr.op("dve", lambda h: h.tensor_scalar_add(out=st8[:, 5:6], in0=st8[:, 5:6], scalar1=LN_EPS), r=["st8"], w=["st8"])
            tr.op("act", lambda h: h.activation(out=st8[:, 7:8], in_=st8[:, 5:6], func=AF.Sqrt), r=["st8"], w=["st8"])
            tr.op("dve", lambda h: h.reciprocal(out=st8[:, 6:7], in_=st8[:, 7:8]), r=["st8"], w=["st8"])
            tr.op("dve", lambda h: h.tensor_scalar(out=dst, in0=src, scalar1=st8[:, 2:3], scalar2=st8[:, 6:7], op0=ALU.subtract, op1=ALU.mult), r=[skey, "st8"], w=[dkey])
            tr.op("dve", lambda h: h.tensor_tensor(out=dst, in0=dst, in1=lnbc[:, gi, :], op=ALU.mult), r=[dkey, ("lnbc", gi)], w=[dkey])
            tr.op("dve", lambda h: h.tensor_tensor(out=dst, in0=dst, in1=lnbc[:, gi + 1, :], op=ALU.add), r=[dkey, ("lnbc", gi + 1)], w=[dkey])

        def rms_norm(src, n, gbc, gkey, dst, skey, dkey, dst_bf=None, bkey=None):
            tr.op("act", lambda h: h.activation(out=tmp2[:, 0:n], in_=src, func=AF.Square, accum_out=st8[:, 8:9]), r=[skey, "st8"], w=["tmp2", "st8"])
            tr.op("dve", lambda h: h.tensor_scalar(out=st8[:, 9:10], in0=st8[:, 8:9], scalar1=1.0 / n, scalar2=RMS_EPS, op0=ALU.mult, op1=ALU.add), r=["st8"], w=["st8"])
            tr.op("act", lambda h: h.activation(out=st8[:, 9:10], in_=st8[:, 9:10], func=AF.Sqrt), r=["st8"], w=["st8"])
            tr.op("dve", lambda h: h.reciprocal(out=st8[:, 10:11], in_=st8[:, 9:10]), r=["st8"], w=["st8"])
            tr.op("dve", lambda h: h.tensor_scalar(out=dst, in0=src, scalar1=st8[:, 10:11], scalar2=None, op0=ALU.mult), r=[skey, "st8"], w=[dkey])
            tr.op("dve", lambda h: h.tensor_tensor(out=dst, in0=dst, in1=gbc, op=ALU.mult), r=[dkey, gkey], w=[dkey])
            if dst_bf is not None:
                tr.op("act", lambda h: h.copy(out=dst_bf, in_=dst), r=[dkey], w=[bkey])

        def rope(src, dst, cos, sin, nh, skey, dkey, ckeys):
            cb = cos.unsqueeze(1).to_broadcast([128, nh, 16]) if nh > 1 else cos
            sbc = sin.unsqueeze(1).to_broadcast([128, nh, 16]) if nh > 1 else sin
            if nh > 1:
                x1_, x2_ = src[:, :, 0:16], src[:, :, 16:32]
                d1, d2 = dst[:, :, 0:16], dst[:, :, 16:32]
                t1 = tmp2[:, 0:nh * 16].rearrange("p (h f) -> p h f", f=16)
                t2 = tmp2[:, 256:256 + nh * 16].rearrange("p (h f) -> p h f", f=16)
            else:
                x1_, x2_ = src[:, 0:16], src[:, 16:32]
                d1, d2 = dst[:, 0:16], dst[:, 16:32]
                t1 = tmp2[:, 0:16]
                t2 = tmp2[:, 256:272]
            rk = [skey] + list(ckeys)
            tr.op("dve", lambda h: h.tensor_tensor(out=t1, in0=x1_, in1=cb, op=ALU.mult), r=rk, w=["tmp2"])
            tr.op("dve", lambda h: h.tensor_tensor(out=t2, in0=x2_, in1=sbc, op=ALU.mult), r=rk, w=["tmp2"])
            tr.op("dve", lambda h: h.tensor_tensor(out=d1, in0=t1, in1=t2, op=ALU.subtract), r=["tmp2"], w=[dkey])
            tr.op("dve", lambda h: h.tensor_tensor(out=t1, in0=x1_, in1=sbc, op=ALU.mult), r=rk, w=["tmp2"])
            tr.op("dve", lambda h: h.tensor_tensor(out=t2, in0=x2_, in1=cb, op=ALU.mult), r=rk, w=["tmp2"])
            tr.op("dve", lambda h: h.tensor_tensor(out=d2, in0=t1, in1=t2, op=ALU.add), r=["tmp2"], w=[dkey])

        def tile(x_src, y_dst, lat_dst, kr_dst, cos, sin, ckeys, prompt, ti=0, conv_dst=None, first=False, last=False):
            xi = 0
            X = xt[xi]
            xk = ("x", xi)
            dma(X[:], x_src, w=[xk])
            tr.op("dve", lambda h: h.tensor_tensor(out=tmp[:], in0=X[:], in1=ada[:, D:2 * D], op=ALU.mult), r=[xk, "ada"], w=["tmp"])
            tr.op("dve", lambda h: h.tensor_tensor(out=hbf[:], in0=tmp[:], in1=ada[:, 0:D], op=ALU.add), r=["tmp", "ada"], w=["hbf"])
            transposes(lambda j: hbf[:, j * 128:(j + 1) * 128], 8, hT, "hT", "hbf")
            ck("tile: h transposes")
            for (c0, c1) in ((0, 512), (512, 672)):
                sl, key = slab("w_in", c0, c1)
                i = nxt("pmm", 2)
                for kc in range(8):
                    mm(pmm[i][:, 0:c1 - c0], hT[:, kc, :], sl[:, kc, :], kc == 0, kc == 7, r=["hT", key], w=[("pmm", i)])
                tr.op("act", lambda h, i=i, c0=c0, c1=c1: h.copy(out=z[:, c0:c1], in_=pmm[i][:, 0:c1 - c0]), r=[("pmm", i)], w=["z"])
            ck("tile: z")
            rms_norm(z[:, 0:QL], QL, qgbc[:], "qgbc", tmp[:, 0:QL], "z", "tmp", qn[:], "qn")
            ck("q: rms")
            transposes(lambda j: qn[:, j * 128:(j + 1) * 128], 3, qnT, "qnT", "qn")
            ck("q: qnT")
            for (c0, c1) in ((0, 384), (384, 768)):
                sl, key = slab("w_uq", c0, c1)
                i = nxt("pmm", 2)
                for kc in range(3):
                    mm(pmm[i][:, 0:384], qnT[:, kc, :], sl[:, kc, :], kc == 0, kc == 2, r=["qnT", key], w=[("pmm", i)])
                ck("q: mm%d" % c0)
                pq = pmm[i][:, 0:384].rearrange("p (h e) -> p h e", e=QKH)
                tr.op("act", lambda h, pq=pq, c0=c0: h.copy(out=q[:, c0 // QKH:c0 // QKH + 4, :], in_=pq[:, :, QKN:QKH]), r=[("pmm", i)], w=["q"])
                ck("q: evA%d" % c0)
                tr.op("act", lambda h, pq=pq, c0=c0: h.copy(out=qb[:, c0 // QKH:c0 // QKH + 4, 0:QKN], in_=pq[:, :, 0:QKN]), r=[("pmm", i)], w=["qb"])
                ck("q: evB%d" % c0)
            ck("q: wuq")
            rope(q[:, :, :], qb[:, :, QKN:QKH], cos, sin, NH, "q", "qb", ckeys)
            ck("tile: q path")
            rms_norm(z[:, QL:QL + KVL], KVL, kvgbc[:], "kvgbc", ckv[:], "z", "ckv", ckvb[:], "ckvb")
            dma(lat_dst, ckv[:], r=["ckv"])
            rope(z[:, QL + KVL:672], krt[:], cos, sin, 1, "z", "krt", ckeys)
            dma(kr_dst, krt[:], r=["krt"])
            transposes(lambda j: ckvb[:, j * 128:(j + 1) * 128], 2, ckvT, "ckvT", "ckvb")
            ck("tile: kv path")
            slc_, keyc = slab("w_in", 672 + CD, 672 + 2 * CD)
            slv_, keyv = slab("w_in", 672 + 2 * CD, 672 + 3 * CD)
            for j in range(4):
                ic = nxt("pmm", 2)
                for kc in range(8):
                    mm(pmm[ic][:, 0:128], slc_[:, kc, j * 128:(j + 1) * 128], hT[:, kc, :], kc == 0, kc == 7, r=["hT", keyc], w=[("pmm", ic)])
                tr.op("act", lambda h, j=j, ic=ic: h.copy(out=cgs[:, j, :], in_=pmm[ic][:, 0:128]), r=[("pmm", ic)], w=["cgs"])
                iv = nxt("pmm", 2)
                for kc in range(8):
                    mm(pmm[iv][:, 0:128], slv_[:, kc, j * 128:(j + 1) * 128], hT[:, kc, :], kc == 0, kc == 7, r=["hT", keyv], w=[("pmm", iv)])
                if prompt:
                    uflat = uT[:].rearrange("p c s t -> p c (s t)")
                    tr.op("dve", lambda h, j=j, iv=iv: h.tensor_tensor(out=uflat[:, j, 2:130], in0=cgs[:, j, :], in1=pmm[iv][:, 0:128], op=ALU.mult), r=["cgs", ("pmm", iv)], w=["uT"])
                else:
                    tr.op("dve", lambda h, j=j, iv=iv: h.tensor_tensor(out=uT[:, j, :, 2:2 + T], in0=cgs[:, j, :].rearrange("p (s t) -> p s t", t=T),
                                                                      in1=pmm[iv][:, 0:128].rearrange("p (s t) -> p s t", t=T), op=ALU.mult), r=["cgs", ("pmm", iv)], w=["uT"])
            sl, key = slab("w_in", 672, 672 + CD)
            for j in range(4):
                ib = nxt("pmm", 2)
                for kc in range(8):
                    mm(pmm[ib][:, 0:128], sl[:, kc, j * 128:(j + 1) * 128], hT[:, kc, :], kc == 0, kc == 7, r=["hT", key], w=[("pmm", ib)])
                if prompt:
                    uflat = uT[:].rearrange("p c s t -> p c (s t)")
                    u0, u1, u2 = uflat[:, j, 0:128], uflat[:, j, 1:129], uflat[:, j, 2:130]
                    cv = cvT[:, j, :]
                    bsrc = pmm[ib][:, 0:128]
                    cbo = cbT[:, j, :]
                else:
                    u0, u1, u2 = uT[:, j, :, 0:T], uT[:, j, :, 1:1 + T], uT[:, j, :, 2:2 + T]
                    cv = cvT[:, j, :].rearrange("p (s t) -> p s t", t=T)
                    bsrc = pmm[ib][:, 0:128].rearrange("p (s t) -> p s t", t=T)
                    cbo = cbT[:, j, :].rearrange("p (s t) -> p s t", t=T)
                tr.op("dve", lambda h, j=j, u0=u0, cv=cv: h.tensor_scalar(out=cv, in0=u0, scalar1=convw[:, j, 0:1], scalar2=None, op0=ALU.mult), r=["uT", "convw"], w=["cvT"])
                tr.op("dve", lambda h, j=j, u1=u1, cv=cv: h.scalar_tensor_tensor(out=cv, in0=u1, scalar=convw[:, j, 1:2], in1=cv, op0=ALU.mult, op1=ALU.add), r=["uT", "convw", "cvT"], w=["cvT"])
                tr.op("dve", lambda h, j=j, u2=u2, cv=cv: h.scalar_tensor_tensor(out=cv, in0=u2, scalar=convw[:, j, 2:3], in1=cv, op0=ALU.mult, op1=ALU.add), r=["uT", "convw", "cvT"], w=["cvT"])
                tr.op("dve", lambda h, cv=cv, bsrc=bsrc, cbo=cbo: h.tensor_tensor(out=cbo, in0=cv, in1=bsrc, op=ALU.mult), r=["cvT", ("pmm", ib)], w=["cbT"])
            if prompt:
                uflat = uT[:].rearrange("p c s t -> p c (s t)")
                if last:
                    for c in range(4):
                        dma(conv_dst[:, c * 128:(c + 1) * 128].rearrange("t p -> p t"), uflat[:, c, 128:130], r=["uT"], nonc=True)
                tr.op("pool", lambda h: h.tensor_copy(out=uflat[:, :, 0:2], in_=uflat[:, :, 128:130]), r=["uT"], w=["uT"])
            else:
                for c in range(4):
                    for t2 in range(2):
                        dma(conv_dst[:, c * 128:(c + 1) * 128].rearrange("(s t) p -> p s t", t=2)[:, :, t2], uT[:, c, :, T + t2], r=["uT"], nonc=True)
            ck("tile: conv")
            if prompt:
                prompt_attention(ti)
            else:
                sample_attention()
            ck("tile: attention")
            for c in range(2):
                for gi, gdst, gkey in ((0, sga, "sga"), (1, sgc, "sgc")):
                    c0 = 672 + 3 * CD + gi * D + c * 512
                    sl2, key2 = slab("w_in", c0, c0 + 512)
                    i = nxt("pmm", 2)
                    for kc in range(8):
                        mm(pmm[i][:, :], hT[:, kc, :], sl2[:, kc, :], kc == 0, kc == 7, r=["hT", key2], w=[("pmm", i)])
                    tr.op("act", lambda h, i=i, gdst=gdst: h.activation(out=gdst[:, :], in_=pmm[i][:, :], func=AF.Sigmoid), r=[("pmm", i)], w=[gkey])
                sla, ka = slab("w_oa", c * 512, (c + 1) * 512)
                ia = nxt("pmm", 2)
                for kc in range(4):
                    mm(pmm[ia][:, :], oT[:, kc, :], sla[:, kc, :], kc == 0, kc == 3, r=["oT", ka], w=[("pmm", ia)])
                tr.op("dve", lambda h, c=c, ia=ia: h.tensor_tensor(out=tmp[:, c * 512:(c + 1) * 512], in0=pmm[ia][:, :], in1=sga[:, :], op=ALU.mult), r=[("pmm", ia), "sga"], w=["tmp"])
                slc, kc_ = slab("w_oc", c * 512, (c + 1) * 512)
                ic = nxt("pmm", 2)
                for kc in range(4):
                    mm(pmm[ic][:, :], cbT[:, kc, :], slc[:, kc, :], kc == 0, kc == 3, r=["cbT", kc_], w=[("pmm", ic)])
                tr.op("dve", lambda h, c=c, ic=ic: h.tensor_tensor(out=tmp2[:, c * 512:(c + 1) * 512], in0=pmm[ic][:, :], in1=sgc[:, :], op=ALU.mult), r=[("pmm", ic), "sgc"], w=["tmp2"])
            tr.op("dve", lambda h: h.tensor_tensor(out=hbf[:], in0=tmp[:], in1=tmp2[:], op=ALU.add), r=["tmp", "tmp2"], w=["hbf"])
            transposes(lambda j: hbf[:, j * 128:(j + 1) * 128], 8, mT, "mT", "hbf")
            ck("tile: m")
            for c in range(2):
                sl3, k3 = slab("w_o", c * 512, (c + 1) * 512)
                i = nxt("pmm", 2)
                for kc in range(8):
                    mm(pmm[i][:, :], mT[:, kc, :], sl3[:, kc, :], kc == 0, kc == 7, r=["mT", k3], w=[("pmm", i)])
                tr.op("dve", lambda h, c=c, i=i: h.tensor_tensor(out=tmp[:, c * 512:(c + 1) * 512], in0=pmm[i][:, :], in1=ada[:, 2 * D + c * 512:2 * D + (c + 1) * 512], op=ALU.mult), r=[("pmm", i), "ada"], w=["tmp"])
            tr.op("dve", lambda h: h.scalar_tensor_tensor(out=tmp[:], in0=X[:], scalar=ALPHA, in1=tmp[:], op0=ALU.mult, op1=ALU.add), r=[xk, "tmp"], w=["tmp"])
            layer_norm(tmp[:], 0, X[:], "tmp", xk)
            ck("tile: LN1")
            tr.op("dve", lambda h: h.tensor_tensor(out=tmp[:], in0=X[:], in1=ada[:, 4 * D:5 * D], op=ALU.mult), r=[xk, "ada"], w=["tmp"])
            tr.op("dve", lambda h: h.tensor_tensor(out=hbf[:], in0=tmp[:], in1=ada[:, 3 * D:4 * D], op=ALU.add), r=["tmp", "ada"], w=["hbf"])
            transposes(lambda j: hbf[:, j * 128:(j + 1) * 128], 8, hT, "hT", "hbf")
            ck("tile: h2")
            chunks = [(c0, min(DFF, c0 + 512)) for c0 in range(0, DFF, 512)]
            pend = {}

            def ffn_mm(c0, c1):
                n = c1 - c0
                s1, k1 = slab("w_ff1", c0, c1)
                i1 = nxt("pmm", 2)
                for kc in range(8):
                    mm(pmm[i1][:, 0:n], hT[:, kc, :], s1[:, kc, :], kc == 0, kc == 7, r=["hT", k1], w=[("pmm", i1)])
                tr.op("act", lambda h: h.activation(out=tmp[:, 0:n], in_=pmm[i1][:, 0:n], func=AF.Silu), r=[("pmm", i1)], w=["tmp"])
                s3, k3 = slab("w_ff3", c0, c1)
                i3 = nxt("pmm", 2)
                for kc in range(8):
                    mm(pmm[i3][:, 0:n], hT[:, kc, :], s3[:, kc, :], kc == 0, kc == 7, r=["hT", k3], w=[("pmm", i3)])
                pend[c0] = i3

            def ffn_mult(c0, c1):
                n = c1 - c0
                i3 = pend.pop(c0)
                tr.op("dve", lambda h: h.tensor_tensor(out=ab[:, 0:n], in0=tmp[:, 0:n], in1=pmm[i3][:, 0:n], op=ALU.mult), r=["tmp", ("pmm", i3)], w=["ab"])

            def ffn_tr(c0, c1):
                n = c1 - c0
                g = c0 // 128
                transposes(lambda j: ab[:, j * 128:(j + 1) * 128], n // 128, aT[:, g:g + n // 128, :], "aT", "ab")

            ffn_mm(*chunks[0])
            ffn_mult(*chunks[0])
            for ci in range(len(chunks)):
                if ci + 1 < len(chunks):
                    ffn_mm(*chunks[ci + 1])
                ffn_tr(*chunks[ci])
                if ci + 1 < len(chunks):
                    ffn_mult(*chunks[ci + 1])
            for c in range(2):
                i = nxt("pmm", 2)
                for (ka, kb) in ((0, 8), (8, 16), (16, 22)):
                    s2, k2 = slab("w_ff2", c * 512, (c + 1) * 512, ka, kb)
                    for kc in range(ka, kb):
                        mm(pmm[i][:, :], aT[:, kc, :], s2[:, kc - ka, :], kc == 0, kc == 21, r=["aT", k2], w=[("pmm", i)])
                tr.op("dve", lambda h, c=c, i=i: h.tensor_tensor(out=tmp[:, c * 512:(c + 1) * 512], in0=pmm[i][:, :], in1=ada[:, 5 * D + c * 512:5 * D + (c + 1) * 512], op=ALU.mult), r=[("pmm", i), "ada"], w=["tmp"])
            tr.op("dve", lambda h: h.scalar_tensor_tensor(out=tmp[:], in0=X[:], scalar=ALPHA, in1=tmp[:], op0=ALU.mult, op1=ALU.add), r=[xk, "tmp"], w=["tmp"])
            ck("tile: FFN")
            layer_norm(tmp[:], 2, X[:], "tmp", xk)
            dma(y_dst, X[:], r=[xk])

        def prompt_attention(ti):
            slk, kk_ = slab("w_ukv", 0, NH * 128)
            for half in range(2):
                i = nxt("pmm", 2)
                for kc in range(2):
                    mm(pmm[i][:, :], ckvT[:, kc, :], slk[:, kc, half * 512:(half + 1) * 512], kc == 0, kc == 1, r=["ckvT", kk_], w=[("pmm", i)])
                pv = pmm[i][:, :].rearrange("p (h e) -> p h e", e=128)
                tr.op("act", lambda h, half=half, pv=pv: h.copy(out=kk[:, half * 4:(half + 1) * 4, 0:QKN], in_=pv[:, :, 0:QKN]), r=[("pmm", i)], w=["kk"])
                tr.op("act", lambda h, half=half, pv=pv: h.copy(out=Vt[:, ti, half * 256:(half + 1) * 256].rearrange("p (h e) -> p h e", e=VH), in_=pv[:, :, QKN:128]), r=[("pmm", i)], w=[("Vt", ti)])
            tr.op("pool", lambda h: h.tensor_copy(out=krb[:], in_=krt[:]), r=["krt"], w=["krb"])
            tr.op("pool", lambda h: h.tensor_copy(out=kk[:, :, QKN:QKH], in_=krb[:].unsqueeze(1).to_broadcast([128, NH, QKR])), r=["krb"], w=["kk"])
            i = nxt("ptr", 2)
            for hh in range(NH):
                tr.op("pe", lambda h, hh=hh, i=i: h.transpose(ptr[i][0:QKH, hh * 128:(hh + 1) * 128], kk[:, hh, :], ident[:]), r=["kk", "ident"], w=[("ptr", i)])
            tr.op("dve", lambda h, i=i: h.tensor_copy(out=Kt[0:QKH, :, ti * 128:(ti + 1) * 128], in_=ptr[i][0:QKH, :].rearrange("p (j t) -> p j t", t=128)), r=[("ptr", i)], w=[("Kt", ti)])
            i = nxt("ptr", 2)
            for hh in range(NH):
                tr.op("pe", lambda h, hh=hh, i=i: h.transpose(ptr[i][0:QKH, hh * 128:(hh + 1) * 128], qb[:, hh, :], ident[:]), r=["qb", "ident"], w=[("ptr", i)])
            tr.op("dve", lambda h, i=i: h.tensor_copy(out=qT[0:QKH, :, :], in_=ptr[i][0:QKH, :].rearrange("p (j t) -> p j t", t=128)), r=[("ptr", i)], w=["qT"])
            nk = ti + 1
            kkeys = [("Kt", t) for t in range(nk)]
            vkeys = [("Vt", t) for t in range(nk)]
            nch = (nk + 3) // 4

            def head_front(hh):
                par = hh % 2
                sh_, shk = sth[par], ("sth", par)
                Pp = P[par]
                for ch in range(nch):
                    k0 = ch * 512
                    k1 = min(nk * 128, k0 + 512)
                    isl = (ch == nch - 1)
                    mm(psc[ch][:, 0:k1 - k0], qT[0:QKH, hh, :], Kt[0:QKH, hh, k0:k1], True, not isl, r=["qT"] + kkeys, w=[("psc", ch)])
                    if isl:
                        d0 = (nk - 1) * 128 - k0
                        mm(psc[ch][:, d0:d0 + 128], ident[:], maskc[:], False, True, r=["ident", "maskc"], w=[("psc", ch)])
                for ch in range(nch):
                    k0 = ch * 512
                    k1 = min(nk * 128, k0 + 512)
                    tr.op("dve", lambda h, ch=ch, n=k1 - k0: h.reduce_max(out=sh_[:, 1 + ch:2 + ch], in_=psc[ch][:, 0:n], axis=AX.X), r=[("psc", ch)], w=[shk])
                tr.op("dve", lambda h: h.reduce_max(out=sh_[:, 0:1], in_=sh_[:, 1:1 + nch], axis=AX.X), r=[shk], w=[shk])
                tr.op("dve", lambda h: h.tensor_scalar(out=sh_[:, 0:1], in0=sh_[:, 0:1], scalar1=-SCALE, scalar2=None, op0=ALU.mult), r=[shk], w=[shk])
                for ch in range(nch):
                    k0 = ch * 512
                    k1 = min(nk * 128, k0 + 512)
                    tr.op("act", lambda h, ch=ch, k0=k0, k1=k1: h.activation(out=Pp[:, k0:k1], in_=psc[ch][:, 0:k1 - k0], func=AF.Exp, bias=sh_[:, 0:1], scale=SCALE, accum_out=sh_[:, 1 + ch:2 + ch]),
                          r=[("psc", ch), shk], w=[("P", par, ch), shk])

            def head_back(hh):
                par = hh % 2
                sh_, shk = sth[par], ("sth", par)
                Pp = P[par]
                io = nxt("pmm", 2)
                for g in range(0, nk, 8):
                    ng = min(8, nk - g)
                    pkeys = [("P", par, c) for c in range(g // 4, (g + ng + 3) // 4)]
                    i = nxt("ptr", 2)
                    for j in range(ng):
                        tr.op("pe", lambda h, j=j, i=i, g=g: h.transpose(ptr[i][:, j * 128:(j + 1) * 128], Pp[:, (g + j) * 128:(g + j + 1) * 128], ident[:]),
                              r=pkeys + ["ident"], w=[("ptr", i)])
                    tr.op("dve", lambda h, i=i, ng=ng: h.tensor_copy(out=PT[:, 0:ng, :], in_=ptr[i][:, 0:ng * 128].rearrange("p (j t) -> p j t", t=128)), r=[("ptr", i)], w=["PT"])
                    for t in range(g, g + ng):
                        mm(pmm[io][:, 0:VH], PT[:, t - g, :], Vt[:, t, hh * VH:(hh + 1) * VH], t == 0, t == nk - 1, r=["PT"] + vkeys, w=[("pmm", io)])
                tr.op("dve", lambda h: h.reduce_sum(out=sh_[:, 5:6], in_=sh_[:, 1:1 + nch], axis=AX.X), r=[shk], w=[shk])
                tr.op("dve", lambda h: h.reciprocal(out=sh_[:, 6:7], in_=sh_[:, 5:6]), r=[shk], w=[shk])
                tr.op("act", lambda h, io=io: h.activation(out=ob[:, hh * VH:(hh + 1) * VH], in_=pmm[io][:, 0:VH], func=AF.Copy, scale=sh_[:, 6:7]), r=[("pmm", io), shk], w=["hbf"])

            head_front(0)
            for hh in range(NH):
                if hh + 1 < NH:
                    head_front(hh + 1)
                head_back(hh)
            transposes(lambda j: ob[:, j * 128:(j + 1) * 128], 4, oT, "oT", "hbf")

        def sample_attention():
            i = nxt("ptr", 2)
            qbf = qb[:]
            tr.op("dve", lambda h: h.tensor_copy(out=hbf[:, 0:512].rearrange("p (h e) -> p h e", e=QKN), in_=qb[:, :, 0:QKN]), r=["qb"], w=["hbf"])
            for pr in range(4):
                tr.op("pe", lambda h, pr=pr, i=i: h.transpose(ptr[i][:, pr * 128:(pr + 1) * 128], hbf[:, pr * 128:(pr + 1) * 128], ident[:]), r=["hbf", "ident"], w=[("ptr", i)])
            tr.op("dve", lambda h, i=i: h.tensor_copy(out=qnpT[:], in_=ptr[i][:, 0:512].rearrange("p (j t) -> p j t", t=128)), r=[("ptr", i)], w=["qnpT"])
            for kc in range(2):
                for hh in range(NH):
                    pr, off = hh // 2, (hh % 2) * 64
                    ia = nxt("pmm", 2)
                    mm(pmm[ia][:, 0:128], wukT[off:off + 64, pr, kc * 128:(kc + 1) * 128], qnpT[off:off + 64, pr, :], True, True, r=["wukT", "qnpT"], w=[("pmm", ia)])
                    tr.op("act", lambda h, kc=kc, hh=hh, ia=ia: h.copy(out=QaT[:, kc, :, hh * T:(hh + 1) * T], in_=pmm[ia][:, 0:128].rearrange("p (s t) -> p s t", t=T)), r=[("pmm", ia)], w=["QaT"])
            tr.op("dve", lambda h: h.tensor_copy(out=hbf[:, 512:768].rearrange("p (h e) -> p h e", e=QKR), in_=qb[:, :, QKN:QKH]), r=["qb"], w=["hbf"])
            i = nxt("ptr", 2)
            for hh in range(NH):
                tr.op("pe", lambda h, hh=hh, i=i: h.transpose(ptr[i][0:QKR, hh * 128:(hh + 1) * 128], hbf[:, 512 + hh * QKR:512 + (hh + 1) * QKR], ident[:]), r=["hbf", "ident"], w=[("ptr", i)])
            for hh in range(NH):
                tr.op("dve", lambda h, hh=hh, i=i: h.tensor_copy(out=QrT[:, :, hh * T:(hh + 1) * T], in_=ptr[i][0:QKR, hh * 128:(hh + 1) * 128].rearrange("p (s t) -> p s t", t=T)), r=[("ptr", i)], w=["QrT"])
            XTn = sb("XTn", [128, 3, 128], BF16)
            tr.op("pool", lambda h: h.tensor_copy(out=krb[:], in_=krt[:]), r=["krt"], w=["krb"])
            tr.op("pool", lambda h: h.tensor_copy(out=XTn[:, 0:2, :], in_=ckvT[:]), r=["ckvT"], w=["XTn"])
            i = nxt("ptr", 2)
            tr.op("pe", lambda h, i=i: h.transpose(ptr[i][0:QKR, 0:128], krb[:], ident[:]), r=["krb", "ident"], w=[("ptr", i)])
            tr.op("dve", lambda h, i=i: h.tensor_copy(out=XTn[0:QKR, 2, :], in_=ptr[i][0:QKR, 0:128]), r=[("ptr", i)], w=["XTn"])

            npg_cnt = [0]
            KPG = NPG

            def flash_front(par, s, xts, KP, mask_off=None):
                for gi, (xv, xk_) in enumerate(xts):
                    o = psc[par][0:64, gi * KP:(gi + 1) * KP]
                    lastmm = mask_off is None
                    mm(o, QaT[:, 0, s, :], xv[:, 0, 0:KP], True, False, r=["QaT", xk_], w=[("psc", par)])
                    mm(o, QaT[:, 1, s, :], xv[:, 1, 0:KP], False, False, r=["QaT", xk_], w=[("psc", par)])
                    mm(o, QrT[:, s, :], xv[0:QKR, 2, 0:KP], False, lastmm, r=["QrT", xk_], w=[("psc", par)])
                    if mask_off is not None:
                        mm(o, ident[0:64, 0:64], mbig[:, mask_off:mask_off + KP], False, True, r=["ident", "mbig"], w=[("psc", par)])

            def flash_back(par, g, xtoks, KP, first):
                W = g * KP
                sc = psc[par]
                pa = psc[2 + par]
                pk = ("psc", par)
                pak = ("psc", 2 + par)
                Psb, PTb = Ps[par], PTs[par]
                Pk, PTk = ("Ps", par), ("PTs", par)
                tr.op("dve", lambda h: h.reduce_max(out=sm[:, 1:2], in_=sc[0:64, 0:W], axis=AX.X), r=[pk], w=["sm"])
                if not first:
                    tr.op("dve", lambda h: h.tensor_tensor(out=sm[:, 1:2], in0=sm[:, 1:2], in1=sm[:, 0:1], op=ALU.max), r=["sm"], w=["sm"])
                    tr.op("dve", lambda h: h.tensor_tensor(out=sm[:, 6:7], in0=sm[:, 0:1], in1=sm[:, 1:2], op=ALU.subtract), r=["sm"], w=["sm"])
                    tr.op("act", lambda h: h.activation(out=sm[:, 3:4], in_=sm[:, 6:7], func=AF.Exp, scale=SCALE), r=["sm"], w=["sm"])
                tr.op("dve", lambda h: h.tensor_scalar(out=sm[:, 2:3], in0=sm[:, 1:2], scalar1=-SCALE, scalar2=None, op0=ALU.mult), r=["sm"], w=["sm"])
                tr.op("act", lambda h: h.activation(out=Psb[:, 0:W], in_=sc[0:64, 0:W], func=AF.Exp, bias=sm[:, 2:3], scale=SCALE, accum_out=sm[:, 5:6]), r=[pk, "sm"], w=[Pk, "sm"])
                tr.op("dve", lambda h: h.tensor_copy(out=sm[:, 0:1], in_=sm[:, 1:2]), r=["sm"], w=["sm"])
                if first:
                    tr.op("dve", lambda h: h.tensor_copy(out=sm[:, 4:5], in_=sm[:, 5:6]), r=["sm"], w=["sm"])
                else:
                    tr.op("dve", lambda h: h.scalar_tensor_tensor(out=sm[:, 4:5], in0=sm[:, 4:5], scalar=sm[:, 3:4], in1=sm[:, 5:6], op0=ALU.mult, op1=ALU.add), r=["sm"], w=["sm"])
                ip = nxt("ptr", 2)
                for gi in range(g):
                    tr.op("pe", lambda h, gi=gi, ip=ip: h.transpose(ptr[ip][0:KP, gi * 128:gi * 128 + 64], Psb[:, gi * KP:(gi + 1) * KP], ident[0:64, 0:64]), r=[Pk, "ident"], w=[("ptr", ip)])
                tr.op("dve", lambda h, ip=ip: h.tensor_copy(out=PTb[0:KP, 0:g, :], in_=ptr[ip][0:KP, 0:g * 128].rearrange("p (j t) -> p j t", t=128)[:, :, 0:64]), r=[("ptr", ip)], w=[PTk])
                for gi, (xa, xak) in enumerate(xtoks):
                    mm(pa[0:64, 0:KVL], PTb[0:KP, gi, :], xa, gi == 0, gi == g - 1, r=[PTk, xak], w=[pak])
                if first:
                    tr.op("dve", lambda h: h.tensor_copy(out=accs[:], in_=pa[0:64, 0:KVL]), r=[pak], w=["accs"])
                else:
                    tr.op("dve", lambda h: h.scalar_tensor_tensor(out=accs[:], in0=accs[:], scalar=sm[:, 3:4], in1=pa[0:64, 0:KVL], op0=ALU.mult, op1=ALU.add), r=["accs", "sm", pak], w=["accs"])

            def finalize(s):
                tr.op("dve", lambda h: h.reciprocal(out=sm[:, 7:8], in_=sm[:, 4:5]), r=["sm"], w=["sm"])
                tr.op("act", lambda h: h.activation(out=olat[:], in_=accs[:], func=AF.Copy, scale=sm[:, 7:8]), r=["accs", "sm"], w=["olat"])
                io = nxt("ptr", 2)
                for kc in range(2):
                    tr.op("pe", lambda h, kc=kc, io=io: h.transpose(ptr[io][:, kc * 128:kc * 128 + 64], olat[:, kc * 128:(kc + 1) * 128], ident[0:64, 0:64]), r=["olat", "ident"], w=[("ptr", io)])
                tr.op("dve", lambda h, io=io: h.tensor_copy(out=olT[:], in_=ptr[io][:, 0:256].rearrange("p (j t) -> p j t", t=128)[:, :, 0:64]), r=[("ptr", io)], w=["olT"])
                for pr in range(4):
                    ia = nxt("pmm", 2)
                    for sub in range(2):
                        hh = pr * 2 + sub
                        for kc in range(2):
                            mm(pmm[ia][sub * 64:(sub + 1) * 64, 0:T], wuvb[:, kc, hh * VH:(hh + 1) * VH], olT[:, kc, hh * T:(hh + 1) * T], kc == 0, kc == 1, r=["wuvb", "olT"], w=[("pmm", ia)])
                    tr.op("act", lambda h, pr=pr, ia=ia, s=s: h.copy(out=oT[:, pr, s * T:(s + 1) * T], in_=pmm[ia][:, 0:T]), r=[("pmm", ia)], w=["oT"])

            dma(ptT[0:KPG, :], pt_d.rearrange("s p -> p s"), w=["ptT"], nonc=True)
            groups = []
            for s in range(SB):
                groups.append(("new", s, None))
                for c in range(NCH):
                    groups.append(("page", s, c))
            state = {}

            issued = [0]
            NPGRP = SB * NCH

            def ensure_issued(upto):
                while issued[0] < min(upto, NPGRP):
                    m = issued[0]
                    issued[0] += 1
                    ms, mc, mk = m // NCH, m % NCH, m % 4
                    tr.op("dve", lambda h, ms=ms, mc=mc, mk=mk: h.tensor_scalar(out=idxr[mk][0:KPG, :], in0=ptT[0:KPG, ms:ms + 1], scalar1=NCH, scalar2=mc, op0=ALU.mult, op1=ALU.add),
                          r=["ptT"], w=[("idxr", mk)])
                    tr.op("pool", lambda h, mk=mk: h.indirect_dma_start(out=Xb[mk][0:KPG].rearrange("p t l -> p (t l)"), out_offset=None, in_=pool_d[:, :],
                                                                       in_offset=bass.IndirectOffsetOnAxis(ap=idxr[mk][0:KPG, 0:1], axis=0)),
                          r=[("idxr", mk)], w=[("Xb", mk)], dma=True)

            def front(gidx):
                kind, s, c = groups[gidx]
                par = gidx % 2
                if kind == "new":
                    flash_front(par, s, [(XTn, "XTn")], 128, mask_off=120 - s * T)
                    state[gidx] = (1, [(ckvb[:, :], "ckvb")], 128, True)
                    return
                n = npg_cnt[0]
                npg_cnt[0] += 1
                b = n % 4
                ensure_issued(n + 3)
                xts, xtoks = [], []
                XTp = XT[par]
                for t in range(TCH):
                    it = nxt("ptr", 2)
                    tr.op("pe", lambda h, t=t, it=it: h.transpose(ptr[it][:, 0:KPG], Xb[b][0:KPG, t, 0:128], ident[0:KPG, 0:KPG]), r=[("Xb", b), "ident"], w=[("ptr", it)])
                    tr.op("pe", lambda h, t=t, it=it: h.transpose(ptr[it][:, 128:128 + KPG], Xb[b][0:KPG, t, 128:256], ident[0:KPG, 0:KPG]), r=[("Xb", b), "ident"], w=[("ptr", it)])
                    tr.op("pe", lambda h, t=t, it=it: h.transpose(ptr[it][0:QKR, 256:256 + KPG], Xb[b][0:KPG, t, 256:LAT], ident[0:KPG, 0:KPG]), r=[("Xb", b), "ident"], w=[("ptr", it)])
                    tr.op("dve", lambda h, t=t, it=it: h.tensor_copy(out=XTp[:, 0:2, t, 0:KPG], in_=ptr[it][:, 0:256].rearrange("p (j t) -> p j t", t=128)[:, :, 0:KPG]), r=[("ptr", it)], w=[("XT", par, t)])
                    tr.op("act", lambda h, t=t, it=it: h.copy(out=XTp[0:QKR, 2, t, 0:KPG], in_=ptr[it][0:QKR, 256:256 + KPG]), r=[("ptr", it)], w=[("XT", par, t)])
                    xts.append((XTp[:, :, t, :], ("XT", par, t)))
                    xtoks.append((Xb[b][0:KPG, t, 0:KVL], ("Xb", b)))
                if KPG == 128:
                    o = psc[par][0:64, 0:TCH * 128]
                    xk_all = [("XT", par, t) for t in range(TCH)]
                    mm(o, QaT[:, 0, s, :], XTp[:, 0, :, :].rearrange("p t k -> p (t k)"), True, False, r=["QaT"] + xk_all, w=[("psc", par)])
                    mm(o, QaT[:, 1, s, :], XTp[:, 1, :, :].rearrange("p t k -> p (t k)"), False, False, r=["QaT"] + xk_all, w=[("psc", par)])
                    mm(o, QrT[:, s, :], XTp[0:QKR, 2, :, :].rearrange("p t k -> p (t k)"), False, True, r=["QrT"] + xk_all, w=[("psc", par)])
                else:
                    flash_front(par, s, xts, KPG)
                state[gidx] = (TCH, xtoks, KPG, False)

            def back(gidx):
                kind, s, c = groups[gidx]
                g, xtoks, KP, first = state.pop(gidx)
                flash_back(gidx % 2, g, xtoks, KP, first)
                if kind == "page" and c == NCH - 1:
                    finalize(s)

            front(0)
            for gidx in range(len(groups)):
                if gidx + 1 < len(groups):
                    front(gidx + 1)
                back(gidx)

        for b in range(PB):
            compute_ada(cp_d[b:b + 1, :], 1, 128)
            ck("ada")
            tr.op("pool", lambda h: h.memset(uT[:], 0.0), r=["uT"], w=["uT"])
            for ti in range(NTS):
                r0 = b * SEQ + ti * 128
                tile(xp[r0:r0 + 128, :], yp[r0:r0 + 128, :], latp[r0:r0 + 128, :], krp[r0:r0 + 128, :],
                     cosp[:, ti, :], sinp[:, ti, :], ["rope", "rope2"], True, ti=ti,
                     conv_dst=convp[b * 2:(b + 1) * 2, :], first=(ti == 0), last=(ti == NTS - 1))
        compute_ada(cs_d, SB, T)
        tr.op("pool", lambda h: h.memset(uT[:], 0.0), r=["uT"], w=["uT", "uTm"])
        for c in range(4):
            for t2 in range(2):
                dma(uT[:, c, :, t2], sconv[:, c * 128:(c + 1) * 128].rearrange("(s t) p -> p s t", t=2)[:, :, t2], r=["uTm"], w=["uT"], nonc=True)
        tile(xs, ys, lats, krs, coss[:], sins[:], ["ropes", "ropes2"], False, conv_dst=convs)

    except _Stop:
        pass
    sem_es = ExitStack()
    sems = {e: sem_es.enter_context(nc.semaphore("s_" + e)) for e in ("pe", "act", "dve", "pool")}
    ring = {q: [sem_es.enter_context(nc.semaphore(f"ring_{q}{i}")) for i in range(NRING)] for q in Tracker.DMAQ}
    tr.emit(nc, sems, ring, None)
    sem_es.close()
    es.close()
    return nc


def rope_tables(pos):
    inv = (10000.0 ** (-(np.arange(0, QKR, 2, dtype=np.float32)) / np.float32(QKR))).astype(np.float32)
    ang = (pos.astype(np.float32)[:, None] * inv[None, :]).astype(np.float32)
    return np.cos(ang).astype(np.float32), np.sin(ang).astype(np.float32)


_CACHE = {}


def run(inputs, SEQ, NPG, NPOOL, PAST):
    key = (SEQ, NPG, NPOOL)
    if key not in _CACHE:
        _CACHE[key] = build(SEQ, NPG, NPOOL)
    nc = _CACHE[key]
    f = lambda a: np.ascontiguousarray(np.asarray(a))
    pool = np.concatenate([np.asarray(inputs["cache_kv_latent"])[0], np.asarray(inputs["cache_k_rope"])[0]], axis=-1)
    pool = np.ascontiguousarray(pool, dtype=np.float32).reshape(NPOOL * NCH, TCH * LAT)
    cosp, sinp = rope_tables(np.arange(SEQ))
    cs_, ss_ = rope_tables(PAST + np.arange(T))
    coss = np.tile(cs_, (SB, 1)); sins = np.tile(ss_, (SB, 1))
    ident = np.eye(128, dtype=np.float32).astype(ml_dtypes.bfloat16)
    maskc = np.where(np.arange(128)[None, :] <= np.arange(128)[:, None], 0.0, NEG).astype(np.float32).astype(ml_dtypes.bfloat16)
    mbig = np.full((64, 248), NEG, np.float32)
    for r in range(64):
        t = r % T
        mbig[r, 120:120 + t + 1] = 0.0
    mbig = mbig.astype(ml_dtypes.bfloat16)
    common = {k: f(inputs[k])[0] for k in WSPEC if k != "w_ukv"}
    common["w_ukv"] = f(inputs["w_ukv"])[0].reshape(KVL, NH * 128)
    for k in ("b_ada", "q_norm_g", "kv_norm_g", "ln1_g", "ln1_b", "ln2_g", "ln2_b"):
        common[k] = f(inputs[k]).reshape(1, -1)
    common["conv_w"] = f(inputs["conv_w"])[0]
    common.update(cosp=cosp, sinp=sinp, coss=coss, sins=sins, ident=ident, maskc=maskc, mbig=mbig, pool=pool)
    xp = f(inputs["x_prompt"]); xs = f(inputs["x_sample"]); ptab = f(inputs["page_table"]).astype(np.int32)
    sc = f(inputs["state_conv"])[0]; cp = f(inputs["c_prompt"]); cs = f(inputs["c_sample"])
    in_maps = []
    for c in range(8):
        m = dict(common)
        m["xp"] = xp[c * PB:(c + 1) * PB].reshape(PB * SEQ, D)
        m["xs"] = xs[c * SB:(c + 1) * SB].reshape(SB * T, D)
        m["pt"] = np.ascontiguousarray(ptab[c * SB:(c + 1) * SB])
        m["sconv"] = np.ascontiguousarray(sc[c * SB:(c + 1) * SB].reshape(SB * 2, CD))
        m["cp"] = np.ascontiguousarray(cp[c * PB:(c + 1) * PB]); m["cs"] = np.ascontiguousarray(cs[c * SB:(c + 1) * SB])
        in_maps.append(m)
    import os as _os
    if _os.environ.get("KTRACE"):
        res = run_bass_kernel_spmd(nc, in_maps, core_ids=list(range(8)), trace=True)
        print("KTRACE exec_time_ns", res.exec_time_ns, flush=True)
    else:
        res = run_bass_kernel_spmd(nc, in_maps, core_ids=list(range(8)))
    R = res.results
    cat = lambda k: np.concatenate([np.asarray(R[c][k]) for c in range(8)], axis=0)
    B = 8 * PB
    y_p = cat("yp").reshape(B, SEQ, D); y_s = cat("ys").reshape(8 * SB, T, D)
    lat_p = cat("latp").reshape(1, B, SEQ, KVL); kr_p = cat("krp").reshape(1, B, SEQ, QKR); cv_p = cat("convp").reshape(1, B, 2, CD)
    lat_s = cat("lats").reshape(1, 8 * SB, T, KVL); kr_s = cat("krs").reshape(1, 8 * SB, T, QKR); cv_s = cat("convs").reshape(1, 8 * SB, 2, CD)
    return (y_p, y_s, lat_p, kr_p, cv_p, lat_s, kr_s, cv_s)


def kernel(**inputs):
    SEQ = inputs["x_prompt"].shape[1]
    NPG = inputs["page_table"].shape[1]
    NPOOL = inputs["cache_kv_latent"].shape[1]
    return run(inputs, SEQ, NPG, NPOOL, NPG * PAGE)
```

```python
import math
from contextlib import ExitStack
import numpy as np
import ml_dtypes
import concourse.bass as bass
import concourse.mybir as mybir
from concourse.bass_utils import run_bass_kernel_spmd

F32 = mybir.dt.float32
BF16 = mybir.dt.bfloat16
I32 = mybir.dt.int32
AF = mybir.ActivationFunctionType
ALU = mybir.AluOpType
AX = mybir.AxisListType

D = 1024
NH = 8
QKN, QKR, QKH, VH = 64, 32, 96, 64
QL, KVL = 384, 256
CD = 512
DFF = 2816
NIN = 4256
DEPTH = 1
ALPHA = (2.0 * DEPTH) ** 0.25
LN_EPS = 1e-5
RMS_EPS = 1e-6
SCALE = QKH ** -0.5
PB, SB, T = 2, 16, 8
PAGE = 128
LAT = KVL + QKR
NEG = -30000.0
TCH = 4
NCH = PAGE // TCH
SAFE_SAME = True
NRING = 16


class Tracker:
    ENG = ("pe", "act", "dve", "pool", "sp")

    def __init__(self):
        self.prog = {e: [] for e in self.ENG}
        self.last_w = {}
        self.readers = {}

    def op(self, eng, fn, r=(), w=(), dma=False):
        idx = len(self.prog[eng])
        deps = set()
        for k in r:
            deps.update(self.last_w.get(k, ()))
        for k in w:
            deps.update(self.last_w.get(k, ()))
            rd = self.readers.get(k)
            if rd:
                for e2, v in rd.items():
                    if isinstance(v, list):
                        deps.update(v)
                    else:
                        deps.add((e2, v))
        deps.discard((eng, idx))
        self.prog[eng].append(dict(fn=fn, deps=deps, dma=dma))
        for k in r:
            rd = self.readers.setdefault(k, {})
            if dma:
                rd.setdefault("dma", []).append((eng, idx))
            else:
                rd[eng] = idx
        for k in w:
            prev = self.last_w.get(k, [])
            if dma and prev and all(self.prog[e2][i2]["dma"] for (e2, i2) in prev) and not self.readers.get(k):
                self.last_w[k] = prev + [(eng, idx)]
            else:
                self.last_w[k] = [(eng, idx)]
            self.readers[k] = {}
        return (eng, idx)

    DMAQ = ("sp", "pool")

    def emit(self, nc, sems, ring, final_sem):
        prog = self.prog
        dma_no = {}
        ndma = {}
        for q in self.DMAQ:
            n = 0
            for i, ins in enumerate(prog[q]):
                if ins["dma"]:
                    dma_no[(q, i)] = n
                    n += 1
            ndma[q] = n

        def is_dma(e2, i2):
            return prog[e2][i2]["dma"]

        need = set()
        for e in self.ENG:
            for i, ins in enumerate(prog[e]):
                for (e2, i2) in ins["deps"]:
                    if is_dma(e2, i2):
                        continue
                    if e2 == e and not (SAFE_SAME and e in ("act", "dve", "pool")):
                        continue
                    need.add((e2, i2))
        val = {}
        for e in self.ENG:
            c = 0
            for i in range(len(prog[e])):
                if (e, i) in need:
                    c += 1
                    val[(e, i)] = c

        def run(e, h):
            waited = {}
            for i, ins in enumerate(prog[e]):
                waits = {}
                for (e2, i2) in ins["deps"]:
                    if is_dma(e2, i2):
                        d = dma_no[(e2, i2)]
                        key = ("ring", e2, d % NRING)
                        v = 16 * (d // NRING + 1)
                    else:
                        if (e2, i2) not in val:
                            continue
                        key = e2
                        v = val[(e2, i2)]
                    if waited.get(key, 0) >= v:
                        continue
                    waits[key] = max(waits.get(key, 0), v)
                if ins["dma"]:
                    d = dma_no[(e, i)]
                    if d >= NRING:
                        key = ("ring", e, d % NRING)
                        v = 16 * (d // NRING)
                        if waited.get(key, 0) < v:
                            waits[key] = max(waits.get(key, 0), v)
                for key, v in waits.items():
                    sem = ring[key[1]][key[2]] if isinstance(key, tuple) else sems[key]
                    h.wait_ge(sem, v)
                    waited[key] = v
                bi = ins["fn"](h)
                if ins["dma"]:
                    bi.then_inc(ring[e][dma_no[(e, i)] % NRING], 16)
                elif (e, i) in val:
                    bi.then_inc(sems[e], 1)
            if e in self.DMAQ:
                for sl in range(min(NRING, ndma[e])):
                    cntd = (ndma[e] - 1 - sl) // NRING + 1
                    h.wait_ge(ring[e][sl], 16 * cntd)

        with nc.Block() as block:
            @block.sync
            def _(h):
                run("sp", h)

            @block.tensor
            def _(h):
                run("pe", h)

            @block.scalar
            def _(h):
                run("act", h)

            @block.vector
            def _(h):
                run("dve", h)

            @block.gpsimd
            def _(h):
                run("pool", h)


WSPEC = {
    "w_ada": (D, 6 * D), "w_in": (D, NIN), "w_uq": (QL, NH * QKH), "w_ukv": (KVL, NH * 128),
    "w_oa": (NH * VH, D), "w_oc": (CD, D), "w_o": (D, D), "w_ff1": (D, DFF), "w_ff3": (D, DFF), "w_ff2": (DFF, D),
}


class _Stop(Exception):
    pass


def _segments():
    seg = {}
    seg["w_ada"] = [(c * 512, (c + 1) * 512, 0, 8) for c in range(12)]
    seg["w_in"] = [(0, 512, 0, 8), (512, 672, 0, 8)] + [(672 + c * 512, 672 + (c + 1) * 512, 0, 8) for c in range(7)]
    seg["w_uq"] = [(0, 384, 0, 3), (384, 768, 0, 3)]
    seg["w_ukv"] = [(0, NH * 128, 0, 2)]
    seg["w_oa"] = [(0, 512, 0, 4), (512, 1024, 0, 4)]
    seg["w_oc"] = [(0, 512, 0, 4), (512, 1024, 0, 4)]
    seg["w_o"] = [(0, 512, 0, 8), (512, 1024, 0, 8)]
    ff = [(c0, min(DFF, c0 + 512), 0, 8) for c0 in range(0, DFF, 512)]
    seg["w_ff1"] = ff
    seg["w_ff3"] = ff
    seg["w_ff2"] = [(c * 512, (c + 1) * 512, ka, kb) for c in range(2) for (ka, kb) in ((0, 8), (8, 16), (16, 22))]
    off = {}
    for name, lst in seg.items():
        o = 0
        for sg in lst:
            off[(name,) + sg] = o
            o += (sg[3] - sg[2]) * (sg[1] - sg[0])
        K, N = WSPEC[name]
        assert o == (K // 128) * N, (name, o)
    return seg, off


def build(SEQ, NPG, NPOOL):
    import os
    dbg_stop = int(os.environ.get("KDBG", "0"))
    ckc = [0]

    def ck(name):
        ckc[0] += 1
        if (dbg_stop and ckc[0] == dbg_stop) or (os.environ.get("KDBG_NAME") == name):
            print("KDBG stop at checkpoint", ckc[0], name, flush=True)
            raise _Stop()
    NTS = SEQ // 128
    nc = bass.Bass("TRN2", target_bir_lowering=False)
    tr = Tracker()
    es = ExitStack()

    def din(name, shape, dt=F32):
        return nc.dram_tensor(name, list(shape), dt, kind="ExternalInput").ap()

    def dout(name, shape, dt=F32):
        return nc.dram_tensor(name, list(shape), dt, kind="ExternalOutput").ap()

    xp = din("xp", [PB * SEQ, D]); xs = din("xs", [128, D])
    pool_d = din("pool", [NPOOL * NCH, TCH * LAT]); pt_d = din("pt", [SB, NPG], I32)
    sconv = din("sconv", [SB * 2, CD]); cp_d = din("cp", [PB, D]); cs_d = din("cs", [SB, D])
    wd = {k: din(k, [v[0], v[1]]) for k, v in WSPEC.items()}
    b_ada = din("b_ada", [1, 6 * D]); qg = din("q_norm_g", [1, QL]); kvg = din("kv_norm_g", [1, KVL])
    convw_d = din("conv_w", [3, CD])
    lnd = [din(n, [1, D]) for n in ("ln1_g", "ln1_b", "ln2_g", "ln2_b")]
    cosp_d = din("cosp", [SEQ, 16]); sinp_d = din("sinp", [SEQ, 16])
    coss_d = din("coss", [128, 16]); sins_d = din("sins", [128, 16])
    ident_d = din("ident", [128, 128], BF16); maskc_d = din("maskc", [128, 128], BF16)
    mbig_d = din("mbig", [64, 248], BF16)

    yp = dout("yp", [PB * SEQ, D]); ys = dout("ys", [128, D])
    latp = dout("latp", [PB * SEQ, KVL]); krp = dout("krp", [PB * SEQ, QKR]); convp = dout("convp", [PB * 2, CD])
    lats = dout("lats", [128, KVL]); krs = dout("krs", [128, QKR]); convs = dout("convs", [SB * 2, CD])

    SEG, SEGOFF = _segments()
    wb = {k: nc.dram_tensor(k + "_bf", [128, (v[0] // 128) * v[1]], BF16, kind="Internal").ap() for k, v in WSPEC.items()}

    def sb(name, shape, dt=F32):
        return es.enter_context(nc.sbuf_tensor("sb_" + name, list(shape), dt))

    def ps(name, shape, dt=F32):
        return es.enter_context(nc.psum_tensor("ps_" + name, list(shape), dt))

    ident = sb("ident", [128, 128], BF16); maskc = sb("maskc", [128, 128], BF16); mbig = sb("mbig", [64, 248], BF16)
    lnbc = sb("lnbc", [128, 2, D]); qgbc = sb("qgbc", [128, QL]); kvgbc = sb("kvgbc", [128, KVL])
    convw = sb("convw", [128, 4, 3])
    cosp = sb("cosp", [128, NTS, 16]); sinp = sb("sinp", [128, NTS, 16])
    coss = sb("coss", [128, 16]); sins = sb("sins", [128, 16])
    ada = sb("ada", [128, 6 * D])
    cT = sb("cT", [128, 8, SB]); cTs = sb("cTs", [128, 8, SB]); cTexp = sb("cTexp", [128, 8, 128], BF16)
    xt = [sb(f"xt{i}", [128, D]) for i in range(1)]
    hbf = sb("hbf", [128, D], BF16); hT = sb("hT", [128, 8, 128], BF16)
    tmp = sb("tmp", [128, D]); tmp2 = sb("tmp2", [128, D])
    z = sb("z", [128, 672])
    st8 = sb("st8", [128, 16]); sth = [sb(f"sth{i}", [128, 8]) for i in range(2)]
    qn = sb("qn", [128, QL], BF16); qnT = sb("qnT", [128, 3, 128], BF16)
    ckv = sb("ckv", [128, KVL]); ckvb = sb("ckvb", [128, KVL], BF16); ckvT = sb("ckvT", [128, 2, 128], BF16)
    krt = sb("krt", [128, QKR]); krb = sb("krb", [128, QKR], BF16)
    q = sb("q", [128, NH, QKR]); qb = sb("qb", [128, NH, QKH], BF16); qT = sb("qT", [128, NH, 128], BF16)
    kk = sb("kk", [128, NH, QKH], BF16)
    Kt = sb("Kt", [128, NH, SEQ], BF16); Vt = sb("Vt", [128, NTS, NH * VH], BF16)
    P = [sb(f"P{i}", [128, SEQ], BF16) for i in range(2)]; PT = sb("PT", [128, 8, 128], BF16)
    oT = sb("oT", [128, 4, 128], BF16)
    uT = sb("uT", [128, 4, SB, 2 + T]); cvT = sb("cvT", [128, 4, 128]); cbT = sb("cbT", [128, 4, 128], BF16)
    cgs = sb("cgs", [128, 4, 128])
    sga = sb("sga", [128, 512]); sgc = sb("sgc", [128, 512])
    mT = sb("mT", [128, 8, 128], BF16)
    ab = sb("ab", [128, 512], BF16); aT = sb("aT", [128, 22, 128], BF16)
    slabs = [sb(f"slab{i}", [128, 8 * 512], BF16) for i in range(3)]
    ptT = sb("ptT", [128, SB], I32)
    idxr = [sb(f"idxr{i}", [128, 1], I32) for i in range(4)]
    wukT = sb("wukT", [128, 4, KVL], BF16)
    wuvb = sb("wuvb", [128, 2, NH * VH], BF16)
    qnpT = sb("qnpT", [128, 4, 128], BF16)
    QaT = sb("QaT", [128, 2, SB, 64], BF16)
    QrT = sb("QrT", [32, SB, 64], BF16)
    Xb = [sb(f"Xb{i}", [128, TCH, LAT], BF16) for i in range(4)]
    XT = [sb(f"XT{i}", [128, 3, TCH, 128], BF16) for i in range(2)]
    Ps = [sb(f"Ps{i}", [64, 512], BF16) for i in range(2)]; PTs = [sb(f"PTs{i}", [128, 4, 64], BF16) for i in range(2)]
    accs = sb("accs", [64, KVL]); sm = sb("sm", [64, 16]); olat = sb("olat", [64, KVL], BF16); olT = sb("olT", [128, 2, 64], BF16)

    pmm = [ps(f"pmm{i}", [128, 512]) for i in range(2)]
    ptr = [ps(f"ptr{i}", [128, 1024], BF16) for i in range(2)]
    psc = [ps(f"psc{i}", [128, 512]) for i in range(4)]

    cnt = {"slab": 0, "pmm": 0, "ptr": 0, "stg": 0, "x": 0, "pg": 0}
    stg = [tmp[:, :], tmp2[:, :], xt[0][:, :], ada[:, 0:1024]]
    stg_key = ["tmp", "tmp2", ("x", 0), "ada"]
    stgb = [hbf[:, :], mT[:].rearrange("p a b -> p (a b)"), aT[:, 0:8, :].rearrange("p a b -> p (a b)"), qT[:].rearrange("p a b -> p (a b)")]
    stgb_key = ["hbf", "mT", "aT", "qT"]
    ob = hbf[:, 0:NH * VH]

    def nxt(kind, n):
        i = cnt[kind] % n
        cnt[kind] += 1
        return i

    def dma(out, in_, r=(), w=(), nonc=False):
        if nonc:
            tr.op("sp", lambda h: h.dma_start(out=out, in_=in_, allow_slow_non_contiguous=True), r=r, w=w, dma=True)
        else:
            tr.op("sp", lambda h: h.dma_start(out=out, in_=in_), r=r, w=w, dma=True)

    try:
        dma(ident[:], ident_d, w=["ident"]); dma(maskc[:], maskc_d, w=["maskc"]); dma(mbig[:], mbig_d, w=["mbig"])
        dma(qgbc[:], qg.partition_broadcast(128), w=["qgbc"]); dma(kvgbc[:], kvg.partition_broadcast(128), w=["kvgbc"])
        for k in range(3):
            dma(convw[:, :, k], convw_d[k, :].rearrange("(c p) -> p c", p=128), w=["convw"], nonc=True)
        dma(cosp[:], cosp_d.rearrange("(n p) f -> p n f", p=128), w=["rope"]); dma(sinp[:], sinp_d.rearrange("(n p) f -> p n f", p=128), w=["rope2"])
        dma(coss[:], coss_d, w=["ropes"]); dma(sins[:], sins_d, w=["ropes2"])

        ck("constants")
        def convert(name):
            for (c0, c1, k0, k1) in SEG[name]:
                n = c1 - c0
                off = SEGOFF[(name, c0, c1, k0, k1)]
                G = max(1, 1024 // n)
                for ka in range(k0, k1, G):
                    kb = min(k1, ka + G)
                    g = kb - ka
                    i = nxt("stg", 4)
                    dma(stg[i][:, 0:g * n].rearrange("p (k n) -> p k n", n=n), wd[name][ka * 128:kb * 128, c0:c1].rearrange("(k p) n -> p k n", p=128), w=[stg_key[i]])
                    if i % 2:
                        tr.op("dve", lambda h, i=i, m=g * n: h.tensor_copy(out=stgb[i][:, 0:m], in_=stg[i][:, 0:m]), r=[stg_key[i]], w=[stgb_key[i]])
                    else:
                        tr.op("act", lambda h, i=i, m=g * n: h.copy(out=stgb[i][:, 0:m], in_=stg[i][:, 0:m]), r=[stg_key[i]], w=[stgb_key[i]])
                    o = off + (ka - k0) * n
                    dma(wb[name][:, o:o + g * n], stgb[i][:, 0:g * n], r=[stgb_key[i]], w=[("wb", name)])

        for name in WSPEC:
            convert(name)

        ck("conversions")
        def slab(name, c0, c1, k0=0, k1=None):
            K, N = WSPEC[name]
            if k1 is None:
                k1 = K // 128
            KC = k1 - k0
            n = c1 - c0
            assert KC * n <= 8 * 512
            i = nxt("slab", 3); buf = slabs[i]; key = ("slab", i)
            view = buf[:, 0:KC * n].rearrange("p (k n) -> p k n", n=n)
            off = SEGOFF[(name, c0, c1, k0, k1)]
            dma(buf[:, 0:KC * n], wb[name][:, off:off + KC * n], r=[("wb", name)], w=[key])
            return view, key

        def mm(out, lhsT, rhs, start, stop, r, w):
            tr.op("pe", lambda h: h.matmul(out, lhsT, rhs, start=start, stop=stop), r=r, w=w)

        def transposes(src_ap_fn, nchunk, dst, dst_key, src_key, rows=128, cols=128):
            i = nxt("ptr", 2)
            for j in range(nchunk):
                tr.op("pe", lambda h, j=j, i=i: h.transpose(ptr[i][0:cols, j * 128:j * 128 + rows], src_ap_fn(j), ident[0:rows, 0:rows]),
                      r=[src_key, "ident"], w=[("ptr", i)])
            tr.op("dve", lambda h, i=i: h.tensor_copy(out=dst[0:cols, 0:nchunk, 0:rows],
                                                     in_=ptr[i][0:cols, 0:nchunk * 128].rearrange("p (j t) -> p j t", t=128)[:, :, 0:rows]),
                  r=[("ptr", i)], w=[dst_key])

        wukc = hbf[:, :].rearrange("p (k n) -> p k n", k=2)
        for kc in range(2):
            src = wb["w_ukv"][:, kc * NH * 128:(kc + 1) * NH * 128].rearrange("p (h e) -> p h e", e=128)
            dma(wuvb[:, kc, :].rearrange("p (h e) -> p h e", e=VH), src[:, :, QKN:128], r=[("wb", "w_ukv")], w=["wuvb"])
            dma(wukc[:, kc, :].rearrange("p (h e) -> p h e", e=QKN), src[:, :, 0:QKN], r=[("wb", "w_ukv")], w=["hbf"])
        for kc in range(2):
            i = nxt("ptr", 2)
            for pr in range(4):
                tr.op("pe", lambda h, kc=kc, pr=pr, i=i: h.transpose(ptr[i][:, pr * 128:(pr + 1) * 128], wukc[:, kc, pr * 128:(pr + 1) * 128], ident[:]),
                      r=["hbf", "ident"], w=[("ptr", i)])
            tr.op("dve", lambda h, kc=kc, i=i: h.tensor_copy(out=wukT[:, :, kc * 128:(kc + 1) * 128],
                                                            in_=ptr[i][:, 0:512].rearrange("p (j t) -> p j t", t=128)),
                  r=[("ptr", i)], w=["wukT"])

        ck("wuk setup")
        def compute_ada(c_src, nseq, rep):
            for kc in range(8):
                dma(cT[:, kc, 0:nseq], c_src[:, kc * 128:(kc + 1) * 128].rearrange("s p -> p s"), w=[("cT", kc)], nonc=True)
            tr.op("act", lambda h: h.activation(out=cTs[:, :, 0:nseq], in_=cT[:, :, 0:nseq], func=AF.Silu), r=[("cT", k) for k in range(8)], w=["cTs"])
            for kc in range(8):
                tr.op("dve", lambda h, kc=kc: h.tensor_copy(out=cTexp[:, kc, :].rearrange("p (s r) -> p s r", r=rep),
                                                             in_=cTs[:, kc, 0:nseq].unsqueeze(2).to_broadcast([128, nseq, rep])),
                      r=["cTs"], w=["cTexp"])
            dma(ada[:], b_ada.partition_broadcast(128), w=["ada"])
            for c in range(12):
                sl, key = slab("w_ada", c * 512, (c + 1) * 512)
                i = nxt("pmm", 2)
                for kc in range(8):
                    mm(pmm[i][:, :], cTexp[:, kc, :], sl[:, kc, :], kc == 0, kc == 7, r=["cTexp", key], w=[("pmm", i)])
                tr.op("dve", lambda h, c=c, i=i: h.tensor_tensor(out=ada[:, c * 512:(c + 1) * 512], in0=pmm[i][:, :], in1=ada[:, c * 512:(c + 1) * 512], op=ALU.add),
                      r=[("pmm", i), "ada"], w=["ada"])
            for off in (1 * D, 4 * D):
                tr.op("dve", lambda h, off=off: h.tensor_scalar_add(out=ada[:, off:off + D], in0=ada[:, off:off + D], scalar1=1.0), r=["ada"], w=["ada"])

        def layer_norm(src, gi, dst, skey, dkey):
            dma(lnbc[:, 0, :], lnd[gi].partition_broadcast(128), w=[("lnbc", 0)])
            dma(lnbc[:, 1, :], lnd[gi + 1].partition_broadcast(128), w=[("lnbc", 1)])
            gi = 0
            tr.op("dve", lambda h: h.reduce_sum(out=st8[:, 0:1], in_=src, axis=AX.X), r=[skey], w=["st8"])
            tr.op("act", lambda h: h.activation(out=tmp2[:], in_=src, func=AF.Square, accum_out=st8[:, 1:2]), r=[skey, "st8"], w=["tmp2", "st8"])
            tr.op("dve", lambda h: h.tensor_scalar(out=st8[:, 2:4], in0=st8[:, 0:2], scalar1=1.0 / D, scalar2=None, op0=ALU.mult), r=["st8"], w=["st8"])
            tr.op("dve", lambda h: h.tensor_tensor(out=st8[:, 4:5], in0=st8[:, 2:3], in1=st8[:, 2:3], op=ALU.mult), r=["st8"], w=["st8"])
            tr.op("dve", lambda h: h.tensor_tensor(out=st8[:, 5:6], in0=st8[:, 3:4], in1=st8[:, 4:5], op=ALU.subtract), r=["st8"], w=["st8"])
            tr.op("dve", lambda h: h.tensor_scalar_add(out=st8[:, 5:6], in0=st8[:, 5:6], scalar1=LN_EPS), r=["st8"], w=["st8"])
            tr.op("act", lambda h: h.activation(out=st8[:, 7:8], in_=st8[:, 5:6], func=AF.Sqrt), r=["st8"], w=["st8"])
            tr.op("dve", lambda h: h.reciprocal(out=st8[:, 6:7], in_=st8[:, 7:8]), r=["st8"], w=["st8"])
            tr.op("dve", lambda h: h.tensor_scalar(out=dst, in0=src, scalar1=st8[:, 2:3], scalar2=st8[:, 6:7], op0=ALU.subtract, op1=ALU.mult), r=[skey, "st8"], w=[dkey])
            tr.op("dve", lambda h: h.tensor_tensor(out=dst, in0=dst, in1=lnbc[:, gi, :], op=ALU.mult), r=[dkey, ("lnbc", gi)], w=[dkey])
            tr.op("dve", lambda h: h.tensor_tensor(out=dst, in0=dst, in1=lnbc[:, gi + 1, :], op=ALU.add), r=[dkey, ("lnbc", gi + 1)], w=[dkey])

        def rms_norm(src, n, gbc, gkey, dst, skey, dkey, dst_bf=None, bkey=None):
            tr.op("act", lambda h: h.activation(out=tmp2[:, 0:n], in_=src, func=AF.Square, accum_out=st8[:, 8:9]), r=[skey, "st8"], w=["tmp2", "st8"])
            tr.op("dve", lambda h: h.tensor_scalar(out=st8[:, 9:10], in0=st8[:, 8:9], scalar1=1.0 / n, scalar2=RMS_EPS, op0=ALU.mult, op1=ALU.add), r=["st8"], w=["st8"])
            tr.op("act", lambda h: h.activation(out=st8[:, 9:10], in_=st8[:, 9:10], func=AF.Sqrt), r=["st8"], w=["st8"])
            tr.op("dve", lambda h: h.reciprocal(out=st8[:, 10:11], in_=st8[:, 9:10]), r=["st8"], w=["st8"])
            tr.op("dve", lambda h: h.tensor_scalar(out=dst, in0=src, scalar1=st8[:, 10:11], scalar2=None, op0=ALU.mult), r=[skey, "st8"], w=[dkey])
            tr.op("dve", lambda h: h.tensor_tensor(out=dst, in0=dst, in1=gbc, op=ALU.mult), r=[dkey, gkey], w=[dkey])
            if dst_bf is not None:
                tr.op("act", lambda h: h.copy(out=dst_bf, in_=dst), r=[dkey], w=[bkey])

        def rope(src, dst, cos, sin, nh, skey, dkey, ckeys):
            cb = cos.unsqueeze(1).to_broadcast([128, nh, 16]) if nh > 1 else cos
            sbc = sin.unsqueeze(1).to_broadcast([128, nh, 16]) if nh > 1 else sin
            if nh > 1:
                x1_, x2_ = src[:, :, 0:16], src[:, :, 16:32]
                d1, d2 = dst[:, :, 0:16], dst[:, :, 16:32]
                t1 = tmp2[:, 0:nh * 16].rearrange("p (h f) -> p h f", f=16)
                t2 = tmp2[:, 256:256 + nh * 16].rearrange("p (h f) -> p h f", f=16)
            else:
                x1_, x2_ = src[:, 0:16], src[:, 16:32]
                d1, d2 = dst[:, 0:16], dst[:, 16:32]
                t1 = tmp2[:, 0:16]
                t2 = tmp2[:, 256:272]
            rk = [skey] + list(ckeys)
            tr.op("dve", lambda h: h.tensor_tensor(out=t1, in0=x1_, in1=cb, op=ALU.mult), r=rk, w=["tmp2"])
            tr.op("dve", lambda h: h.tensor_tensor(out=t2, in0=x2_, in1=sbc, op=ALU.mult), r=rk, w=["tmp2"])
            tr.op("dve", lambda h: h.tensor_tensor(out=d1, in0=t1, in1=t2, op=ALU.subtract), r=["tmp2"], w=[dkey])
            tr.op("dve", lambda h: h.tensor_tensor(out=t1, in0=x1_, in1=sbc, op=ALU.mult), r=rk, w=["tmp2"])
            tr.op("dve", lambda h: h.tensor_tensor(out=t2, in0=x2_, in1=cb, op=ALU.mult), r=rk, w=["tmp2"])
            tr.op("dve", lambda h: h.tensor_tensor(out=d2, in0=t1, in1=t2, op=ALU.add), r=["tmp2"], w=[dkey])

        def tile(x_src, y_dst, lat_dst, kr_dst, cos, sin, ckeys, prompt, ti=0, conv_dst=None, first=False, last=False):
            xi = 0
            X = xt[xi]
            xk = ("x", xi)
            dma(X[:], x_src, w=[xk])
            tr.op("dve", lambda h: h.tensor_tensor(out=tmp[:], in0=X[:], in1=ada[:, D:2 * D], op=ALU.mult), r=[xk, "ada"], w=["tmp"])
            tr.op("dve", lambda h: h.tensor_tensor(out=hbf[:], in0=tmp[:], in1=ada[:, 0:D], op=ALU.add), r=["tmp", "ada"], w=["hbf"])
            transposes(lambda j: hbf[:, j * 128:(j + 1) * 128], 8, hT, "hT", "hbf")
            ck("tile: h transposes")
            for (c0, c1) in ((0, 512), (512, 672)):
                sl, key = slab("w_in", c0, c1)
                i = nxt("pmm", 2)
                for kc in range(8):
                    mm(pmm[i][:, 0:c1 - c0], hT[:, kc, :], sl[:, kc, :], kc == 0, kc == 7, r=["hT", key], w=[("pmm", i)])
                tr.op("act", lambda h, i=i, c0=c0, c1=c1: h.copy(out=z[:, c0:c1], in_=pmm[i][:, 0:c1 - c0]), r=[("pmm", i)], w=["z"])
            ck("tile: z")
            rms_norm(z[:, 0:QL], QL, qgbc[:], "qgbc", tmp[:, 0:QL], "z", "tmp", qn[:], "qn")
            ck("q: rms")
            transposes(lambda j: qn[:, j * 128:(j + 1) * 128], 3, qnT, "qnT", "qn")
            ck("q: qnT")
            for (c0, c1) in ((0, 384), (384, 768)):
                sl, key = slab("w_uq", c0, c1)
                i = nxt("pmm", 2)
                for kc in range(3):
                    mm(pmm[i][:, 0:384], qnT[:, kc, :], sl[:, kc, :], kc == 0, kc == 2, r=["qnT", key], w=[("pmm", i)])
                ck("q: mm%d" % c0)
                pq = pmm[i][:, 0:384].rearrange("p (h e) -> p h e", e=QKH)
                tr.op("act", lambda h, pq=pq, c0=c0: h.copy(out=q[:, c0 // QKH:c0 // QKH + 4, :], in_=pq[:, :, QKN:QKH]), r=[("pmm", i)], w=["q"])
                ck("q: evA%d" % c0)
                tr.op("act", lambda h, pq=pq, c0=c0: h.copy(out=qb[:, c0 // QKH:c0 // QKH + 4, 0:QKN], in_=pq[:, :, 0:QKN]), r=[("pmm", i)], w=["qb"])
                ck("q: evB%d" % c0)
            ck("q: wuq")
            rope(q[:, :, :], qb[:, :, QKN:QKH], cos, sin, NH, "q", "qb", ckeys)
            ck("tile: q path")
            rms_norm(z[:, QL:QL + KVL], KVL, kvgbc[:], "kvgbc", ckv[:], "z", "ckv", ckvb[:], "ckvb")
            dma(lat_dst, ckv[:], r=["ckv"])
            rope(z[:, QL + KVL:672], krt[:], cos, sin, 1, "z", "krt", ckeys)
            dma(kr_dst, krt[:], r=["krt"])
            transposes(lambda j: ckvb[:, j * 128:(j + 1) * 128], 2, ckvT, "ckvT", "ckvb")
            ck("tile: kv path")
            slc_, keyc = slab("w_in", 672 + CD, 672 + 2 * CD)
            slv_, keyv = slab("w_in", 672 + 2 * CD, 672 + 3 * CD)
            for j in range(4):
                ic = nxt("pmm", 2)
                for kc in range(8):
                    mm(pmm[ic][:, 0:128], slc_[:, kc, j * 128:(j + 1) * 128], hT[:, kc, :], kc == 0, kc == 7, r=["hT", keyc], w=[("pmm", ic)])
                tr.op("act", lambda h, j=j, ic=ic: h.copy(out=cgs[:, j, :], in_=pmm[ic][:, 0:128]), r=[("pmm", ic)], w=["cgs"])
                iv = nxt("pmm", 2)
                for kc in range(8):
                    mm(pmm[iv][:, 0:128], slv_[:, kc, j * 128:(j + 1) * 128], hT[:, kc, :], kc == 0, kc == 7, r=["hT", keyv], w=[("pmm", iv)])
                if prompt:
                    uflat = uT[:].rearrange("p c s t -> p c (s t)")
                    tr.op("dve", lambda h, j=j, iv=iv: h.tensor_tensor(out=uflat[:, j, 2:130], in0=cgs[:, j, :], in1=pmm[iv][:, 0:128], op=ALU.mult), r=["cgs", ("pmm", iv)], w=["uT"])
                else:
                    tr.op("dve", lambda h, j=j, iv=iv: h.tensor_tensor(out=uT[:, j, :, 2:2 + T], in0=cgs[:, j, :].rearrange("p (s t) -> p s t", t=T),
                                                                      in1=pmm[iv][:, 0:128].rearrange("p (s t) -> p s t", t=T), op=ALU.mult), r=["cgs", ("pmm", iv)], w=["uT"])
            sl, key = slab("w_in", 672, 672 + CD)
            for j in range(4):
                ib = nxt("pmm", 2)
                for kc in range(8):
                    mm(pmm[ib][:, 0:128], sl[:, kc, j * 128:(j + 1) * 128], hT[:, kc, :], kc == 0, kc == 7, r=["hT", key], w=[("pmm", ib)])
                if prompt:
                    uflat = uT[:].rearrange("p c s t -> p c (s t)")
                    u0, u1, u2 = uflat[:, j, 0:128], uflat[:, j, 1:129], uflat[:, j, 2:130]
                    cv = cvT[:, j, :]
                    bsrc = pmm[ib][:, 0:128]
                    cbo = cbT[:, j, :]
                else:
                    u0, u1, u2 = uT[:, j, :, 0:T], uT[:, j, :, 1:1 + T], uT[:, j, :, 2:2 + T]
                    cv = cvT[:, j, :].rearrange("p (s t) -> p s t", t=T)
                    bsrc = pmm[ib][:, 0:128].rearrange("p (s t) -> p s t", t=T)
                    cbo = cbT[:, j, :].rearrange("p (s t) -> p s t", t=T)
                tr.op("dve", lambda h, j=j, u0=u0, cv=cv: h.tensor_scalar(out=cv, in0=u0, scalar1=convw[:, j, 0:1], scalar2=None, op0=ALU.mult), r=["uT", "convw"], w=["cvT"])
                tr.op("dve", lambda h, j=j, u1=u1, cv=cv: h.scalar_tensor_tensor(out=cv, in0=u1, scalar=convw[:, j, 1:2], in1=cv, op0=ALU.mult, op1=ALU.add), r=["uT", "convw", "cvT"], w=["cvT"])
                tr.op("dve", lambda h, j=j, u2=u2, cv=cv: h.scalar_tensor_tensor(out=cv, in0=u2, scalar=convw[:, j, 2:3], in1=cv, op0=ALU.mult, op1=ALU.add), r=["uT", "convw", "cvT"], w=["cvT"])
                tr.op("dve", lambda h, cv=cv, bsrc=bsrc, cbo=cbo: h.tensor_tensor(out=cbo, in0=cv, in1=bsrc, op=ALU.mult), r=["cvT", ("pmm", ib)], w=["cbT"])
            if prompt:
                uflat = uT[:].rearrange("p c s t -> p c (s t)")
                if last:
                    for c in range(4):
                        dma(conv_dst[:, c * 128:(c + 1) * 128].rearrange("t p -> p t"), uflat[:, c, 128:130], r=["uT"], nonc=True)
                tr.op("pool", lambda h: h.tensor_copy(out=uflat[:, :, 0:2], in_=uflat[:, :, 128:130]), r=["uT"], w=["uT"])
            else:
                for c in range(4):
                    for t2 in range(2):
                        dma(conv_dst[:, c * 128:(c + 1) * 128].rearrange("(s t) p -> p s t", t=2)[:, :, t2], uT[:, c, :, T + t2], r=["uT"], nonc=True)
            ck("tile: conv")
            if prompt:
                prompt_attention(ti)
            else:
                sample_attention()
            ck("tile: attention")
            for c in range(2):
                for gi, gdst, gkey in ((0, sga, "sga"), (1, sgc, "sgc")):
                    c0 = 672 + 3 * CD + gi * D + c * 512
                    sl2, key2 = slab("w_in", c0, c0 + 512)
                    i = nxt("pmm", 2)
                    for kc in range(8):
                        mm(pmm[i][:, :], hT[:, kc, :], sl2[:, kc, :], kc == 0, kc == 7, r=["hT", key2], w=[("pmm", i)])
                    tr.op("act", lambda h, i=i, gdst=gdst: h.activation(out=gdst[:, :], in_=pmm[i][:, :], func=AF.Sigmoid), r=[("pmm", i)], w=[gkey])
                sla, ka = slab("w_oa", c * 512, (c + 1) * 512)
                ia = nxt("pmm", 2)
                for kc in range(4):
                    mm(pmm[ia][:, :], oT[:, kc, :], sla[:, kc, :], kc == 0, kc == 3, r=["oT", ka], w=[("pmm", ia)])
                tr.op("dve", lambda h, c=c, ia=ia: h.tensor_tensor(out=tmp[:, c * 512:(c + 1) * 512], in0=pmm[ia][:, :], in1=sga[:, :], op=ALU.mult), r=[("pmm", ia), "sga"], w=["tmp"])
                slc, kc_ = slab("w_oc", c * 512, (c + 1) * 512)
                ic = nxt("pmm", 2)
                for kc in range(4):
                    mm(pmm[ic][:, :], cbT[:, kc, :], slc[:, kc, :], kc == 0, kc == 3, r=["cbT", kc_], w=[("pmm", ic)])
                tr.op("dve", lambda h, c=c, ic=ic: h.tensor_tensor(out=tmp2[:, c * 512:(c + 1) * 512], in0=pmm[ic][:, :], in1=sgc[:, :], op=ALU.mult), r=[("pmm", ic), "sgc"], w=["tmp2"])
            tr.op("dve", lambda h: h.tensor_tensor(out=hbf[:], in0=tmp[:], in1=tmp2[:], op=ALU.add), r=["tmp", "tmp2"], w=["hbf"])
            transposes(lambda j: hbf[:, j * 128:(j + 1) * 128], 8, mT, "mT", "hbf")
            ck("tile: m")
            for c in range(2):
                sl3, k3 = slab("w_o", c * 512, (c + 1) * 512)
                i = nxt("pmm", 2)
                for kc in range(8):
                    mm(pmm[i][:, :], mT[:, kc, :], sl3[:, kc, :], kc == 0, kc == 7, r=["mT", k3], w=[("pmm", i)])
                tr.op("dve", lambda h, c=c, i=i: h.tensor_tensor(out=tmp[:, c * 512:(c + 1) * 512], in0=pmm[i][:, :], in1=ada[:, 2 * D + c * 512:2 * D + (c + 1) * 512], op=ALU.mult), r=[("pmm", i), "ada"], w=["tmp"])
            tr.op("dve", lambda h: h.scalar_tensor_tensor(out=tmp[:], in0=X[:], scalar=ALPHA, in1=tmp[:], op0=ALU.mult, op1=ALU.add), r=[xk, "tmp"], w=["tmp"])
            layer_norm(tmp[:], 0, X[:], "tmp", xk)
            ck("tile: LN1")
            tr.op("dve", lambda h: h.tensor_tensor(out=tmp[:], in0=X[:], in1=ada[:, 4 * D:5 * D], op=ALU.mult), r=[xk, "ada"], w=["tmp"])
            tr.op("dve", lambda h: h.tensor_tensor(out=hbf[:], in0=tmp[:], in1=ada[:, 3 * D:4 * D], op=ALU.add), r=["tmp", "ada"], w=["hbf"])
            transposes(lambda j: hbf[:, j * 128:(j + 1) * 128], 8, hT, "hT", "hbf")
            ck("tile: h2")
            chunks = [(c0, min(DFF, c0 + 512)) for c0 in range(0, DFF, 512)]
            pend = {}

            def ffn_mm(c0, c1):
                n = c1 - c0
                s1, k1 = slab("w_ff1", c0, c1)
                i1 = nxt("pmm", 2)
                for kc in range(8):
                    mm(pmm[i1][:, 0:n], hT[:, kc, :], s1[:, kc, :], kc == 0, kc == 7, r=["hT", k1], w=[("pmm", i1)])
                tr.op("act", lambda h: h.activation(out=tmp[:, 0:n], in_=pmm[i1][:, 0:n], func=AF.Silu), r=[("pmm", i1)], w=["tmp"])
                s3, k3 = slab("w_ff3", c0, c1)
                i3 = nxt("pmm", 2)
                for kc in range(8):
                    mm(pmm[i3][:, 0:n], hT[:, kc, :], s3[:, kc, :], kc == 0, kc == 7, r=["hT", k3], w=[("pmm", i3)])
                pend[c0] = i3

            def ffn_mult(c0, c1):
                n = c1 - c0
                i3 = pend.pop(c0)
                tr.op("dve", lambda h: h.tensor_tensor(out=ab[:, 0:n], in0=tmp[:, 0:n], in1=pmm[i3][:, 0:n], op=ALU.mult), r=["tmp", ("pmm", i3)], w=["ab"])

            def ffn_tr(c0, c1):
                n = c1 - c0
                g = c0 // 128
                transposes(lambda j: ab[:, j * 128:(j + 1) * 128], n // 128, aT[:, g:g + n // 128, :], "aT", "ab")

            ffn_mm(*chunks[0])
            ffn_mult(*chunks[0])
            for ci in range(len(chunks)):
                if ci + 1 < len(chunks):
                    ffn_mm(*chunks[ci + 1])
                ffn_tr(*chunks[ci])
                if ci + 1 < len(chunks):
                    ffn_mult(*chunks[ci + 1])
            for c in range(2):
                i = nxt("pmm", 2)
                for (ka, kb) in ((0, 8), (8, 16), (16, 22)):
                    s2, k2 = slab("w_ff2", c * 512, (c + 1) * 512, ka, kb)
                    for kc in range(ka, kb):
                        mm(pmm[i][:, :], aT[:, kc, :], s2[:, kc - ka, :], kc == 0, kc == 21, r=["aT", k2], w=[("pmm", i)])
                tr.op("dve", lambda h, c=c, i=i: h.tensor_tensor(out=tmp[:, c * 512:(c + 1) * 512], in0=pmm[i][:, :], in1=ada[:, 5 * D + c * 512:5 * D + (c + 1) * 512], op=ALU.mult), r=[("pmm", i), "ada"], w=["tmp"])
            tr.op("dve", lambda h: h.scalar_tensor_tensor(out=tmp[:], in0=X[:], scalar=ALPHA, in1=tmp[:], op0=ALU.mult, op1=ALU.add), r=[xk, "tmp"], w=["tmp"])
            ck("tile: FFN")
            layer_norm(tmp[:], 2, X[:], "tmp", xk)
            dma(y_dst, X[:], r=[xk])

        def prompt_attention(ti):
            slk, kk_ = slab("w_ukv", 0, NH * 128)
            for half in range(2):
                i = nxt("pmm", 2)
                for kc in range(2):
                    mm(pmm[i][:, :], ckvT[:, kc, :], slk[:, kc, half * 512:(half + 1) * 512], kc == 0, kc == 1, r=["ckvT", kk_], w=[("pmm", i)])
                pv = pmm[i][:, :].rearrange("p (h e) -> p h e", e=128)
                tr.op("act", lambda h, half=half, pv=pv: h.copy(out=kk[:, half * 4:(half + 1) * 4, 0:QKN], in_=pv[:, :, 0:QKN]), r=[("pmm", i)], w=["kk"])
                tr.op("act", lambda h, half=half, pv=pv: h.copy(out=Vt[:, ti, half * 256:(half + 1) * 256].rearrange("p (h e) -> p h e", e=VH), in_=pv[:, :, QKN:128]), r=[("pmm", i)], w=[("Vt", ti)])
            tr.op("pool", lambda h: h.tensor_copy(out=krb[:], in_=krt[:]), r=["krt"], w=["krb"])
            tr.op("pool", lambda h: h.tensor_copy(out=kk[:, :, QKN:QKH], in_=krb[:].unsqueeze(1).to_broadcast([128, NH, QKR])), r=["krb"], w=["kk"])
            i = nxt("ptr", 2)
            for hh in range(NH):
                tr.op("pe", lambda h, hh=hh, i=i: h.transpose(ptr[i][0:QKH, hh * 128:(hh + 1) * 128], kk[:, hh, :], ident[:]), r=["kk", "ident"], w=[("ptr", i)])
            tr.op("dve", lambda h, i=i: h.tensor_copy(out=Kt[0:QKH, :, ti * 128:(ti + 1) * 128], in_=ptr[i][0:QKH, :].rearrange("p (j t) -> p j t", t=128)), r=[("ptr", i)], w=[("Kt", ti)])
            i = nxt("ptr", 2)
            for hh in range(NH):
                tr.op("pe", lambda h, hh=hh, i=i: h.transpose(ptr[i][0:QKH, hh * 128:(hh + 1) * 128], qb[:, hh, :], ident[:]), r=["qb", "ident"], w=[("ptr", i)])
            tr.op("dve", lambda h, i=i: h.tensor_copy(out=qT[0:QKH, :, :], in_=ptr[i][0:QKH, :].rearrange("p (j t) -> p j t", t=128)), r=[("ptr", i)], w=["qT"])
            nk = ti + 1
            kkeys = [("Kt", t) for t in range(nk)]
            vkeys = [("Vt", t) for t in range(nk)]
            nch = (nk + 3) // 4

            def head_front(hh):
                par = hh % 2
                sh_, shk = sth[par], ("sth", par)
                Pp = P[par]
                for ch in range(nch):
                    k0 = ch * 512
                    k1 = min(nk * 128, k0 + 512)
                    isl = (ch == nch - 1)
                    mm(psc[ch][:, 0:k1 - k0], qT[0:QKH, hh, :], Kt[0:QKH, hh, k0:k1], True, not isl, r=["qT"] + kkeys, w=[("psc", ch)])
                    if isl:
                        d0 = (nk - 1) * 128 - k0
                        mm(psc[ch][:, d0:d0 + 128], ident[:], maskc[:], False, True, r=["ident", "maskc"], w=[("psc", ch)])
                for ch in range(nch):
                    k0 = ch * 512
                    k1 = min(nk * 128, k0 + 512)
                    tr.op("dve", lambda h, ch=ch, n=k1 - k0: h.reduce_max(out=sh_[:, 1 + ch:2 + ch], in_=psc[ch][:, 0:n], axis=AX.X), r=[("psc", ch)], w=[shk])
                tr.op("dve", lambda h: h.reduce_max(out=sh_[:, 0:1], in_=sh_[:, 1:1 + nch], axis=AX.X), r=[shk], w=[shk])
                tr.op("dve", lambda h: h.tensor_scalar(out=sh_[:, 0:1], in0=sh_[:, 0:1], scalar1=-SCALE, scalar2=None, op0=ALU.mult), r=[shk], w=[shk])
                for ch in range(nch):
                    k0 = ch * 512
                    k1 = min(nk * 128, k0 + 512)
                    tr.op("act", lambda h, ch=ch, k0=k0, k1=k1: h.activation(out=Pp[:, k0:k1], in_=psc[ch][:, 0:k1 - k0], func=AF.Exp, bias=sh_[:, 0:1], scale=SCALE, accum_out=sh_[:, 1 + ch:2 + ch]),
                          r=[("psc", ch), shk], w=[("P", par, ch), shk])

            def head_back(hh):
                par = hh % 2
                sh_, shk = sth[par], ("sth", par)
                Pp = P[par]
                io = nxt("pmm", 2)
                for g in range(0, nk, 8):
                    ng = min(8, nk - g)
                    pkeys = [("P", par, c) for c in range(g // 4, (g + ng + 3) // 4)]
                    i = nxt("ptr", 2)
                    for j in range(ng):
                        tr.op("pe", lambda h, j=j, i=i, g=g: h.transpose(ptr[i][:, j * 128:(j + 1) * 128], Pp[:, (g + j) * 128:(g + j + 1) * 128], ident[:]),
                              r=pkeys + ["ident"], w=[("ptr", i)])
                    tr.op("dve", lambda h, i=i, ng=ng: h.tensor_copy(out=PT[:, 0:ng, :], in_=ptr[i][:, 0:ng * 128].rearrange("p (j t) -> p j t", t=128)), r=[("ptr", i)], w=["PT"])
                    for t in range(g, g + ng):
                        mm(pmm[io][:, 0:VH], PT[:, t - g, :], Vt[:, t, hh * VH:(hh + 1) * VH], t == 0, t == nk - 1, r=["PT"] + vkeys, w=[("pmm", io)])
                tr.op("dve", lambda h: h.reduce_sum(out=sh_[:, 5:6], in_=sh_[:, 1:1 + nch], axis=AX.X), r=[shk], w=[shk])
                tr.op("dve", lambda h: h.reciprocal(out=sh_[:, 6:7], in_=sh_[:, 5:6]), r=[shk], w=[shk])
                tr.op("act", lambda h, io=io: h.activation(out=ob[:, hh * VH:(hh + 1) * VH], in_=pmm[io][:, 0:VH], func=AF.Copy, scale=sh_[:, 6:7]), r=[("pmm", io), shk], w=["hbf"])

            head_front(0)
            for hh in range(NH):
                if hh + 1 < NH:
                    head_front(hh + 1)
                head_back(hh)
            transposes(lambda j: ob[:, j * 128:(j + 1) * 128], 4, oT, "oT", "hbf")

        def sample_attention():
            i = nxt("ptr", 2)
            qbf = qb[:]
            tr.op("dve", lambda h: h.tensor_copy(out=hbf[:, 0:512].rearrange("p (h e) -> p h e", e=QKN), in_=qb[:, :, 0:QKN]), r=["qb"], w=["hbf"])
            for pr in range(4):
                tr.op("pe", lambda h, pr=pr, i=i: h.transpose(ptr[i][:, pr * 128:(pr + 1) * 128], hbf[:, pr * 128:(pr + 1) * 128], ident[:]), r=["hbf", "ident"], w=[("ptr", i)])
            tr.op("dve", lambda h, i=i: h.tensor_copy(out=qnpT[:], in_=ptr[i][:, 0:512].rearrange("p (j t) -> p j t", t=128)), r=[("ptr", i)], w=["qnpT"])
            for kc in range(2):
                for hh in range(NH):
                    pr, off = hh // 2, (hh % 2) * 64
                    ia = nxt("pmm", 2)
                    mm(pmm[ia][:, 0:128], wukT[off:off + 64, pr, kc * 128:(kc + 1) * 128], qnpT[off:off + 64, pr, :], True, True, r=["wukT", "qnpT"], w=[("pmm", ia)])
                    tr.op("act", lambda h, kc=kc, hh=hh, ia=ia: h.copy(out=QaT[:, kc, :, hh * T:(hh + 1) * T], in_=pmm[ia][:, 0:128].rearrange("p (s t) -> p s t", t=T)), r=[("pmm", ia)], w=["QaT"])
            tr.op("dve", lambda h: h.tensor_copy(out=hbf[:, 512:768].rearrange("p (h e) -> p h e", e=QKR), in_=qb[:, :, QKN:QKH]), r=["qb"], w=["hbf"])
            i = nxt("ptr", 2)
            for hh in range(NH):
                tr.op("pe", lambda h, hh=hh, i=i: h.transpose(ptr[i][0:QKR, hh * 128:(hh + 1) * 128], hbf[:, 512 + hh * QKR:512 + (hh + 1) * QKR], ident[:]), r=["hbf", "ident"], w=[("ptr", i)])
            for hh in range(NH):
                tr.op("dve", lambda h, hh=hh, i=i: h.tensor_copy(out=QrT[:, :, hh * T:(hh + 1) * T], in_=ptr[i][0:QKR, hh * 128:(hh + 1) * 128].rearrange("p (s t) -> p s t", t=T)), r=[("ptr", i)], w=["QrT"])
            XTn = sb("XTn", [128, 3, 128], BF16)
            tr.op("pool", lambda h: h.tensor_copy(out=krb[:], in_=krt[:]), r=["krt"], w=["krb"])
            tr.op("pool", lambda h: h.tensor_copy(out=XTn[:, 0:2, :], in_=ckvT[:]), r=["ckvT"], w=["XTn"])
            i = nxt("ptr", 2)
            tr.op("pe", lambda h, i=i: h.transpose(ptr[i][0:QKR, 0:128], krb[:], ident[:]), r=["krb", "ident"], w=[("ptr", i)])
            tr.op("dve", lambda h, i=i: h.tensor_copy(out=XTn[0:QKR, 2, :], in_=ptr[i][0:QKR, 0:128]), r=[("ptr", i)], w=["XTn"])

            npg_cnt = [0]
            KPG = NPG

            def flash_front(par, s, xts, KP, mask_off=None):
                for gi, (xv, xk_) in enumerate(xts):
                    o = psc[par][0:64, gi * KP:(gi + 1) * KP]
                    lastmm = mask_off is None
                    mm(o, QaT[:, 0, s, :], xv[:, 0, 0:KP], True, False, r=["QaT", xk_], w=[("psc", par)])
                    mm(o, QaT[:, 1, s, :], xv[:, 1, 0:KP], False, False, r=["QaT", xk_], w=[("psc", par)])
                    mm(o, QrT[:, s, :], xv[0:QKR, 2, 0:KP], False, lastmm, r=["QrT", xk_], w=[("psc", par)])
                    if mask_off is not None:
                        mm(o, ident[0:64, 0:64], mbig[:, mask_off:mask_off + KP], False, True, r=["ident", "mbig"], w=[("psc", par)])

            def flash_back(par, g, xtoks, KP, first):
                W = g * KP
                sc = psc[par]
                pa = psc[2 + par]
                pk = ("psc", par)
                pak = ("psc", 2 + par)
                Psb, PTb = Ps[par], PTs[par]
                Pk, PTk = ("Ps", par), ("PTs", par)
                tr.op("dve", lambda h: h.reduce_max(out=sm[:, 1:2], in_=sc[0:64, 0:W], axis=AX.X), r=[pk], w=["sm"])
                if not first:
                    tr.op("dve", lambda h: h.tensor_tensor(out=sm[:, 1:2], in0=sm[:, 1:2], in1=sm[:, 0:1], op=ALU.max), r=["sm"], w=["sm"])
                    tr.op("dve", lambda h: h.tensor_tensor(out=sm[:, 6:7], in0=sm[:, 0:1], in1=sm[:, 1:2], op=ALU.subtract), r=["sm"], w=["sm"])
                    tr.op("act", lambda h: h.activation(out=sm[:, 3:4], in_=sm[:, 6:7], func=AF.Exp, scale=SCALE), r=["sm"], w=["sm"])
                tr.op("dve", lambda h: h.tensor_scalar(out=sm[:, 2:3], in0=sm[:, 1:2], scalar1=-SCALE, scalar2=None, op0=ALU.mult), r=["sm"], w=["sm"])
                tr.op("act", lambda h: h.activation(out=Psb[:, 0:W], in_=sc[0:64, 0:W], func=AF.Exp, bias=sm[:, 2:3], scale=SCALE, accum_out=sm[:, 5:6]), r=[pk, "sm"], w=[Pk, "sm"])
                tr.op("dve", lambda h: h.tensor_copy(out=sm[:, 0:1], in_=sm[:, 1:2]), r=["sm"], w=["sm"])
                if first:
                    tr.op("dve", lambda h: h.tensor_copy(out=sm[:, 4:5], in_=sm[:, 5:6]), r=["sm"], w=["sm"])
                else:
                    tr.op("dve", lambda h: h.scalar_tensor_tensor(out=sm[:, 4:5], in0=sm[:, 4:5], scalar=sm[:, 3:4], in1=sm[:, 5:6], op0=ALU.mult, op1=ALU.add), r=["sm"], w=["sm"])
                ip = nxt("ptr", 2)
                for gi in range(g):
                    tr.op("pe", lambda h, gi=gi, ip=ip: h.transpose(ptr[ip][0:KP, gi * 128:gi * 128 + 64], Psb[:, gi * KP:(gi + 1) * KP], ident[0:64, 0:64]), r=[Pk, "ident"], w=[("ptr", ip)])
                tr.op("dve", lambda h, ip=ip: h.tensor_copy(out=PTb[0:KP, 0:g, :], in_=ptr[ip][0:KP, 0:g * 128].rearrange("p (j t) -> p j t", t=128)[:, :, 0:64]), r=[("ptr", ip)], w=[PTk])
                for gi, (xa, xak) in enumerate(xtoks):
                    mm(pa[0:64, 0:KVL], PTb[0:KP, gi, :], xa, gi == 0, gi == g - 1, r=[PTk, xak], w=[pak])
                if first:
                    tr.op("dve", lambda h: h.tensor_copy(out=accs[:], in_=pa[0:64, 0:KVL]), r=[pak], w=["accs"])
                else:
                    tr.op("dve", lambda h: h.scalar_tensor_tensor(out=accs[:], in0=accs[:], scalar=sm[:, 3:4], in1=pa[0:64, 0:KVL], op0=ALU.mult, op1=ALU.add), r=["accs", "sm", pak], w=["accs"])

            def finalize(s):
                tr.op("dve", lambda h: h.reciprocal(out=sm[:, 7:8], in_=sm[:, 4:5]), r=["sm"], w=["sm"])
                tr.op("act", lambda h: h.activation(out=olat[:], in_=accs[:], func=AF.Copy, scale=sm[:, 7:8]), r=["accs", "sm"], w=["olat"])
                io = nxt("ptr", 2)
                for kc in range(2):
                    tr.op("pe", lambda h, kc=kc, io=io: h.transpose(ptr[io][:, kc * 128:kc * 128 + 64], olat[:, kc * 128:(kc + 1) * 128], ident[0:64, 0:64]), r=["olat", "ident"], w=[("ptr", io)])
                tr.op("dve", lambda h, io=io: h.tensor_copy(out=olT[:], in_=ptr[io][:, 0:256].rearrange("p (j t) -> p j t", t=128)[:, :, 0:64]), r=[("ptr", io)], w=["olT"])
                for pr in range(4):
                    ia = nxt("pmm", 2)
                    for sub in range(2):
                        hh = pr * 2 + sub
                        for kc in range(2):
                            mm(pmm[ia][sub * 64:(sub + 1) * 64, 0:T], wuvb[:, kc, hh * VH:(hh + 1) * VH], olT[:, kc, hh * T:(hh + 1) * T], kc == 0, kc == 1, r=["wuvb", "olT"], w=[("pmm", ia)])
                    tr.op("act", lambda h, pr=pr, ia=ia, s=s: h.copy(out=oT[:, pr, s * T:(s + 1) * T], in_=pmm[ia][:, 0:T]), r=[("pmm", ia)], w=["oT"])

            dma(ptT[0:KPG, :], pt_d.rearrange("s p -> p s"), w=["ptT"], nonc=True)
            groups = []
            for s in range(SB):
                groups.append(("new", s, None))
                for c in range(NCH):
                    groups.append(("page", s, c))
            state = {}

            issued = [0]
            NPGRP = SB * NCH

            def ensure_issued(upto):
                while issued[0] < min(upto, NPGRP):
                    m = issued[0]
                    issued[0] += 1
                    ms, mc, mk = m // NCH, m % NCH, m % 4
                    tr.op("dve", lambda h, ms=ms, mc=mc, mk=mk: h.tensor_scalar(out=idxr[mk][0:KPG, :], in0=ptT[0:KPG, ms:ms + 1], scalar1=NCH, scalar2=mc, op0=ALU.mult, op1=ALU.add),
                          r=["ptT"], w=[("idxr", mk)])
                    tr.op("pool", lambda h, mk=mk: h.indirect_dma_start(out=Xb[mk][0:KPG].rearrange("p t l -> p (t l)"), out_offset=None, in_=pool_d[:, :],
                                                                       in_offset=bass.IndirectOffsetOnAxis(ap=idxr[mk][0:KPG, 0:1], axis=0)),
                          r=[("idxr", mk)], w=[("Xb", mk)], dma=True)

            def front(gidx):
                kind, s, c = groups[gidx]
                par = gidx % 2
                if kind == "new":
                    flash_front(par, s, [(XTn, "XTn")], 128, mask_off=120 - s * T)
                    state[gidx] = (1, [(ckvb[:, :], "ckvb")], 128, True)
                    return
                n = npg_cnt[0]
                npg_cnt[0] += 1
                b = n % 4
                ensure_issued(n + 3)
                xts, xtoks = [], []
                XTp = XT[par]
                for t in range(TCH):
                    it = nxt("ptr", 2)
                    tr.op("pe", lambda h, t=t, it=it: h.transpose(ptr[it][:, 0:KPG], Xb[b][0:KPG, t, 0:128], ident[0:KPG, 0:KPG]), r=[("Xb", b), "ident"], w=[("ptr", it)])
                    tr.op("pe", lambda h, t=t, it=it: h.transpose(ptr[it][:, 128:128 + KPG], Xb[b][0:KPG, t, 128:256], ident[0:KPG, 0:KPG]), r=[("Xb", b), "ident"], w=[("ptr", it)])
                    tr.op("pe", lambda h, t=t, it=it: h.transpose(ptr[it][0:QKR, 256:256 + KPG], Xb[b][0:KPG, t, 256:LAT], ident[0:KPG, 0:KPG]), r=[("Xb", b), "ident"], w=[("ptr", it)])
                    tr.op("dve", lambda h, t=t, it=it: h.tensor_copy(out=XTp[:, 0:2, t, 0:KPG], in_=ptr[it][:, 0:256].rearrange("p (j t) -> p j t", t=128)[:, :, 0:KPG]), r=[("ptr", it)], w=[("XT", par, t)])
                    tr.op("act", lambda h, t=t, it=it: h.copy(out=XTp[0:QKR, 2, t, 0:KPG], in_=ptr[it][0:QKR, 256:256 + KPG]), r=[("ptr", it)], w=[("XT", par, t)])
                    xts.append((XTp[:, :, t, :], ("XT", par, t)))
                    xtoks.append((Xb[b][0:KPG, t, 0:KVL], ("Xb", b)))
                if KPG == 128:
                    o = psc[par][0:64, 0:TCH * 128]
                    xk_all = [("XT", par, t) for t in range(TCH)]
                    mm(o, QaT[:, 0, s, :], XTp[:, 0, :, :].rearrange("p t k -> p (t k)"), True, False, r=["QaT"] + xk_all, w=[("psc", par)])
                    mm(o, QaT[:, 1, s, :], XTp[:, 1, :, :].rearrange("p t k -> p (t k)"), False, False, r=["QaT"] + xk_all, w=[("psc", par)])
                    mm(o, QrT[:, s, :], XTp[0:QKR, 2, :, :].rearrange("p t k -> p (t k)"), False, True, r=["QrT"] + xk_all, w=[("psc", par)])
                else:
                    flash_front(par, s, xts, KPG)
                state[gidx] = (TCH, xtoks, KPG, False)

            def back(gidx):
                kind, s, c = groups[gidx]
                g, xtoks, KP, first = state.pop(gidx)
                flash_back(gidx % 2, g, xtoks, KP, first)
                if kind == "page" and c == NCH - 1:
                    finalize(s)

            front(0)
            for gidx in range(len(groups)):
                if gidx + 1 < len(groups):
                    front(gidx + 1)
                back(gidx)

        for b in range(PB):
            compute_ada(cp_d[b:b + 1, :], 1, 128)
            ck("ada")
            tr.op("pool", lambda h: h.memset(uT[:], 0.0), r=["uT"], w=["uT"])
            for ti in range(NTS):
                r0 = b * SEQ + ti * 128
                tile(xp[r0:r0 + 128, :], yp[r0:r0 + 128, :], latp[r0:r0 + 128, :], krp[r0:r0 + 128, :],
                     cosp[:, ti, :], sinp[:, ti, :], ["rope", "rope2"], True, ti=ti,
                     conv_dst=convp[b * 2:(b + 1) * 2, :], first=(ti == 0), last=(ti == NTS - 1))
        compute_ada(cs_d, SB, T)
        tr.op("pool", lambda h: h.memset(uT[:], 0.0), r=["uT"], w=["uT", "uTm"])
        for c in range(4):
            for t2 in range(2):
                dma(uT[:, c, :, t2], sconv[:, c * 128:(c + 1) * 128].rearrange("(s t) p -> p s t", t=2)[:, :, t2], r=["uTm"], w=["uT"], nonc=True)
        tile(xs, ys, lats, krs, coss[:], sins[:], ["ropes", "ropes2"], False, conv_dst=convs)

    except _Stop:
        pass
    sem_es = ExitStack()
    sems = {e: sem_es.enter_context(nc.semaphore("s_" + e)) for e in ("pe", "act", "dve", "pool")}
    ring = {q: [sem_es.enter_context(nc.semaphore(f"ring_{q}{i}")) for i in range(NRING)] for q in Tracker.DMAQ}
    tr.emit(nc, sems, ring, None)
    sem_es.close()
    es.close()
    return nc


def rope_tables(pos):
    inv = (10000.0 ** (-(np.arange(0, QKR, 2, dtype=np.float32)) / np.float32(QKR))).astype(np.float32)
    ang = (pos.astype(np.float32)[:, None] * inv[None, :]).astype(np.float32)
    return np.cos(ang).astype(np.float32), np.sin(ang).astype(np.float32)


_CACHE = {}


def run(inputs, SEQ, NPG, NPOOL, PAST):
    key = (SEQ, NPG, NPOOL)
    if key not in _CACHE:
        _CACHE[key] = build(SEQ, NPG, NPOOL)
    nc = _CACHE[key]
    f = lambda a: np.ascontiguousarray(np.asarray(a))
    pool = np.concatenate([np.asarray(inputs["cache_kv_latent"])[0], np.asarray(inputs["cache_k_rope"])[0]], axis=-1)
    pool = np.ascontiguousarray(pool, dtype=np.float32).reshape(NPOOL * NCH, TCH * LAT)
    cosp, sinp = rope_tables(np.arange(SEQ))
    cs_, ss_ = rope_tables(PAST + np.arange(T))
    coss = np.tile(cs_, (SB, 1)); sins = np.tile(ss_, (SB, 1))
    ident = np.eye(128, dtype=np.float32).astype(ml_dtypes.bfloat16)
    maskc = np.where(np.arange(128)[None, :] <= np.arange(128)[:, None], 0.0, NEG).astype(np.float32).astype(ml_dtypes.bfloat16)
    mbig = np.full((64, 248), NEG, np.float32)
    for r in range(64):
        t = r % T
        mbig[r, 120:120 + t + 1] = 0.0
    mbig = mbig.astype(ml_dtypes.bfloat16)
    common = {k: f(inputs[k])[0] for k in WSPEC if k != "w_ukv"}
    common["w_ukv"] = f(inputs["w_ukv"])[0].reshape(KVL, NH * 128)
    for k in ("b_ada", "q_norm_g", "kv_norm_g", "ln1_g", "ln1_b", "ln2_g", "ln2_b"):
        common[k] = f(inputs[k]).reshape(1, -1)
    common["conv_w"] = f(inputs["conv_w"])[0]
    common.update(cosp=cosp, sinp=sinp, coss=coss, sins=sins, ident=ident, maskc=maskc, mbig=mbig, pool=pool)
    xp = f(inputs["x_prompt"]); xs = f(inputs["x_sample"]); ptab = f(inputs["page_table"]).astype(np.int32)
    sc = f(inputs["state_conv"])[0]; cp = f(inputs["c_prompt"]); cs = f(inputs["c_sample"])
    in_maps = []
    for c in range(8):
        m = dict(common)
        m["xp"] = xp[c * PB:(c + 1) * PB].reshape(PB * SEQ, D)
        m["xs"] = xs[c * SB:(c + 1) * SB].reshape(SB * T, D)
        m["pt"] = np.ascontiguousarray(ptab[c * SB:(c + 1) * SB])
        m["sconv"] = np.ascontiguousarray(sc[c * SB:(c + 1) * SB].reshape(SB * 2, CD))
        m["cp"] = np.ascontiguousarray(cp[c * PB:(c + 1) * PB]); m["cs"] = np.ascontiguousarray(cs[c * SB:(c + 1) * SB])
        in_maps.append(m)
    import os as _os
    if _os.environ.get("KTRACE"):
        res = run_bass_kernel_spmd(nc, in_maps, core_ids=list(range(8)), trace=True)
        print("KTRACE exec_time_ns", res.exec_time_ns, flush=True)
    else:
        res = run_bass_kernel_spmd(nc, in_maps, core_ids=list(range(8)))
    R = res.results
    cat = lambda k: np.concatenate([np.asarray(R[c][k]) for c in range(8)], axis=0)
    B = 8 * PB
    y_p = cat("yp").reshape(B, SEQ, D); y_s = cat("ys").reshape(8 * SB, T, D)
    lat_p = cat("latp").reshape(1, B, SEQ, KVL); kr_p = cat("krp").reshape(1, B, SEQ, QKR); cv_p = cat("convp").reshape(1, B, 2, CD)
    lat_s = cat("lats").reshape(1, 8 * SB, T, KVL); kr_s = cat("krs").reshape(1, 8 * SB, T, QKR); cv_s = cat("convs").reshape(1, 8 * SB, 2, CD)
    return (y_p, y_s, lat_p, kr_p, cv_p, lat_s, kr_s, cv_s)


def kernel(**inputs):
    SEQ = inputs["x_prompt"].shape[1]
    NPG = inputs["page_table"].shape[1]
    NPOOL = inputs["cache_kv_latent"].shape[1]
    return run(inputs, SEQ, NPG, NPOOL, NPG * PAGE)
```

```python
import math
from contextlib import ExitStack
import numpy as np
import ml_dtypes
import concourse.bass as bass
import concourse.mybir as mybir
from concourse.bass_utils import run_bass_kernel_spmd

F32 = mybir.dt.float32
BF16 = mybir.dt.bfloat16
I32 = mybir.dt.int32
AF = mybir.ActivationFunctionType
ALU = mybir.AluOpType
AX = mybir.AxisListType

D = 1024
NH = 8
QKN, QKR, QKH, VH = 64, 32, 96, 64
QL, KVL = 384, 256
CD = 512
DFF = 2816
NIN = 4256
DEPTH = 1
ALPHA = (2.0 * DEPTH) ** 0.25
LN_EPS = 1e-5
RMS_EPS = 1e-6
SCALE = QKH ** -0.5
PB, SB, T = 2, 16, 8
PAGE = 128
LAT = KVL + QKR
NEG = -30000.0
TCH = 4
NCH = PAGE // TCH
SAFE_SAME = True
NRING = 16


class Tracker:
    ENG = ("pe", "act", "dve", "pool", "sp")

    def __init__(self):
        self.prog = {e: [] for e in self.ENG}
        self.last_w = {}
        self.readers = {}

    def op(self, eng, fn, r=(), w=(), dma=False):
        idx = len(self.prog[eng])
        deps = set()
        for k in r:
            deps.update(self.last_w.get(k, ()))
        raw = set(deps)
        for k in w:
            deps.update(self.last_w.get(k, ()))
            rd = self.readers.get(k)
            if rd:
                for e2, v in rd.items():
                    if isinstance(v, list):
                        deps.update(v)
                    else:
                        deps.add((e2, v))
        deps.discard((eng, idx))
        self.prog[eng].append(dict(fn=fn, deps=deps, dma=dma, raw=raw))
        for k in r:
            rd = self.readers.setdefault(k, {})
            if dma:
                rd.setdefault("dma", []).append((eng, idx))
            else:
                rd[eng] = idx
        for k in w:
            prev = self.last_w.get(k, [])
            if dma and prev and all(self.prog[e2][i2]["dma"] for (e2, i2) in prev) and not self.readers.get(k):
                self.last_w[k] = prev + [(eng, idx)]
            else:
                self.last_w[k] = [(eng, idx)]
            self.readers[k] = {}
        return (eng, idx)

    DMAQ = ("sp", "pool")

    def emit(self, nc, sems, ring, final_sem):
        prog = self.prog
        dma_no = {}
        ndma = {}
        for q in self.DMAQ:
            n = 0
            for i, ins in enumerate(prog[q]):
                if ins["dma"]:
                    dma_no[(q, i)] = n
                    n += 1
            ndma[q] = n

        def is_dma(e2, i2):
            return prog[e2][i2]["dma"]

        need = set()
        for e in self.ENG:
            for i, ins in enumerate(prog[e]):
                for (e2, i2) in ins["deps"]:
                    if is_dma(e2, i2):
                        continue
                    if e2 == e and not (SAFE_SAME and e in ("act", "dve", "pool") and (e2, i2) in ins["raw"]):
                        continue
                    need.add((e2, i2))
        val = {}
        for e in self.ENG:
            c = 0
            for i in range(len(prog[e])):
                if (e, i) in need:
                    c += 1
                    val[(e, i)] = c

        def run(e, h):
            waited = {}
            for i, ins in enumerate(prog[e]):
                waits = {}
                for (e2, i2) in ins["deps"]:
                    if is_dma(e2, i2):
                        d = dma_no[(e2, i2)]
                        key = ("ring", e2, d % NRING)
                        v = 16 * (d // NRING + 1)
                    else:
                        if (e2, i2) not in val:
                            continue
                        if e2 == e and (e2, i2) not in ins["raw"]:
                            continue
                        key = e2
                        v = val[(e2, i2)]
                    if waited.get(key, 0) >= v:
                        continue
                    waits[key] = max(waits.get(key, 0), v)
                if ins["dma"]:
                    d = dma_no[(e, i)]
                    if d >= NRING:
                        key = ("ring", e, d % NRING)
                        v = 16 * (d // NRING)
                        if waited.get(key, 0) < v:
                            waits[key] = max(waits.get(key, 0), v)
                for key, v in waits.items():
                    sem = ring[key[1]][key[2]] if isinstance(key, tuple) else sems[key]
                    h.wait_ge(sem, v)
                    waited[key] = v
                bi = ins["fn"](h)
                if ins["dma"]:
                    bi.then_inc(ring[e][dma_no[(e, i)] % NRING], 16)
                elif (e, i) in val:
                    bi.then_inc(sems[e], 1)
            if e in self.DMAQ:
                for sl in range(min(NRING, ndma[e])):
                    cntd = (ndma[e] - 1 - sl) // NRING + 1
                    h.wait_ge(ring[e][sl], 16 * cntd)

        with nc.Block() as block:
            @block.sync
            def _(h):
                run("sp", h)

            @block.tensor
            def _(h):
                run("pe", h)

            @block.scalar
            def _(h):
                run("act", h)

            @block.vector
            def _(h):
                run("dve", h)

            @block.gpsimd
            def _(h):
                run("pool", h)


WSPEC = {
    "w_ada": (D, 6 * D), "w_in": (D, NIN), "w_uq": (QL, NH * QKH), "w_ukv": (KVL, NH * 128),
    "w_oa": (NH * VH, D), "w_oc": (CD, D), "w_o": (D, D), "w_ff1": (D, DFF), "w_ff3": (D, DFF), "w_ff2": (DFF, D),
}


class _Stop(Exception):
    pass


def build(SEQ, NPG, NPOOL):
    import os
    dbg_stop = int(os.environ.get("KDBG", "0"))
    ckc = [0]

    def ck(name):
        ckc[0] += 1
        if (dbg_stop and ckc[0] == dbg_stop) or (os.environ.get("KDBG_NAME") == name):
            print("KDBG stop at checkpoint", ckc[0], name, flush=True)
            raise _Stop()
    NTS = SEQ // 128
    nc = bass.Bass("TRN2", target_bir_lowering=False)
    tr = Tracker()
    es = ExitStack()

    def din(name, shape, dt=F32):
        return nc.dram_tensor(name, list(shape), dt, kind="ExternalInput").ap()

    def dout(name, shape, dt=F32):
        return nc.dram_tensor(name, list(shape), dt, kind="ExternalOutput").ap()

    xp = din("xp", [PB * SEQ, D]); xs = din("xs", [128, D])
    pool_d = din("pool", [NPOOL * NCH, TCH * LAT]); pt_d = din("pt", [SB, NPG], I32)
    sconv = din("sconv", [SB * 2, CD]); cp_d = din("cp", [PB, D]); cs_d = din("cs", [SB, D])
    wd = {k: din(k, [v[0], v[1]]) for k, v in WSPEC.items()}
    b_ada = din("b_ada", [1, 6 * D]); qg = din("q_norm_g", [1, QL]); kvg = din("kv_norm_g", [1, KVL])
    convw_d = din("conv_w", [3, CD])
    lnd = [din(n, [1, D]) for n in ("ln1_g", "ln1_b", "ln2_g", "ln2_b")]
    cosp_d = din("cosp", [SEQ, 16]); sinp_d = din("sinp", [SEQ, 16])
    coss_d = din("coss", [128, 16]); sins_d = din("sins", [128, 16])
    ident_d = din("ident", [128, 128], BF16); maskc_d = din("maskc", [128, 128], BF16)
    mbig_d = din("mbig", [64, 248], BF16)

    yp = dout("yp", [PB * SEQ, D]); ys = dout("ys", [128, D])
    latp = dout("latp", [PB * SEQ, KVL]); krp = dout("krp", [PB * SEQ, QKR]); convp = dout("convp", [PB * 2, CD])
    lats = dout("lats", [128, KVL]); krs = dout("krs", [128, QKR]); convs = dout("convs", [SB * 2, CD])

    wb = {k: nc.dram_tensor(k + "_bf", [128, v[0] // 128, v[1]], BF16, kind="Internal").ap() for k, v in WSPEC.items()}

    def sb(name, shape, dt=F32):
        return es.enter_context(nc.sbuf_tensor("sb_" + name, list(shape), dt))

    def ps(name, shape, dt=F32):
        return es.enter_context(nc.psum_tensor("ps_" + name, list(shape), dt))

    ident = sb("ident", [128, 128], BF16); maskc = sb("maskc", [128, 128], BF16); mbig = sb("mbig", [64, 248], BF16)
    lnbc = sb("lnbc", [128, 2, D]); qgbc = sb("qgbc", [128, QL]); kvgbc = sb("kvgbc", [128, KVL])
    convw = sb("convw", [128, 4, 3])
    cosp = sb("cosp", [128, NTS, 16]); sinp = sb("sinp", [128, NTS, 16])
    coss = sb("coss", [128, 16]); sins = sb("sins", [128, 16])
    ada = sb("ada", [128, 6 * D])
    cT = sb("cT", [128, 8, SB]); cTs = sb("cTs", [128, 8, SB]); cTexp = sb("cTexp", [128, 8, 128], BF16)
    xt = [sb(f"xt{i}", [128, D]) for i in range(1)]
    hbf = sb("hbf", [128, D], BF16); hT = sb("hT", [128, 8, 128], BF16)
    tmp = sb("tmp", [128, D]); tmp2 = sb("tmp2", [128, D])
    z = sb("z", [128, 672])
    st8 = sb("st8", [128, 16]); sth = [sb(f"sth{i}", [128, 8]) for i in range(2)]
    qn = sb("qn", [128, QL], BF16); qnT = sb("qnT", [128, 3, 128], BF16)
    ckv = sb("ckv", [128, KVL]); ckvb = sb("ckvb", [128, KVL], BF16); ckvT = sb("ckvT", [128, 2, 128], BF16)
    krt = sb("krt", [128, QKR]); krb = sb("krb", [128, QKR], BF16)
    q = sb("q", [128, NH, QKR]); qb = sb("qb", [128, NH, QKH], BF16); qT = sb("qT", [128, NH, 128], BF16)
    kk = sb("kk", [128, NH, QKH], BF16)
    Kt = sb("Kt", [128, NH, SEQ], BF16); Vt = sb("Vt", [128, NTS, NH * VH], BF16)
    P = [sb(f"P{i}", [128, SEQ], BF16) for i in range(2)]; PT = sb("PT", [128, 8, 128], BF16)
    oT = sb("oT", [128, 4, 128], BF16)
    uT = sb("uT", [128, 4, SB, 2 + T]); cvT = sb("cvT", [128, 4, 128]); cbT = sb("cbT", [128, 4, 128], BF16)
    cgs = sb("cgs", [128, 4, 128])
    sga = sb("sga", [128, 512]); sgc = sb("sgc", [128, 512])
    mT = sb("mT", [128, 8, 128], BF16)
    ab = sb("ab", [128, 512], BF16); aT = sb("aT", [128, 22, 128], BF16)
    slabs = [sb(f"slab{i}", [128, 8 * 512], BF16) for i in range(3)]
    ptT = sb("ptT", [128, SB], I32)
    idxr = [sb(f"idxr{i}", [128, 1], I32) for i in range(4)]
    wukT = sb("wukT", [128, 4, KVL], BF16)
    wuvb = sb("wuvb", [128, 2, NH * VH], BF16)
    qnpT = sb("qnpT", [128, 4, 128], BF16)
    QaT = sb("QaT", [128, 2, SB, 64], BF16)
    QrT = sb("QrT", [32, SB, 64], BF16)
    Xb = [sb(f"Xb{i}", [128, TCH, LAT], BF16) for i in range(4)]
    XT = [sb(f"XT{i}", [128, 3, TCH, 128], BF16) for i in range(2)]
    Ps = [sb(f"Ps{i}", [64, 512], BF16) for i in range(2)]; PTs = [sb(f"PTs{i}", [128, 4, 64], BF16) for i in range(2)]
    accs = sb("accs", [64, KVL]); sm = sb("sm", [64, 16]); olat = sb("olat", [64, KVL], BF16); olT = sb("olT", [128, 2, 64], BF16)

    pmm = [ps(f"pmm{i}", [128, 512]) for i in range(2)]
    ptr = [ps(f"ptr{i}", [128, 1024], BF16) for i in range(2)]
    psc = [ps(f"psc{i}", [128, 512]) for i in range(4)]

    cnt = {"slab": 0, "pmm": 0, "ptr": 0, "stg": 0, "x": 0, "pg": 0}
    stg = [tmp[:, :], tmp2[:, :], xt[0][:, :], ada[:, 0:1024]]
    stg_key = ["tmp", "tmp2", ("x", 0), "ada"]
    stgb = [hbf[:, :], mT[:].rearrange("p a b -> p (a b)"), aT[:, 0:8, :].rearrange("p a b -> p (a b)"), qT[:].rearrange("p a b -> p (a b)")]
    stgb_key = ["hbf", "mT", "aT", "qT"]
    ob = hbf[:, 0:NH * VH]

    def nxt(kind, n):
        i = cnt[kind] % n
        cnt[kind] += 1
        return i

    def dma(out, in_, r=(), w=(), nonc=False):
        if nonc:
            tr.op("sp", lambda h: h.dma_start(out=out, in_=in_, allow_slow_non_contiguous=True), r=r, w=w, dma=True)
        else:
            tr.op("sp", lambda h: h.dma_start(out=out, in_=in_), r=r, w=w, dma=True)

    try:
        dma(ident[:], ident_d, w=["ident"]); dma(maskc[:], maskc_d, w=["maskc"]); dma(mbig[:], mbig_d, w=["mbig"])
        dma(qgbc[:], qg.partition_broadcast(128), w=["qgbc"]); dma(kvgbc[:], kvg.partition_broadcast(128), w=["kvgbc"])
        for k in range(3):
            dma(convw[:, :, k], convw_d[k, :].rearrange("(c p) -> p c", p=128), w=["convw"], nonc=True)
        dma(cosp[:], cosp_d.rearrange("(n p) f -> p n f", p=128), w=["rope"]); dma(sinp[:], sinp_d.rearrange("(n p) f -> p n f", p=128), w=["rope2"])
        dma(coss[:], coss_d, w=["ropes"]); dma(sins[:], sins_d, w=["ropes2"])

        ck("constants")
        def convert(name):
            K, N = WSPEC[name]
            for kc in range(K // 128):
                for c0 in range(0, N, 1024):
                    c1 = min(N, c0 + 1024)
                    i = nxt("stg", 4)
                    dma(stg[i][:, 0:c1 - c0], wd[name][kc * 128:(kc + 1) * 128, c0:c1], w=[stg_key[i]])
                    if i % 2:
                        tr.op("dve", lambda h, i=i, n=c1 - c0: h.tensor_copy(out=stgb[i][:, 0:n], in_=stg[i][:, 0:n]), r=[stg_key[i]], w=[stgb_key[i]])
                    else:
                        tr.op("act", lambda h, i=i, n=c1 - c0: h.copy(out=stgb[i][:, 0:n], in_=stg[i][:, 0:n]), r=[stg_key[i]], w=[stgb_key[i]])
                    tr.op("pool", lambda h, i=i, kc=kc, c0=c0, c1=c1: h.dma_start(out=wb[name][:, kc, c0:c1], in_=stgb[i][:, 0:c1 - c0]),
                          r=[stgb_key[i]], w=[("wb", name)], dma=True)

        for name in WSPEC:
            convert(name)

        ck("conversions")
        def slab(name, c0, c1, k0=0, k1=None):
            K, N = WSPEC[name]
            if k1 is None:
                k1 = K // 128
            KC = k1 - k0
            n = c1 - c0
            assert KC * n <= 8 * 512
            i = nxt("slab", 3); buf = slabs[i]; key = ("slab", i)
            view = buf[:, 0:KC * n].rearrange("p (k n) -> p k n", n=n)
            dma(view, wb[name][:, k0:k1, c0:c1], r=[("wb", name)], w=[key])
            return view, key

        def mm(out, lhsT, rhs, start, stop, r, w):
            tr.op("pe", lambda h: h.matmul(out, lhsT, rhs, start=start, stop=stop), r=r, w=w)

        def transposes(src_ap_fn, nchunk, dst, dst_key, src_key, rows=128, cols=128):
            i = nxt("ptr", 2)
            for j in range(nchunk):
                tr.op("pe", lambda h, j=j, i=i: h.transpose(ptr[i][0:cols, j * 128:j * 128 + rows], src_ap_fn(j), ident[0:rows, 0:rows]),
                      r=[src_key, "ident"], w=[("ptr", i)])
            tr.op("dve", lambda h, i=i: h.tensor_copy(out=dst[0:cols, 0:nchunk, 0:rows],
                                                     in_=ptr[i][0:cols, 0:nchunk * 128].rearrange("p (j t) -> p j t", t=128)[:, :, 0:rows]),
                  r=[("ptr", i)], w=[dst_key])

        wukc = hbf[:, :].rearrange("p (k n) -> p k n", k=2)
        for kc in range(2):
            src = wb["w_ukv"][:, kc, :].rearrange("p (h e) -> p h e", e=128)
            dma(wuvb[:, kc, :].rearrange("p (h e) -> p h e", e=VH), src[:, :, QKN:128], r=[("wb", "w_ukv")], w=["wuvb"])
            dma(wukc[:, kc, :].rearrange("p (h e) -> p h e", e=QKN), src[:, :, 0:QKN], r=[("wb", "w_ukv")], w=["hbf"])
        for kc in range(2):
            i = nxt("ptr", 2)
            for pr in range(4):
                tr.op("pe", lambda h, kc=kc, pr=pr, i=i: h.transpose(ptr[i][:, pr * 128:(pr + 1) * 128], wukc[:, kc, pr * 128:(pr + 1) * 128], ident[:]),
                      r=["hbf", "ident"], w=[("ptr", i)])
            tr.op("dve", lambda h, kc=kc, i=i: h.tensor_copy(out=wukT[:, :, kc * 128:(kc + 1) * 128],
                                                            in_=ptr[i][:, 0:512].rearrange("p (j t) -> p j t", t=128)),
                  r=[("ptr", i)], w=["wukT"])

        ck("wuk setup")
        def compute_ada(c_src, nseq, rep):
            for kc in range(8):
                dma(cT[:, kc, 0:nseq], c_src[:, kc * 128:(kc + 1) * 128].rearrange("s p -> p s"), w=[("cT", kc)], nonc=True)
            tr.op("act", lambda h: h.activation(out=cTs[:, :, 0:nseq], in_=cT[:, :, 0:nseq], func=AF.Silu), r=[("cT", k) for k in range(8)], w=["cTs"])
            for kc in range(8):
                tr.op("dve", lambda h, kc=kc: h.tensor_copy(out=cTexp[:, kc, :].rearrange("p (s r) -> p s r", r=rep),
                                                             in_=cTs[:, kc, 0:nseq].unsqueeze(2).to_broadcast([128, nseq, rep])),
                      r=["cTs"], w=["cTexp"])
            dma(ada[:], b_ada.partition_broadcast(128), w=["ada"])
            for c in range(12):
                sl, key = slab("w_ada", c * 512, (c + 1) * 512)
                i = nxt("pmm", 2)
                for kc in range(8):
                    mm(pmm[i][:, :], cTexp[:, kc, :], sl[:, kc, :], kc == 0, kc == 7, r=["cTexp", key], w=[("pmm", i)])
                tr.op("dve", lambda h, c=c, i=i: h.tensor_tensor(out=ada[:, c * 512:(c + 1) * 512], in0=pmm[i][:, :], in1=ada[:, c * 512:(c + 1) * 512], op=ALU.add),
                      r=[("pmm", i), "ada"], w=["ada"])
            for off in (1 * D, 4 * D):
                tr.op("dve", lambda h, off=off: h.tensor_scalar_add(out=ada[:, off:off + D], in0=ada[:, off:off + D], scalar1=1.0), r=["ada"], w=["ada"])

        def layer_norm(src, gi, dst, skey, dkey):
            dma(lnbc[:, 0, :], lnd[gi].partition_broadcast(128), w=[("lnbc", 0)])
            dma(lnbc[:, 1, :], lnd[gi + 1].partition_broadcast(128), w=[("lnbc", 1)])
            gi = 0
            tr.op("dve", lambda h: h.reduce_sum(out=st8[:, 0:1], in_=src, axis=AX.X), r=[skey], w=["st8"])
            tr.op("act", lambda h: h.activation(out=tmp2[:], in_=src, func=AF.Square, accum_out=st8[:, 1:2]), r=[skey, "st8"], w=["tmp2", "st8"])
            tr.op("dve", lambda h: h.tensor_scalar(out=st8[:, 2:4], in0=st8[:, 0:2], scalar1=1.0 / D, scalar2=None, op0=ALU.mult), r=["st8"], w=["st8"])
            tr.op("dve", lambda h: h.tensor_tensor(out=st8[:, 4:5], in0=st8[:, 2:3], in1=st8[:, 2:3], op=ALU.mult), r=["st8"], w=["st8"])
            tr.op("dve", lambda h: h.tensor_tensor(out=st8[:, 5:6], in0=st8[:, 3:4], in1=st8[:, 4:5], op=ALU.subtract), r=["st8"], w=["st8"])
            tr.op("dve", lambda h: h.tensor_scalar_add(out=st8[:, 5:6], in0=st8[:, 5:6], scalar1=LN_EPS), r=["st8"], w=["st8"])
            tr.op("act", lambda h: h.activation(out=st8[:, 7:8], in_=st8[:, 5:6], func=AF.Sqrt), r=["st8"], w=["st8"])
            tr.op("dve", lambda h: h.reciprocal(out=st8[:, 6:7], in_=st8[:, 7:8]), r=["st8"], w=["st8"])
            tr.op("dve", lambda h: h.tensor_scalar(out=dst, in0=src, scalar1=st8[:, 2:3], scalar2=st8[:, 6:7], op0=ALU.subtract, op1=ALU.mult), r=[skey, "st8"], w=[dkey])
            tr.op("dve", lambda h: h.tensor_tensor(out=dst, in0=dst, in1=lnbc[:, gi, :], op=ALU.mult), r=[dkey, ("lnbc", gi)], w=[dkey])
            tr.op("dve", lambda h: h.tensor_tensor(out=dst, in0=dst, in1=lnbc[:, gi + 1, :], op=ALU.add), r=[dkey, ("lnbc", gi + 1)], w=[dkey])

        def rms_norm(src, n, gbc, gkey, dst, skey, dkey, dst_bf=None, bkey=None):
            tr.op("act", lambda h: h.activation(out=tmp2[:, 0:n], in_=src, func=AF.Square, accum_out=st8[:, 8:9]), r=[skey, "st8"], w=["tmp2", "st8"])
            tr.op("dve", lambda h: h.tensor_scalar(out=st8[:, 9:10], in0=st8[:, 8:9], scalar1=1.0 / n, scalar2=RMS_EPS, op0=ALU.mult, op1=ALU.add), r=["st8"], w=["st8"])
            tr.op("act", lambda h: h.activation(out=st8[:, 9:10], in_=st8[:, 9:10], func=AF.Sqrt), r=["st8"], w=["st8"])
            tr.op("dve", lambda h: h.reciprocal(out=st8[:, 10:11], in_=st8[:, 9:10]), r=["st8"], w=["st8"])
            tr.op("dve", lambda h: h.tensor_scalar(out=dst, in0=src, scalar1=st8[:, 10:11], scalar2=None, op0=ALU.mult), r=[skey, "st8"], w=[dkey])
            tr.op("dve", lambda h: h.tensor_tensor(out=dst, in0=dst, in1=gbc, op=ALU.mult), r=[dkey, gkey], w=[dkey])
            if dst_bf is not None:
                tr.op("act", lambda h: h.copy(out=dst_bf, in_=dst), r=[dkey], w=[bkey])

        def rope(src, dst, cos, sin, nh, skey, dkey, ckeys):
            cb = cos.unsqueeze(1).to_broadcast([128, nh, 16]) if nh > 1 else cos
            sbc = sin.unsqueeze(1).to_broadcast([128, nh, 16]) if nh > 1 else sin
            if nh > 1:
                x1_, x2_ = src[:, :, 0:16], src[:, :, 16:32]
                d1, d2 = dst[:, :, 0:16], dst[:, :, 16:32]
                t1 = tmp2[:, 0:nh * 16].rearrange("p (h f) -> p h f", f=16)
                t2 = tmp2[:, 256:256 + nh * 16].rearrange("p (h f) -> p h f", f=16)
            else:
                x1_, x2_ = src[:, 0:16], src[:, 16:32]
                d1, d2 = dst[:, 0:16], dst[:, 16:32]
                t1 = tmp2[:, 0:16]
                t2 = tmp2[:, 256:272]
            rk = [skey] + list(ckeys)
            tr.op("dve", lambda h: h.tensor_tensor(out=t1, in0=x1_, in1=cb, op=ALU.mult), r=rk, w=["tmp2"])
            tr.op("dve", lambda h: h.tensor_tensor(out=t2, in0=x2_, in1=sbc, op=ALU.mult), r=rk, w=["tmp2"])
            tr.op("dve", lambda h: h.tensor_tensor(out=d1, in0=t1, in1=t2, op=ALU.subtract), r=["tmp2"], w=[dkey])
            tr.op("dve", lambda h: h.tensor_tensor(out=t1, in0=x1_, in1=sbc, op=ALU.mult), r=rk, w=["tmp2"])
            tr.op("dve", lambda h: h.tensor_tensor(out=t2, in0=x2_, in1=cb, op=ALU.mult), r=rk, w=["tmp2"])
            tr.op("dve", lambda h: h.tensor_tensor(out=d2, in0=t1, in1=t2, op=ALU.add), r=["tmp2"], w=[dkey])

        def tile(x_src, y_dst, lat_dst, kr_dst, cos, sin, ckeys, prompt, ti=0, conv_dst=None, first=False, last=False):
            xi = 0
            X = xt[xi]
            xk = ("x", xi)
            dma(X[:], x_src, w=[xk])
            tr.op("dve", lambda h: h.tensor_tensor(out=tmp[:], in0=X[:], in1=ada[:, D:2 * D], op=ALU.mult), r=[xk, "ada"], w=["tmp"])
            tr.op("dve", lambda h: h.tensor_tensor(out=hbf[:], in0=tmp[:], in1=ada[:, 0:D], op=ALU.add), r=["tmp", "ada"], w=["hbf"])
            transposes(lambda j: hbf[:, j * 128:(j + 1) * 128], 8, hT, "hT", "hbf")
            ck("tile: h transposes")
            for (c0, c1) in ((0, 512), (512, 672)):
                sl, key = slab("w_in", c0, c1)
                i = nxt("pmm", 2)
                for kc in range(8):
                    mm(pmm[i][:, 0:c1 - c0], hT[:, kc, :], sl[:, kc, :], kc == 0, kc == 7, r=["hT", key], w=[("pmm", i)])
                tr.op("act", lambda h, i=i, c0=c0, c1=c1: h.copy(out=z[:, c0:c1], in_=pmm[i][:, 0:c1 - c0]), r=[("pmm", i)], w=["z"])
            ck("tile: z")
            rms_norm(z[:, 0:QL], QL, qgbc[:], "qgbc", tmp[:, 0:QL], "z", "tmp", qn[:], "qn")
            ck("q: rms")
            transposes(lambda j: qn[:, j * 128:(j + 1) * 128], 3, qnT, "qnT", "qn")
            ck("q: qnT")
            for (c0, c1) in ((0, 384), (384, 768)):
                sl, key = slab("w_uq", c0, c1)
                i = nxt("pmm", 2)
                for kc in range(3):
                    mm(pmm[i][:, 0:384], qnT[:, kc, :], sl[:, kc, :], kc == 0, kc == 2, r=["qnT", key], w=[("pmm", i)])
                ck("q: mm%d" % c0)
                pq = pmm[i][:, 0:384].rearrange("p (h e) -> p h e", e=QKH)
                tr.op("act", lambda h, pq=pq, c0=c0: h.copy(out=q[:, c0 // QKH:c0 // QKH + 4, :], in_=pq[:, :, QKN:QKH]), r=[("pmm", i)], w=["q"])
                ck("q: evA%d" % c0)
                tr.op("act", lambda h, pq=pq, c0=c0: h.copy(out=qb[:, c0 // QKH:c0 // QKH + 4, 0:QKN], in_=pq[:, :, 0:QKN]), r=[("pmm", i)], w=["qb"])
                ck("q: evB%d" % c0)
            ck("q: wuq")
            rope(q[:, :, :], qb[:, :, QKN:QKH], cos, sin, NH, "q", "qb", ckeys)
            ck("tile: q path")
            rms_norm(z[:, QL:QL + KVL], KVL, kvgbc[:], "kvgbc", ckv[:], "z", "ckv", ckvb[:], "ckvb")
            dma(lat_dst, ckv[:], r=["ckv"])
            rope(z[:, QL + KVL:672], krt[:], cos, sin, 1, "z", "krt", ckeys)
            dma(kr_dst, krt[:], r=["krt"])
            transposes(lambda j: ckvb[:, j * 128:(j + 1) * 128], 2, ckvT, "ckvT", "ckvb")
            ck("tile: kv path")
            slc_, keyc = slab("w_in", 672 + CD, 672 + 2 * CD)
            slv_, keyv = slab("w_in", 672 + 2 * CD, 672 + 3 * CD)
            for j in range(4):
                ic = nxt("pmm", 2)
                for kc in range(8):
                    mm(pmm[ic][:, 0:128], slc_[:, kc, j * 128:(j + 1) * 128], hT[:, kc, :], kc == 0, kc == 7, r=["hT", keyc], w=[("pmm", ic)])
                tr.op("act", lambda h, j=j, ic=ic: h.copy(out=cgs[:, j, :], in_=pmm[ic][:, 0:128]), r=[("pmm", ic)], w=["cgs"])
                iv = nxt("pmm", 2)
                for kc in range(8):
                    mm(pmm[iv][:, 0:128], slv_[:, kc, j * 128:(j + 1) * 128], hT[:, kc, :], kc == 0, kc == 7, r=["hT", keyv], w=[("pmm", iv)])
                if prompt:
                    uflat = uT[:].rearrange("p c s t -> p c (s t)")
                    tr.op("dve", lambda h, j=j, iv=iv: h.tensor_tensor(out=uflat[:, j, 2:130], in0=cgs[:, j, :], in1=pmm[iv][:, 0:128], op=ALU.mult), r=["cgs", ("pmm", iv)], w=["uT"])
                else:
                    tr.op("dve", lambda h, j=j, iv=iv: h.tensor_tensor(out=uT[:, j, :, 2:2 + T], in0=cgs[:, j, :].rearrange("p (s t) -> p s t", t=T),
                                                                      in1=pmm[iv][:, 0:128].rearrange("p (s t) -> p s t", t=T), op=ALU.mult), r=["cgs", ("pmm", iv)], w=["uT"])
            sl, key = slab("w_in", 672, 672 + CD)
            for j in range(4):
                ib = nxt("pmm", 2)
                for kc in range(8):
                    mm(pmm[ib][:, 0:128], sl[:, kc, j * 128:(j + 1) * 128], hT[:, kc, :], kc == 0, kc == 7, r=["hT", key], w=[("pmm", ib)])
                if prompt:
                    uflat = uT[:].rearrange("p c s t -> p c (s t)")
                    u0, u1, u2 = uflat[:, j, 0:128], uflat[:, j, 1:129], uflat[:, j, 2:130]
                    cv = cvT[:, j, :]
                    bsrc = pmm[ib][:, 0:128]
                    cbo = cbT[:, j, :]
                else:
                    u0, u1, u2 = uT[:, j, :, 0:T], uT[:, j, :, 1:1 + T], uT[:, j, :, 2:2 + T]
                    cv = cvT[:, j, :].rearrange("p (s t) -> p s t", t=T)
                    bsrc = pmm[ib][:, 0:128].rearrange("p (s t) -> p s t", t=T)
                    cbo = cbT[:, j, :].rearrange("p (s t) -> p s t", t=T)
                tr.op("dve", lambda h, j=j, u0=u0, cv=cv: h.tensor_scalar(out=cv, in0=u0, scalar1=convw[:, j, 0:1], scalar2=None, op0=ALU.mult), r=["uT", "convw"], w=["cvT"])
                tr.op("dve", lambda h, j=j, u1=u1, cv=cv: h.scalar_tensor_tensor(out=cv, in0=u1, scalar=convw[:, j, 1:2], in1=cv, op0=ALU.mult, op1=ALU.add), r=["uT", "convw", "cvT"], w=["cvT"])
                tr.op("dve", lambda h, j=j, u2=u2, cv=cv: h.scalar_tensor_tensor(out=cv, in0=u2, scalar=convw[:, j, 2:3], in1=cv, op0=ALU.mult, op1=ALU.add), r=["uT", "convw", "cvT"], w=["cvT"])
                tr.op("dve", lambda h, cv=cv, bsrc=bsrc, cbo=cbo: h.tensor_tensor(out=cbo, in0=cv, in1=bsrc, op=ALU.mult), r=["cvT", ("pmm", ib)], w=["cbT"])
            if prompt:
                uflat = uT[:].rearrange("p c s t -> p c (s t)")
                if last:
                    for c in range(4):
                        dma(conv_dst[:, c * 128:(c + 1) * 128].rearrange("t p -> p t"), uflat[:, c, 128:130], r=["uT"], nonc=True)
                tr.op("pool", lambda h: h.tensor_copy(out=uflat[:, :, 0:2], in_=uflat[:, :, 128:130]), r=["uT"], w=["uT"])
            else:
                for c in range(4):
                    for t2 in range(2):
                        dma(conv_dst[:, c * 128:(c + 1) * 128].rearrange("(s t) p -> p s t", t=2)[:, :, t2], uT[:, c, :, T + t2], r=["uT"], nonc=True)
            ck("tile: conv")
            if prompt:
                prompt_attention(ti)
            else:
                sample_attention()
            ck("tile: attention")
            for c in range(2):
                for gi, gdst, gkey in ((0, sga, "sga"), (1, sgc, "sgc")):
                    c0 = 672 + 3 * CD + gi * D + c * 512
                    sl2, key2 = slab("w_in", c0, c0 + 512)
                    i = nxt("pmm", 2)
                    for kc in range(8):
                        mm(pmm[i][:, :], hT[:, kc, :], sl2[:, kc, :], kc == 0, kc == 7, r=["hT", key2], w=[("pmm", i)])
                    tr.op("act", lambda h, i=i, gdst=gdst: h.activation(out=gdst[:, :], in_=pmm[i][:, :], func=AF.Sigmoid), r=[("pmm", i)], w=[gkey])
                sla, ka = slab("w_oa", c * 512, (c + 1) * 512)
                ia = nxt("pmm", 2)
                for kc in range(4):
                    mm(pmm[ia][:, :], oT[:, kc, :], sla[:, kc, :], kc == 0, kc == 3, r=["oT", ka], w=[("pmm", ia)])
                tr.op("dve", lambda h, c=c, ia=ia: h.tensor_tensor(out=tmp[:, c * 512:(c + 1) * 512], in0=pmm[ia][:, :], in1=sga[:, :], op=ALU.mult), r=[("pmm", ia), "sga"], w=["tmp"])
                slc, kc_ = slab("w_oc", c * 512, (c + 1) * 512)
                ic = nxt("pmm", 2)
                for kc in range(4):
                    mm(pmm[ic][:, :], cbT[:, kc, :], slc[:, kc, :], kc == 0, kc == 3, r=["cbT", kc_], w=[("pmm", ic)])
                tr.op("dve", lambda h, c=c, ic=ic: h.tensor_tensor(out=tmp2[:, c * 512:(c + 1) * 512], in0=pmm[ic][:, :], in1=sgc[:, :], op=ALU.mult), r=[("pmm", ic), "sgc"], w=["tmp2"])
            tr.op("dve", lambda h: h.tensor_tensor(out=hbf[:], in0=tmp[:], in1=tmp2[:], op=ALU.add), r=["tmp", "tmp2"], w=["hbf"])
            transposes(lambda j: hbf[:, j * 128:(j + 1) * 128], 8, mT, "mT", "hbf")
            ck("tile: m")
            for c in range(2):
                sl3, k3 = slab("w_o", c * 512, (c + 1) * 512)
                i = nxt("pmm", 2)
                for kc in range(8):
                    mm(pmm[i][:, :], mT[:, kc, :], sl3[:, kc, :], kc == 0, kc == 7, r=["mT", k3], w=[("pmm", i)])
                tr.op("dve", lambda h, c=c, i=i: h.tensor_tensor(out=tmp[:, c * 512:(c + 1) * 512], in0=pmm[i][:, :], in1=ada[:, 2 * D + c * 512:2 * D + (c + 1) * 512], op=ALU.mult), r=[("pmm", i), "ada"], w=["tmp"])
            tr.op("dve", lambda h: h.scalar_tensor_tensor(out=tmp[:], in0=X[:], scalar=ALPHA, in1=tmp[:], op0=ALU.mult, op1=ALU.add), r=[xk, "tmp"], w=["tmp"])
            layer_norm(tmp[:], 0, X[:], "tmp", xk)
            ck("tile: LN1")
            tr.op("dve", lambda h: h.tensor_tensor(out=tmp[:], in0=X[:], in1=ada[:, 4 * D:5 * D], op=ALU.mult), r=[xk, "ada"], w=["tmp"])
            tr.op("dve", lambda h: h.tensor_tensor(out=hbf[:], in0=tmp[:], in1=ada[:, 3 * D:4 * D], op=ALU.add), r=["tmp", "ada"], w=["hbf"])
            transposes(lambda j: hbf[:, j * 128:(j + 1) * 128], 8, hT, "hT", "hbf")
            ck("tile: h2")
            chunks = [(c0, min(DFF, c0 + 512)) for c0 in range(0, DFF, 512)]
            pend = {}

            def ffn_mm(c0, c1):
                n = c1 - c0
                s1, k1 = slab("w_ff1", c0, c1)
                i1 = nxt("pmm", 2)
                for kc in range(8):
                    mm(pmm[i1][:, 0:n], hT[:, kc, :], s1[:, kc, :], kc == 0, kc == 7, r=["hT", k1], w=[("pmm", i1)])
                tr.op("act", lambda h: h.activation(out=tmp[:, 0:n], in_=pmm[i1][:, 0:n], func=AF.Silu), r=[("pmm", i1)], w=["tmp"])
                s3, k3 = slab("w_ff3", c0, c1)
                i3 = nxt("pmm", 2)
                for kc in range(8):
                    mm(pmm[i3][:, 0:n], hT[:, kc, :], s3[:, kc, :], kc == 0, kc == 7, r=["hT", k3], w=[("pmm", i3)])
                pend[c0] = i3

            def ffn_mult(c0, c1):
                n = c1 - c0
                i3 = pend.pop(c0)
                tr.op("dve", lambda h: h.tensor_tensor(out=ab[:, 0:n], in0=tmp[:, 0:n], in1=pmm[i3][:, 0:n], op=ALU.mult), r=["tmp", ("pmm", i3)], w=["ab"])

            def ffn_tr(c0, c1):
                n = c1 - c0
                g = c0 // 128
                transposes(lambda j: ab[:, j * 128:(j + 1) * 128], n // 128, aT[:, g:g + n // 128, :], "aT", "ab")

            ffn_mm(*chunks[0])
            ffn_mult(*chunks[0])
            for ci in range(len(chunks)):
                if ci + 1 < len(chunks):
                    ffn_mm(*chunks[ci + 1])
                ffn_tr(*chunks[ci])
                if ci + 1 < len(chunks):
                    ffn_mult(*chunks[ci + 1])
            for c in range(2):
                i = nxt("pmm", 2)
                for (ka, kb) in ((0, 8), (8, 16), (16, 22)):
                    s2, k2 = slab("w_ff2", c * 512, (c + 1) * 512, ka, kb)
                    for kc in range(ka, kb):
                        mm(pmm[i][:, :], aT[:, kc, :], s2[:, kc - ka, :], kc == 0, kc == 21, r=["aT", k2], w=[("pmm", i)])
                tr.op("dve", lambda h, c=c, i=i: h.tensor_tensor(out=tmp[:, c * 512:(c + 1) * 512], in0=pmm[i][:, :], in1=ada[:, 5 * D + c * 512:5 * D + (c + 1) * 512], op=ALU.mult), r=[("pmm", i), "ada"], w=["tmp"])
            tr.op("dve", lambda h: h.scalar_tensor_tensor(out=tmp[:], in0=X[:], scalar=ALPHA, in1=tmp[:], op0=ALU.mult, op1=ALU.add), r=[xk, "tmp"], w=["tmp"])
            ck("tile: FFN")
            layer_norm(tmp[:], 2, X[:], "tmp", xk)
            dma(y_dst, X[:], r=[xk])

        def prompt_attention(ti):
            slk, kk_ = slab("w_ukv", 0, NH * 128)
            for half in range(2):
                i = nxt("pmm", 2)
                for kc in range(2):
                    mm(pmm[i][:, :], ckvT[:, kc, :], slk[:, kc, half * 512:(half + 1) * 512], kc == 0, kc == 1, r=["ckvT", kk_], w=[("pmm", i)])
                pv = pmm[i][:, :].rearrange("p (h e) -> p h e", e=128)
                tr.op("act", lambda h, half=half, pv=pv: h.copy(out=kk[:, half * 4:(half + 1) * 4, 0:QKN], in_=pv[:, :, 0:QKN]), r=[("pmm", i)], w=["kk"])
                tr.op("act", lambda h, half=half, pv=pv: h.copy(out=Vt[:, ti, half * 256:(half + 1) * 256].rearrange("p (h e) -> p h e", e=VH), in_=pv[:, :, QKN:128]), r=[("pmm", i)], w=[("Vt", ti)])
            tr.op("pool", lambda h: h.tensor_copy(out=krb[:], in_=krt[:]), r=["krt"], w=["krb"])
            tr.op("pool", lambda h: h.tensor_copy(out=kk[:, :, QKN:QKH], in_=krb[:].unsqueeze(1).to_broadcast([128, NH, QKR])), r=["krb"], w=["kk"])
            i = nxt("ptr", 2)
            for hh in range(NH):
                tr.op("pe", lambda h, hh=hh, i=i: h.transpose(ptr[i][0:QKH, hh * 128:(hh + 1) * 128], kk[:, hh, :], ident[:]), r=["kk", "ident"], w=[("ptr", i)])
            tr.op("dve", lambda h, i=i: h.tensor_copy(out=Kt[0:QKH, :, ti * 128:(ti + 1) * 128], in_=ptr[i][0:QKH, :].rearrange("p (j t) -> p j t", t=128)), r=[("ptr", i)], w=[("Kt", ti)])
            i = nxt("ptr", 2)
            for hh in range(NH):
                tr.op("pe", lambda h, hh=hh, i=i: h.transpose(ptr[i][0:QKH, hh * 128:(hh + 1) * 128], qb[:, hh, :], ident[:]), r=["qb", "ident"], w=[("ptr", i)])
            tr.op("dve", lambda h, i=i: h.tensor_copy(out=qT[0:QKH, :, :], in_=ptr[i][0:QKH, :].rearrange("p (j t) -> p j t", t=128)), r=[("ptr", i)], w=["qT"])
            nk = ti + 1
            kkeys = [("Kt", t) for t in range(nk)]
            vkeys = [("Vt", t) for t in range(nk)]
            nch = (nk + 3) // 4

            def head_front(hh):
                par = hh % 2
                sh_, shk = sth[par], ("sth", par)
                Pp = P[par]
                for ch in range(nch):
                    k0 = ch * 512
                    k1 = min(nk * 128, k0 + 512)
                    isl = (ch == nch - 1)
                    mm(psc[ch][:, 0:k1 - k0], qT[0:QKH, hh, :], Kt[0:QKH, hh, k0:k1], True, not isl, r=["qT"] + kkeys, w=[("psc", ch)])
                    if isl:
                        d0 = (nk - 1) * 128 - k0
                        mm(psc[ch][:, d0:d0 + 128], ident[:], maskc[:], False, True, r=["ident", "maskc"], w=[("psc", ch)])
                for ch in range(nch):
                    k0 = ch * 512
                    k1 = min(nk * 128, k0 + 512)
                    tr.op("dve", lambda h, ch=ch, n=k1 - k0: h.reduce_max(out=sh_[:, 1 + ch:2 + ch], in_=psc[ch][:, 0:n], axis=AX.X), r=[("psc", ch)], w=[shk])
                tr.op("dve", lambda h: h.reduce_max(out=sh_[:, 0:1], in_=sh_[:, 1:1 + nch], axis=AX.X), r=[shk], w=[shk])
                tr.op("dve", lambda h: h.tensor_scalar(out=sh_[:, 0:1], in0=sh_[:, 0:1], scalar1=-SCALE, scalar2=None, op0=ALU.mult), r=[shk], w=[shk])
                for ch in range(nch):
                    k0 = ch * 512
                    k1 = min(nk * 128, k0 + 512)
                    tr.op("act", lambda h, ch=ch, k0=k0, k1=k1: h.activation(out=Pp[:, k0:k1], in_=psc[ch][:, 0:k1 - k0], func=AF.Exp, bias=sh_[:, 0:1], scale=SCALE, accum_out=sh_[:, 1 + ch:2 + ch]),
                          r=[("psc", ch), shk], w=[("P", par, ch), shk])

            def head_back(hh):
                par = hh % 2
                sh_, shk = sth[par], ("sth", par)
                Pp = P[par]
                io = nxt("pmm", 2)
                for g in range(0, nk, 8):
                    ng = min(8, nk - g)
                    pkeys = [("P", par, c) for c in range(g // 4, (g + ng + 3) // 4)]
                    i = nxt("ptr", 2)
                    for j in range(ng):
                        tr.op("pe", lambda h, j=j, i=i, g=g: h.transpose(ptr[i][:, j * 128:(j + 1) * 128], Pp[:, (g + j) * 128:(g + j + 1) * 128], ident[:]),
                              r=pkeys + ["ident"], w=[("ptr", i)])
                    tr.op("dve", lambda h, i=i, ng=ng: h.tensor_copy(out=PT[:, 0:ng, :], in_=ptr[i][:, 0:ng * 128].rearrange("p (j t) -> p j t", t=128)), r=[("ptr", i)], w=["PT"])
                    for t in range(g, g + ng):
                        mm(pmm[io][:, 0:VH], PT[:, t - g, :], Vt[:, t, hh * VH:(hh + 1) * VH], t == 0, t == nk - 1, r=["PT"] + vkeys, w=[("pmm", io)])
                tr.op("dve", lambda h: h.reduce_sum(out=sh_[:, 5:6], in_=sh_[:, 1:1 + nch], axis=AX.X), r=[shk], w=[shk])
                tr.op("dve", lambda h: h.reciprocal(out=sh_[:, 6:7], in_=sh_[:, 5:6]), r=[shk], w=[shk])
                tr.op("act", lambda h, io=io: h.activation(out=ob[:, hh * VH:(hh + 1) * VH], in_=pmm[io][:, 0:VH], func=AF.Copy, scale=sh_[:, 6:7]), r=[("pmm", io), shk], w=["hbf"])

            head_front(0)
            for hh in range(NH):
                if hh + 1 < NH:
                    head_front(hh + 1)
                head_back(hh)
            transposes(lambda j: ob[:, j * 128:(j + 1) * 128], 4, oT, "oT", "hbf")

        def sample_attention():
            i = nxt("ptr", 2)
            qbf = qb[:]
            tr.op("dve", lambda h: h.tensor_copy(out=hbf[:, 0:512].rearrange("p (h e) -> p h e", e=QKN), in_=qb[:, :, 0:QKN]), r=["qb"], w=["hbf"])
            for pr in range(4):
                tr.op("pe", lambda h, pr=pr, i=i: h.transpose(ptr[i][:, pr * 128:(pr + 1) * 128], hbf[:, pr * 128:(pr + 1) * 128], ident[:]), r=["hbf", "ident"], w=[("ptr", i)])
            tr.op("dve", lambda h, i=i: h.tensor_copy(out=qnpT[:], in_=ptr[i][:, 0:512].rearrange("p (j t) -> p j t", t=128)), r=[("ptr", i)], w=["qnpT"])
            for kc in range(2):
                for hh in range(NH):
                    pr, off = hh // 2, (hh % 2) * 64
                    ia = nxt("pmm", 2)
                    mm(pmm[ia][:, 0:128], wukT[off:off + 64, pr, kc * 128:(kc + 1) * 128], qnpT[off:off + 64, pr, :], True, True, r=["wukT", "qnpT"], w=[("pmm", ia)])
                    tr.op("act", lambda h, kc=kc, hh=hh, ia=ia: h.copy(out=QaT[:, kc, :, hh * T:(hh + 1) * T], in_=pmm[ia][:, 0:128].rearrange("p (s t) -> p s t", t=T)), r=[("pmm", ia)], w=["QaT"])
            tr.op("dve", lambda h: h.tensor_copy(out=hbf[:, 512:768].rearrange("p (h e) -> p h e", e=QKR), in_=qb[:, :, QKN:QKH]), r=["qb"], w=["hbf"])
            i = nxt("ptr", 2)
            for hh in range(NH):
                tr.op("pe", lambda h, hh=hh, i=i: h.transpose(ptr[i][0:QKR, hh * 128:(hh + 1) * 128], hbf[:, 512 + hh * QKR:512 + (hh + 1) * QKR], ident[:]), r=["hbf", "ident"], w=[("ptr", i)])
            for hh in range(NH):
                tr.op("dve", lambda h, hh=hh, i=i: h.tensor_copy(out=QrT[:, :, hh * T:(hh + 1) * T], in_=ptr[i][0:QKR, hh * 128:(hh + 1) * 128].rearrange("p (s t) -> p s t", t=T)), r=[("ptr", i)], w=["QrT"])
            XTn = sb("XTn", [128, 3, 128], BF16)
            tr.op("pool", lambda h: h.tensor_copy(out=krb[:], in_=krt[:]), r=["krt"], w=["krb"])
            tr.op("pool", lambda h: h.tensor_copy(out=XTn[:, 0:2, :], in_=ckvT[:]), r=["ckvT"], w=["XTn"])
            i = nxt("ptr", 2)
            tr.op("pe", lambda h, i=i: h.transpose(ptr[i][0:QKR, 0:128], krb[:], ident[:]), r=["krb", "ident"], w=[("ptr", i)])
            tr.op("dve", lambda h, i=i: h.tensor_copy(out=XTn[0:QKR, 2, :], in_=ptr[i][0:QKR, 0:128]), r=[("ptr", i)], w=["XTn"])

            npg_cnt = [0]
            KPG = NPG

            def flash_front(par, s, xts, KP, mask_off=None):
                for gi, (xv, xk_) in enumerate(xts):
                    o = psc[par][0:64, gi * KP:(gi + 1) * KP]
                    lastmm = mask_off is None
                    mm(o, QaT[:, 0, s, :], xv[:, 0, 0:KP], True, False, r=["QaT", xk_], w=[("psc", par)])
                    mm(o, QaT[:, 1, s, :], xv[:, 1, 0:KP], False, False, r=["QaT", xk_], w=[("psc", par)])
                    mm(o, QrT[:, s, :], xv[0:QKR, 2, 0:KP], False, lastmm, r=["QrT", xk_], w=[("psc", par)])
                    if mask_off is not None:
                        mm(o, ident[0:64, 0:64], mbig[:, mask_off:mask_off + KP], False, True, r=["ident", "mbig"], w=[("psc", par)])

            def flash_back(par, g, xtoks, KP, first):
                W = g * KP
                sc = psc[par]
                pa = psc[2 + par]
                pk = ("psc", par)
                pak = ("psc", 2 + par)
                Psb, PTb = Ps[par], PTs[par]
                Pk, PTk = ("Ps", par), ("PTs", par)
                tr.op("dve", lambda h: h.reduce_max(out=sm[:, 1:2], in_=sc[0:64, 0:W], axis=AX.X), r=[pk], w=["sm"])
                if not first:
                    tr.op("dve", lambda h: h.tensor_tensor(out=sm[:, 1:2], in0=sm[:, 1:2], in1=sm[:, 0:1], op=ALU.max), r=["sm"], w=["sm"])
                    tr.op("dve", lambda h: h.tensor_tensor(out=sm[:, 6:7], in0=sm[:, 0:1], in1=sm[:, 1:2], op=ALU.subtract), r=["sm"], w=["sm"])
                    tr.op("act", lambda h: h.activation(out=sm[:, 3:4], in_=sm[:, 6:7], func=AF.Exp, scale=SCALE), r=["sm"], w=["sm"])
                tr.op("dve", lambda h: h.tensor_scalar(out=sm[:, 2:3], in0=sm[:, 1:2], scalar1=-SCALE, scalar2=None, op0=ALU.mult), r=["sm"], w=["sm"])
                tr.op("act", lambda h: h.activation(out=Psb[:, 0:W], in_=sc[0:64, 0:W], func=AF.Exp, bias=sm[:, 2:3], scale=SCALE, accum_out=sm[:, 5:6]), r=[pk, "sm"], w=[Pk, "sm"])
                tr.op("dve", lambda h: h.tensor_copy(out=sm[:, 0:1], in_=sm[:, 1:2]), r=["sm"], w=["sm"])
                if first:
                    tr.op("dve", lambda h: h.tensor_copy(out=sm[:, 4:5], in_=sm[:, 5:6]), r=["sm"], w=["sm"])
                else:
                    tr.op("dve", lambda h: h.scalar_tensor_tensor(out=sm[:, 4:5], in0=sm[:, 4:5], scalar=sm[:, 3:4], in1=sm[:, 5:6], op0=ALU.mult, op1=ALU.add), r=["sm"], w=["sm"])
                ip = nxt("ptr", 2)
                for gi in range(g):
                    tr.op("pe", lambda h, gi=gi, ip=ip: h.transpose(ptr[ip][0:KP, gi * 128:gi * 128 + 64], Psb[:, gi * KP:(gi + 1) * KP], ident[0:64, 0:64]), r=[Pk, "ident"], w=[("ptr", ip)])
                tr.op("dve", lambda h, ip=ip: h.tensor_copy(out=PTb[0:KP, 0:g, :], in_=ptr[ip][0:KP, 0:g * 128].rearrange("p (j t) -> p j t", t=128)[:, :, 0:64]), r=[("ptr", ip)], w=[PTk])
                for gi, (xa, xak) in enumerate(xtoks):
                    mm(pa[0:64, 0:KVL], PTb[0:KP, gi, :], xa, gi == 0, gi == g - 1, r=[PTk, xak], w=[pak])
                if first:
                    tr.op("dve", lambda h: h.tensor_copy(out=accs[:], in_=pa[0:64, 0:KVL]), r=[pak], w=["accs"])
                else:
                    tr.op("dve", lambda h: h.scalar_tensor_tensor(out=accs[:], in0=accs[:], scalar=sm[:, 3:4], in1=pa[0:64, 0:KVL], op0=ALU.mult, op1=ALU.add), r=["accs", "sm", pak], w=["accs"])

            def finalize(s):
                tr.op("dve", lambda h: h.reciprocal(out=sm[:, 7:8], in_=sm[:, 4:5]), r=["sm"], w=["sm"])
                tr.op("act", lambda h: h.activation(out=olat[:], in_=accs[:], func=AF.Copy, scale=sm[:, 7:8]), r=["accs", "sm"], w=["olat"])
                io = nxt("ptr", 2)
                for kc in range(2):
                    tr.op("pe", lambda h, kc=kc, io=io: h.transpose(ptr[io][:, kc * 128:kc * 128 + 64], olat[:, kc * 128:(kc + 1) * 128], ident[0:64, 0:64]), r=["olat", "ident"], w=[("ptr", io)])
                tr.op("dve", lambda h, io=io: h.tensor_copy(out=olT[:], in_=ptr[io][:, 0:256].rearrange("p (j t) -> p j t", t=128)[:, :, 0:64]), r=[("ptr", io)], w=["olT"])
                for pr in range(4):
                    ia = nxt("pmm", 2)
                    for sub in range(2):
                        hh = pr * 2 + sub
                        for kc in range(2):
                            mm(pmm[ia][sub * 64:(sub + 1) * 64, 0:T], wuvb[:, kc, hh * VH:(hh + 1) * VH], olT[:, kc, hh * T:(hh + 1) * T], kc == 0, kc == 1, r=["wuvb", "olT"], w=[("pmm", ia)])
                    tr.op("act", lambda h, pr=pr, ia=ia, s=s: h.copy(out=oT[:, pr, s * T:(s + 1) * T], in_=pmm[ia][:, 0:T]), r=[("pmm", ia)], w=["oT"])

            dma(ptT[0:KPG, :], pt_d.rearrange("s p -> p s"), w=["ptT"], nonc=True)
            groups = []
            for s in range(SB):
                groups.append(("new", s, None))
                for c in range(NCH):
                    groups.append(("page", s, c))
            state = {}

            issued = [0]
            NPGRP = SB * NCH

            def ensure_issued(upto):
                while issued[0] < min(upto, NPGRP):
                    m = issued[0]
                    issued[0] += 1
                    ms, mc, mk = m // NCH, m % NCH, m % 4
                    tr.op("dve", lambda h, ms=ms, mc=mc, mk=mk: h.tensor_scalar(out=idxr[mk][0:KPG, :], in0=ptT[0:KPG, ms:ms + 1], scalar1=NCH, scalar2=mc, op0=ALU.mult, op1=ALU.add),
                          r=["ptT"], w=[("idxr", mk)])
                    tr.op("pool", lambda h, mk=mk: h.indirect_dma_start(out=Xb[mk][0:KPG].rearrange("p t l -> p (t l)"), out_offset=None, in_=pool_d[:, :],
                                                                       in_offset=bass.IndirectOffsetOnAxis(ap=idxr[mk][0:KPG, 0:1], axis=0)),
                          r=[("idxr", mk)], w=[("Xb", mk)], dma=True)

            def front(gidx):
                kind, s, c = groups[gidx]
                par = gidx % 2
                if kind == "new":
                    flash_front(par, s, [(XTn, "XTn")], 128, mask_off=120 - s * T)
                    state[gidx] = (1, [(ckvb[:, :], "ckvb")], 128, True)
                    return
                n = npg_cnt[0]
                npg_cnt[0] += 1
                b = n % 4
                ensure_issued(n + 3)
                xts, xtoks = [], []
                XTp = XT[par]
                for t in range(TCH):
                    it = nxt("ptr", 2)
                    tr.op("pe", lambda h, t=t, it=it: h.transpose(ptr[it][:, 0:KPG], Xb[b][0:KPG, t, 0:128], ident[0:KPG, 0:KPG]), r=[("Xb", b), "ident"], w=[("ptr", it)])
                    tr.op("pe", lambda h, t=t, it=it: h.transpose(ptr[it][:, 128:128 + KPG], Xb[b][0:KPG, t, 128:256], ident[0:KPG, 0:KPG]), r=[("Xb", b), "ident"], w=[("ptr", it)])
                    tr.op("pe", lambda h, t=t, it=it: h.transpose(ptr[it][0:QKR, 256:256 + KPG], Xb[b][0:KPG, t, 256:LAT], ident[0:KPG, 0:KPG]), r=[("Xb", b), "ident"], w=[("ptr", it)])
                    tr.op("dve", lambda h, t=t, it=it: h.tensor_copy(out=XTp[:, 0:2, t, 0:KPG], in_=ptr[it][:, 0:256].rearrange("p (j t) -> p j t", t=128)[:, :, 0:KPG]), r=[("ptr", it)], w=[("XT", par, t)])
                    tr.op("act", lambda h, t=t, it=it: h.copy(out=XTp[0:QKR, 2, t, 0:KPG], in_=ptr[it][0:QKR, 256:256 + KPG]), r=[("ptr", it)], w=[("XT", par, t)])
                    xts.append((XTp[:, :, t, :], ("XT", par, t)))
                    xtoks.append((Xb[b][0:KPG, t, 0:KVL], ("Xb", b)))
                if KPG == 128:
                    o = psc[par][0:64, 0:TCH * 128]
                    xk_all = [("XT", par, t) for t in range(TCH)]
                    mm(o, QaT[:, 0, s, :], XTp[:, 0, :, :].rearrange("p t k -> p (t k)"), True, False, r=["QaT"] + xk_all, w=[("psc", par)])
                    mm(o, QaT[:, 1, s, :], XTp[:, 1, :, :].rearrange("p t k -> p (t k)"), False, False, r=["QaT"] + xk_all, w=[("psc", par)])
                    mm(o, QrT[:, s, :], XTp[0:QKR, 2, :, :].rearrange("p t k -> p (t k)"), False, True, r=["QrT"] + xk_all, w=[("psc", par)])
                else:
                    flash_front(par, s, xts, KPG)
                state[gidx] = (TCH, xtoks, KPG, False)

            def back(gidx):
                kind, s, c = groups[gidx]
                g, xtoks, KP, first = state.pop(gidx)
                flash_back(gidx % 2, g, xtoks, KP, first)
                if kind == "page" and c == NCH - 1:
                    finalize(s)

            front(0)
            for gidx in range(len(groups)):
                if gidx + 1 < len(groups):
                    front(gidx + 1)
                back(gidx)

        for b in range(PB):
            compute_ada(cp_d[b:b + 1, :], 1, 128)
            ck("ada")
            tr.op("pool", lambda h: h.memset(uT[:], 0.0), r=["uT"], w=["uT"])
            for ti in range(NTS):
                r0 = b * SEQ + ti * 128
                tile(xp[r0:r0 + 128, :], yp[r0:r0 + 128, :], latp[r0:r0 + 128, :], krp[r0:r0 + 128, :],
                     cosp[:, ti, :], sinp[:, ti, :], ["rope", "rope2"], True, ti=ti,
                     conv_dst=convp[b * 2:(b + 1) * 2, :], first=(ti == 0), last=(ti == NTS - 1))
        compute_ada(cs_d, SB, T)
        tr.op("pool", lambda h: h.memset(uT[:], 0.0), r=["uT"], w=["uT", "uTm"])
        for c in range(4):
            for t2 in range(2):
                dma(uT[:, c, :, t2], sconv[:, c * 128:(c + 1) * 128].rearrange("(s t) p -> p s t", t=2)[:, :, t2], r=["uTm"], w=["uT"], nonc=True)
        tile(xs, ys, lats, krs, coss[:], sins[:], ["ropes", "ropes2"], False, conv_dst=convs)

    except _Stop:
        pass
    sem_es = ExitStack()
    sems = {e: sem_es.enter_context(nc.semaphore("s_" + e)) for e in ("pe", "act", "dve", "pool")}
    ring = {q: [sem_es.enter_context(nc.semaphore(f"ring_{q}{i}")) for i in range(NRING)] for q in Tracker.DMAQ}
    tr.emit(nc, sems, ring, None)
    sem_es.close()
    es.close()
    return nc


def rope_tables(pos):
    inv = (10000.0 ** (-(np.arange(0, QKR, 2, dtype=np.float32)) / np.float32(QKR))).astype(np.float32)
    ang = (pos.astype(np.float32)[:, None] * inv[None, :]).astype(np.float32)
    return np.cos(ang).astype(np.float32), np.sin(ang).astype(np.float32)


_CACHE = {}


def run(inputs, SEQ, NPG, NPOOL, PAST):
    key = (SEQ, NPG, NPOOL)
    if key not in _CACHE:
        _CACHE[key] = build(SEQ, NPG, NPOOL)
    nc = _CACHE[key]
    f = lambda a: np.ascontiguousarray(np.asarray(a))
    pool = np.concatenate([np.asarray(inputs["cache_kv_latent"])[0], np.asarray(inputs["cache_k_rope"])[0]], axis=-1)
    pool = np.ascontiguousarray(pool, dtype=np.float32).reshape(NPOOL * NCH, TCH * LAT)
    cosp, sinp = rope_tables(np.arange(SEQ))
    cs_, ss_ = rope_tables(PAST + np.arange(T))
    coss = np.tile(cs_, (SB, 1)); sins = np.tile(ss_, (SB, 1))
    ident = np.eye(128, dtype=np.float32).astype(ml_dtypes.bfloat16)
    maskc = np.where(np.arange(128)[None, :] <= np.arange(128)[:, None], 0.0, NEG).astype(np.float32).astype(ml_dtypes.bfloat16)
    mbig = np.full((64, 248), NEG, np.float32)
    for r in range(64):
        t = r % T
        mbig[r, 120:120 + t + 1] = 0.0
    mbig = mbig.astype(ml_dtypes.bfloat16)
    common = {k: f(inputs[k])[0] for k in WSPEC if k != "w_ukv"}
    common["w_ukv"] = f(inputs["w_ukv"])[0].reshape(KVL, NH * 128)
    for k in ("b_ada", "q_norm_g", "kv_norm_g", "ln1_g", "ln1_b", "ln2_g", "ln2_b"):
        common[k] = f(inputs[k]).reshape(1, -1)
    common["conv_w"] = f(inputs["conv_w"])[0]
    common.update(cosp=cosp, sinp=sinp, coss=coss, sins=sins, ident=ident, maskc=maskc, mbig=mbig, pool=pool)
    xp = f(inputs["x_prompt"]); xs = f(inputs["x_sample"]); ptab = f(inputs["page_table"]).astype(np.int32)
    sc = f(inputs["state_conv"])[0]; cp = f(inputs["c_prompt"]); cs = f(inputs["c_sample"])
    in_maps = []
    for c in range(8):
        m = dict(common)
        m["xp"] = xp[c * PB:(c + 1) * PB].reshape(PB * SEQ, D)
        m["xs"] = xs[c * SB:(c + 1) * SB].reshape(SB * T, D)
        m["pt"] = np.ascontiguousarray(ptab[c * SB:(c + 1) * SB])
        m["sconv"] = np.ascontiguousarray(sc[c * SB:(c + 1) * SB].reshape(SB * 2, CD))
        m["cp"] = np.ascontiguousarray(cp[c * PB:(c + 1) * PB]); m["cs"] = np.ascontiguousarray(cs[c * SB:(c + 1) * SB])
        in_maps.append(m)
    import os as _os
    if _os.environ.get("KTRACE"):
        res = run_bass_kernel_spmd(nc, in_maps, core_ids=list(range(8)), trace=True)
        print("KTRACE exec_time_ns", res.exec_time_ns, flush=True)
    else:
        res = run_bass_kernel_spmd(nc, in_maps, core_ids=list(range(8)))
    R = res.results
    cat = lambda k: np.concatenate([np.asarray(R[c][k]) for c in range(8)], axis=0)
    B = 8 * PB
    y_p = cat("yp").reshape(B, SEQ, D); y_s = cat("ys").reshape(8 * SB, T, D)
    lat_p = cat("latp").reshape(1, B, SEQ, KVL); kr_p = cat("krp").reshape(1, B, SEQ, QKR); cv_p = cat("convp").reshape(1, B, 2, CD)
    lat_s = cat("lats").reshape(1, 8 * SB, T, KVL); kr_s = cat("krs").reshape(1, 8 * SB, T, QKR); cv_s = cat("convs").reshape(1, 8 * SB, 2, CD)
    return (y_p, y_s, lat_p, kr_p, cv_p, lat_s, kr_s, cv_s)


def kernel(**inputs):
    SEQ = inputs["x_prompt"].shape[1]
    NPG = inputs["page_table"].shape[1]
    NPOOL = inputs["cache_kv_latent"].shape[1]
    return run(inputs, SEQ, NPG, NPOOL, NPG * PAGE)
```
